# Optimizing a Trainium2 kernel written in Bass

```python
import math
import jax, jax.numpy as jnp
from jax import lax
import numpy as np

D_MODEL = 2048
BATCH = 1
SEQ = 8192
DEPTH = 2

CHUNK = 64
N_BRANCH = 3
EPS = 1e-6

HG_DK = 128
HG_DV = 128
HG_WIDTH = D_MODEL // 2
HG_HEADS = HG_WIDTH // HG_DK
HG_VWIDTH = HG_HEADS * HG_DV

S5_WIDTH = D_MODEL // 2
S5_GROUP = 16
S5_GROUPS = S5_WIDTH // S5_GROUP
S5_STATE = 64

FOX_DH = 128
FOX_WIDTH = D_MODEL // 2
FOX_HEADS = FOX_WIDTH // FOX_DH
Q_BLOCK = 128

IN_SIZES = (HG_WIDTH, HG_WIDTH, HG_VWIDTH, HG_VWIDTH,
            S5_WIDTH, S5_WIDTH,
            FOX_WIDTH, FOX_WIDTH, FOX_WIDTH, FOX_HEADS, FOX_WIDTH,
            N_BRANCH * D_MODEL)
IN_WIDTH = sum(IN_SIZES)

kernel_name = 'hybrid_hgrn2_s5_fox_gated_merge'


def rms_norm(x, g):
    xf = x.astype(jnp.float32)
    y = xf * lax.rsqrt(jnp.mean(xf * xf, axis=-1, keepdims=True) + EPS)
    return (y * g.astype(jnp.float32)).astype(x.dtype)


def split_columns(z):
    parts = []
    off = 0
    for n in IN_SIZES:
        parts.append(z[..., off:off + n])
        off += n
    return parts


def hgrn_lower_bounds(lb_param):
    p = jax.nn.softmax(lb_param.astype(jnp.float32), axis=0)
    c = jnp.cumsum(p, axis=0)
    return c - c[0:1]


def hgrn2_mixer(q, f_logit, i, lb):
    b_, s_, _ = q.shape
    nc = s_ // CHUNK
    f32 = jnp.float32
    lbf = lb.astype(f32)
    zf = f_logit.astype(f32)
    log_f = jnp.logaddexp(jnp.log(lbf), jnp.log1p(-lbf) + jax.nn.log_sigmoid(zf))
    k = (1.0 - lbf) * jax.nn.sigmoid(-zf)

    def to_chunks(t, d):
        return t.astype(f32).reshape(b_, nc, CHUNK, HG_HEADS, d).transpose(1, 0, 3, 2, 4)

    xs = (to_chunks(q, HG_DK), to_chunks(k, HG_DK), to_chunks(i, HG_DV), to_chunks(log_f, HG_DK))
    causal = jnp.tril(jnp.ones((CHUNK, CHUNK), dtype=bool))[:, :, None]

    def step(state, inp):
        qb, kb, ib, gb = inp
        cum = jnp.cumsum(gb, axis=2)
        last = cum[:, :, -1:, :]
        diff = cum[:, :, :, None, :] - cum[:, :, None, :, :]
        decay = jnp.exp(jnp.where(causal, diff, -jnp.inf))
        scores = jnp.einsum('bhtk,bhtsk,bhsk->bhts', qb, decay, kb)
        out = (jnp.einsum('bhts,bhsv->bhtv', scores, ib)
               + jnp.einsum('bhtk,bhkv->bhtv', qb * jnp.exp(cum), state))
        new_state = (jnp.exp(last[:, :, 0, :, None]) * state
                     + jnp.einsum('bhsk,bhsv->bhkv', kb * jnp.exp(last - cum), ib))
        return new_state, out

    s0 = jnp.zeros((b_, HG_HEADS, HG_DK, HG_DV), f32)
    _, o = lax.scan(step, s0, xs)
    return o.transpose(1, 0, 3, 2, 4).reshape(b_, s_, HG_HEADS, HG_DV)


def s5_mixer(u, a_re, a_im, log_dt, b_re, b_im, c_re, c_im, d_skip, w_glu):
    b_, s_, _ = u.shape
    f32 = jnp.float32
    uf = u.astype(f32).reshape(b_, s_, S5_GROUPS, S5_GROUP)
    ar = a_re.astype(f32)
    ai = a_im.astype(f32)
    dt = jnp.exp(log_dt.astype(f32))[:, None]
    mag = jnp.exp(ar * dt)
    ang = ai * dt
    abar_re = mag * jnp.cos(ang)
    abar_im = mag * jnp.sin(ang)
    nr = abar_re - 1.0
    den = ar * ar + ai * ai
    zr = (nr * ar + abar_im * ai) / den
    zi = (abar_im * ar - nr * ai) / den
    br = b_re.astype(f32)
    bi = b_im.astype(f32)
    bbar_re = zr[:, :, None] * br - zi[:, :, None] * bi
    bbar_im = zr[:, :, None] * bi + zi[:, :, None] * br
    bu_re = jnp.einsum('bsgc,gpc->bsgp', uf, bbar_re)
    bu_im = jnp.einsum('bsgc,gpc->bsgp', uf, bbar_im)
    a_full_re = jnp.broadcast_to(abar_re, bu_re.shape)
    a_full_im = jnp.broadcast_to(abar_im, bu_im.shape)

    def combine(e1, e2):
        a1r, a1i, b1r, b1i = e1
        a2r, a2i, b2r, b2i = e2
        return (a2r * a1r - a2i * a1i,
                a2r * a1i + a2i * a1r,
                a2r * b1r - a2i * b1i + b2r,
                a2r * b1i + a2i * b1r + b2i)

    _, _, xr, xi = lax.associative_scan(combine, (a_full_re, a_full_im, bu_re, bu_im), axis=1)
    y = (jnp.einsum('bsgp,gcp->bsgc', xr, c_re.astype(f32))
         - jnp.einsum('bsgp,gcp->bsgc', xi, c_im.astype(f32))
         + d_skip.astype(f32).reshape(S5_GROUPS, S5_GROUP) * uf)
    y = jax.nn.gelu(y.reshape(b_, s_, S5_WIDTH))
    zg = jnp.einsum('bsw,wv->bsv', y, w_glu.astype(f32))
    return zg[..., :S5_WIDTH] * jax.nn.sigmoid(zg[..., S5_WIDTH:])


def fox_mixer(q, k, v, f_logit):
    b_, s_, _ = q.shape
    f32 = jnp.float32
    nb = s_ // Q_BLOCK

    def heads(t):
        return t.astype(f32).reshape(b_, s_, FOX_HEADS, FOX_DH).transpose(0, 2, 1, 3)

    qh, kh, vh = heads(q), heads(k), heads(v)
    c = jnp.cumsum(jax.nn.log_sigmoid(f_logit.astype(f32)), axis=1).transpose(0, 2, 1)
    pos = jnp.arange(s_)
    scale = FOX_DH ** -0.5
    q_blocks = qh.reshape(b_, FOX_HEADS, nb, Q_BLOCK, FOX_DH).transpose(2, 0, 1, 3, 4)
    c_blocks = c.reshape(b_, FOX_HEADS, nb, Q_BLOCK).transpose(2, 0, 1, 3)
    pos_blocks = pos.reshape(nb, Q_BLOCK)

    def block(args):
        qb, cb, pb = args
        s = (jnp.einsum('bhtd,bhsd->bhts', qb, kh) * scale
             + cb[..., None] - c[:, :, None, :])
        s = jnp.where(pb[:, None] >= pos[None, :], s, -jnp.inf)
        p = jax.nn.softmax(s, axis=-1)
        return jnp.einsum('bhts,bhsd->bhtd', p, vh)

    o = lax.map(block, (q_blocks, c_blocks, pos_blocks))
    return o.transpose(1, 0, 3, 2, 4).reshape(b_, s_, FOX_WIDTH)


def hybrid_layer(x, norm_g, w_in, b_gate, fox_bf, lb, hg_norm_g,
                 s5_a_re, s5_a_im, s5_log_dt, s5_b_re, s5_b_im, s5_c_re, s5_c_im,
                 s5_d, s5_w_glu, w_br_a, w_br_b, w_br_c, w_out):
    b_, s_, _ = x.shape
    dt = x.dtype
    h = rms_norm(x, norm_g)
    z = jnp.einsum('bsd,de->bse', h, w_in)
    (hq, hf, hi, hgate, su, sgate, fq, fk, fv, ff, fgate, mg) = split_columns(z)

    o_a = hgrn2_mixer(hq, hf, hi, lb)
    o_a = o_a * lax.rsqrt(jnp.mean(o_a * o_a, axis=-1, keepdims=True) + EPS)
    o_a = o_a.reshape(b_, s_, HG_VWIDTH) * hg_norm_g.astype(jnp.float32)
    o_a = (o_a * jax.nn.silu(hgate.astype(jnp.float32))).astype(dt)

    o_b = s5_mixer(su, s5_a_re, s5_a_im, s5_log_dt, s5_b_re, s5_b_im,
                   s5_c_re, s5_c_im, s5_d, s5_w_glu)
    o_b = (o_b * jax.nn.silu(sgate.astype(jnp.float32))).astype(dt)

    o_c = fox_mixer(fq, fk, fv, ff + fox_bf)
    o_c = (o_c * jax.nn.silu(fgate.astype(jnp.float32))).astype(dt)

    gates = jax.nn.sigmoid((mg + b_gate).astype(jnp.float32)).reshape(b_, s_, N_BRANCH, D_MODEL)
    merged = (gates[:, :, 0] * jnp.einsum('bsw,wd->bsd', o_a, w_br_a).astype(jnp.float32)
              + gates[:, :, 1] * jnp.einsum('bsw,wd->bsd', o_b, w_br_b).astype(jnp.float32)
              + gates[:, :, 2] * jnp.einsum('bsw,wd->bsd', o_c, w_br_c).astype(jnp.float32))
    return x + jnp.einsum('bsd,de->bse', merged.astype(dt), w_out).astype(dt)


def setup_inputs(seed: int = 0) -> dict:
    key = jax.random.key(seed)
    ks = jax.random.split(key, 24)
    f32 = jnp.float32
    nrm = lambda k, shape, s: jax.random.normal(k, shape, f32) * s
    n_idx = jnp.arange(S5_STATE, dtype=f32)
    return {
        'x': nrm(ks[0], (BATCH, SEQ, D_MODEL), 1.0),
        'norm_g': 1.0 + nrm(ks[1], (DEPTH, D_MODEL), 0.02),
        'w_in': nrm(ks[2], (DEPTH, D_MODEL, IN_WIDTH), D_MODEL ** -0.5),
        'b_gate': nrm(ks[3], (DEPTH, N_BRANCH * D_MODEL), 0.02),
        'fox_bf': jax.random.uniform(ks[4], (DEPTH, FOX_HEADS), f32, 1.0, 4.0),
        'hg_lb': nrm(ks[5], (DEPTH, HG_WIDTH), 0.1),
        'hg_norm_g': 1.0 + nrm(ks[6], (DEPTH, HG_VWIDTH), 0.02),
        's5_a_re': -0.5 + nrm(ks[7], (DEPTH, S5_GROUPS, S5_STATE), 0.01),
        's5_a_im': math.pi * n_idx + nrm(ks[8], (DEPTH, S5_GROUPS, S5_STATE), 0.01),
        's5_log_dt': jax.random.uniform(ks[9], (DEPTH, S5_GROUPS), f32, math.log(1e-3), math.log(1e-1)),
        's5_b_re': nrm(ks[10], (DEPTH, S5_GROUPS, S5_STATE, S5_GROUP), (2 * S5_GROUP) ** -0.5),
        's5_b_im': nrm(ks[11], (DEPTH, S5_GROUPS, S5_STATE, S5_GROUP), (2 * S5_GROUP) ** -0.5),
        's5_c_re': nrm(ks[12], (DEPTH, S5_GROUPS, S5_GROUP, S5_STATE), (2 * S5_STATE) ** -0.5),
        's5_c_im': nrm(ks[13], (DEPTH, S5_GROUPS, S5_GROUP, S5_STATE), (2 * S5_STATE) ** -0.5),
        's5_d': nrm(ks[14], (DEPTH, S5_WIDTH), 1.0),
        's5_w_glu': nrm(ks[15], (DEPTH, S5_WIDTH, 2 * S5_WIDTH), S5_WIDTH ** -0.5),
        'w_br_a': nrm(ks[16], (DEPTH, HG_VWIDTH, D_MODEL), HG_VWIDTH ** -0.5),
        'w_br_b': nrm(ks[17], (DEPTH, S5_WIDTH, D_MODEL), S5_WIDTH ** -0.5),
        'w_br_c': nrm(ks[18], (DEPTH, FOX_WIDTH, D_MODEL), FOX_WIDTH ** -0.5),
        'w_out': nrm(ks[19], (DEPTH, D_MODEL, D_MODEL), D_MODEL ** -0.5),
        'final_g': 1.0 + nrm(ks[20], (D_MODEL,), 0.02),
    }


def reference(x, norm_g, w_in, b_gate, fox_bf, hg_lb, hg_norm_g, s5_a_re, s5_a_im,
              s5_log_dt, s5_b_re, s5_b_im, s5_c_re, s5_c_im, s5_d, s5_w_glu,
              w_br_a, w_br_b, w_br_c, w_out, final_g):
    lbs = hgrn_lower_bounds(hg_lb)
    for l in range(DEPTH):
        x = hybrid_layer(x, norm_g[l], w_in[l], b_gate[l], fox_bf[l], lbs[l], hg_norm_g[l],
                         s5_a_re[l], s5_a_im[l], s5_log_dt[l], s5_b_re[l], s5_b_im[l],
                         s5_c_re[l], s5_c_im[l], s5_d[l], s5_w_glu[l],
                         w_br_a[l], w_br_b[l], w_br_c[l], w_out[l])
    return rms_norm(x, final_g)
```

```python
import contextlib, math
from concourse.bass_utils import run_bass_kernel_spmd
import numpy as np
import concourse.bass as bass
import concourse.mybir as mybir

F32 = mybir.dt.float32
ALU = mybir.AluOpType
AF = mybir.ActivationFunctionType
AX = mybir.AxisListType


class Buf:
    __slots__ = ("name", "w", "r")

    def __init__(self, name=""):
        self.name = name
        self.w = None
        self.r = []


class Prog:
    ENGS = ("pe", "dve", "act", "pool", "sp")
    NDMA = 12

    def __init__(self, nc):
        self.nc = nc
        self.ops = []
        self.floor = {}
        self.since = set()

    def add(self, eng, fn, reads=(), writes=(), dma=False):
        idx = len(self.ops)
        deps = set()
        for b in reads:
            if b.w is not None:
                deps.add(b.w)
        for b in writes:
            if b.w is not None:
                deps.add(b.w)
            deps.update(b.r)
        for b in reads:
            b.r.append(idx)
        for b in writes:
            b.w = idx
            b.r = []
        if eng in self.floor:
            deps |= self.floor.pop(eng)
        deps.discard(idx)
        self.ops.append(dict(eng=eng, fn=fn, deps=deps, dma=dma))
        if dma:
            self.since.add(idx)
        return idx

    def barrier(self):
        last = {}
        for i, o in enumerate(self.ops):
            if not o["dma"]:
                last[o["eng"]] = i
        fl = set(last.values()) | self.since
        for e in self.ENGS:
            self.floor[e] = set(fl) | self.floor.get(e, set())
        self.since = set()

    def pe(self, fn, reads=(), writes=()):
        return self.add("pe", fn, reads, writes)

    def dve(self, fn, reads=(), writes=()):
        return self.add("dve", fn, reads, writes)

    def act(self, fn, reads=(), writes=()):
        return self.add("act", fn, reads, writes)

    def pool(self, fn, reads=(), writes=()):
        return self.add("pool", fn, reads, writes)

    def dma(self, out, in_, reads=(), writes=(), q="sp"):
        return self.add(q, lambda e: e.dma_start(out=out, in_=in_), reads, writes, dma=True)

    def emit(self):
        nc = self.nc
        ops = self.ops
        per = {e: [] for e in self.ENGS}
        seq = {}
        for i, o in enumerate(ops):
            seq[i] = len(per[o["eng"]])
            per[o["eng"]].append(i)
        dmacount = {e: 0 for e in self.ENGS}
        dmainfo = {}
        for i, o in enumerate(ops):
            if o["dma"]:
                j = dmacount[o["eng"]]
                dmacount[o["eng"]] += 1
                dmainfo[i] = (o["eng"], j % self.NDMA, 16 * (j // self.NDMA + 1))
        import contextlib
        waits = {}
        signal = set()
        for ename in self.ENGS:
            waited = {}
            for i in per[ename]:
                o = ops[i]
                need = {}
                for d in o["deps"]:
                    od = ops[d]
                    if od["dma"]:
                        q, k, v = dmainfo[d]
                        sk = ("d", q, k)
                        need[sk] = max(need.get(sk, 0), v)
                    elif od["eng"] == ename:
                        if ename == "pe":
                            continue
                        if seq[i] - seq[d] <= 3:
                            sk = ("e", ename)
                            need[sk] = max(need.get(sk, 0), seq[d] + 1)
                    else:
                        sk = ("e", od["eng"])
                        need[sk] = max(need.get(sk, 0), seq[d] + 1)
                if o["dma"]:
                    q, k, v = dmainfo[i]
                    if v > 16:
                        sk = ("d", q, k)
                        need[sk] = max(need.get(sk, 0), v - 16)
                wl = []
                for sk, v in need.items():
                    if waited.get(sk, 0) >= v:
                        continue
                    waited[sk] = v
                    wl.append((sk, v))
                    if sk[0] == "e":
                        signal.add((sk[1], v - 1))
                waits[i] = wl
        cnt = {}
        for ename in self.ENGS:
            c = 0
            arr = []
            for sq_ in range(len(per[ename])):
                if (ename, sq_) in signal:
                    c += 1
                arr.append(c)
            cnt[ename] = arr
        self.n_signal = len(signal)
        with contextlib.ExitStack() as st:
            esem = {e: st.enter_context(nc.semaphore("s_" + e)) for e in self.ENGS}
            dsem = {}
            for e in self.ENGS:
                if dmacount[e]:
                    dsem[e] = [st.enter_context(nc.semaphore("d_%s_%d" % (e, k)))
                               for k in range(min(self.NDMA, dmacount[e]))]
            block = st.enter_context(nc.Block())

            def body(ename):
                def f(eng):
                    for i in per[ename]:
                        o = ops[i]
                        for sk, v in waits[i]:
                            if sk[0] == "e":
                                eng.wait_ge(esem[sk[1]], cnt[sk[1]][v - 1])
                            else:
                                eng.wait_ge(dsem[sk[1]][sk[2]], v)
                        ins = o["fn"](eng)
                        if o["dma"]:
                            q, k, v = dmainfo[i]
                            ins.then_inc(dsem[q][k], 16)
                        elif (ename, seq[i]) in signal:
                            ins.then_inc(esem[ename], 1)
                    if ename in dsem:
                        n = dmacount[ename]
                        for k in range(len(dsem[ename])):
                            c_ = (n - k + self.NDMA - 1) // self.NDMA
                            if c_ > 0:
                                eng.wait_ge(dsem[ename][k], 16 * c_)
                return f

            if per["sp"]:
                block.sync(body("sp"))
            if per["pe"]:
                block.tensor(body("pe"))
            if per["dve"]:
                block.vector(body("dve"))
            if per["act"]:
                block.scalar(body("act"))
            if per["pool"]:
                block.gpsimd(body("pool"))


D = 2048
S = 8192
NFM = 641
NTM = 512
NCOL = NFM + NTM
EPS = 1e-6


class Ctx:
    cnt = [0]

    def __init__(self, P, nc, st):
        self.P, self.nc, self.st = P, nc, st
        Ctx.cnt[0] += 1
        self.pfx = "c%d_" % Ctx.cnt[0]

    def sb(self, name, shape):
        return self.st.enter_context(self.nc.sbuf_tensor(self.pfx + name, shape, F32))

    def ps(self, name):
        return self.st.enter_context(self.nc.psum_tensor(self.pfx + name, [128, 512], F32))


def make_ident(P, c, n=128):
    ident = c.sb("ident", [128, 128]); bid = Buf()
    P.pool(lambda e: e.memset(ident[:, :], 0.0), writes=[bid])
    P.pool(lambda e: e.affine_select(out=ident[:, :], in_=ident[:, :], pattern=[[-1, 128]],
                                     compare_op=ALU.not_equal, fill=1.0, base=0, channel_multiplier=1),
           reads=[bid], writes=[bid])
    return ident, bid


def phase_a(P, nc, x, g, w, zT, z, bz):
    with contextlib.ExitStack() as st:
        c = Ctx(P, nc, st)
        KC = D // 128
        W = c.sb("W", [128, KC, NCOL]); bW = Buf("W")
        gb = c.sb("gb", [128, D]); bgb = Buf()
        ident, bid = make_ident(P, c)
        xt = [c.sb("xt%d" % i, [128, D]) for i in range(2)]; bxt = [Buf(), Buf()]
        junk = c.sb("junk", [128, D]); bjunk = Buf()
        stat = [c.sb("stat%d" % i, [128, 4]) for i in range(2)]; bstat = [Buf(), Buf()]
        h = [c.sb("h%d" % i, [128, D]) for i in range(2)]; bh = [Buf(), Buf()]
        hT = c.sb("hT", [128, KC, 512]); bhT = [Buf() for _ in range(4)]
        zst = [c.sb("zst%d" % i, [128, 512]) for i in range(3)]; bzst = [Buf() for _ in range(3)]
        zt2 = [c.sb("zt2%d" % i, [128, NTM]) for i in range(2)]; bzt2 = [Buf(), Buf()]
        pT = [c.ps("pT%d" % i) for i in range(2)]; bpT = [Buf(), Buf()]
        pF = [c.ps("pF%d" % i) for i in range(2)]; bpF = [Buf(), Buf()]
        pM = [c.ps("pM%d" % i) for i in range(2)]; bpM = [Buf() for _ in range(2)]

        P.dma(W[:, :, :], w.rearrange("(c p) n -> p c n", p=128), writes=[bW])
        P.dma(gb[:, :], g.partition_broadcast(128), writes=[bgb])
        nst = S // 512
        ti = 0
        zi = 0
        for s in range(nst):
            for tt in range(4):
                t0 = s * 512 + tt * 128
                b = ti % 2
                P.dma(xt[b][:, :], x[t0:t0 + 128, :], writes=[bxt[b]])
                P.act(lambda e, b=b: e.activation(out=junk[:, :], in_=xt[b][:, :], func=AF.Square,
                                                  accum_out=stat[b][:, 0:1]),
                      reads=[bxt[b]], writes=[bjunk, bstat[b]])
                P.dve(lambda e, b=b: e.tensor_scalar(out=stat[b][:, 1:2], in0=stat[b][:, 0:1], scalar1=1.0 / D,
                                                     scalar2=EPS, op0=ALU.mult, op1=ALU.add),
                      reads=[bstat[b]], writes=[bstat[b]])
                P.act(lambda e, b=b: e.activation(out=stat[b][:, 2:3], in_=stat[b][:, 1:2], func=AF.Sqrt),
                      reads=[bstat[b]], writes=[bstat[b]])
                P.dve(lambda e, b=b: e.reciprocal(out=stat[b][:, 3:4], in_=stat[b][:, 2:3]),
                      reads=[bstat[b]], writes=[bstat[b]])
                P.dve(lambda e, b=b: e.scalar_tensor_tensor(out=h[b][:, :], in0=xt[b][:, :],
                                                            scalar=stat[b][:, 3:4], in1=gb[:, :],
                                                            op0=ALU.mult, op1=ALU.mult),
                      reads=[bxt[b], bstat[b], bgb], writes=[bh[b]])
                for q in range(KC // 4):
                    pb = (ti * (KC // 4) + q) % 2
                    for jj in range(4):
                        j = q * 4 + jj
                        P.pe(lambda e, b=b, j=j, jj=jj, pb=pb: e.transpose(
                            out=pT[pb][:, jj * 128:(jj + 1) * 128], in_=h[b][:, j * 128:(j + 1) * 128],
                            identity=ident[:, :]),
                            reads=[bh[b], bid], writes=[bpT[pb]])
                    if q % 2 == 0:
                        P.act(lambda e, q=q, tt=tt, pb=pb: e.copy(
                            out=hT[:, q * 4:(q + 1) * 4, tt * 128:(tt + 1) * 128],
                            in_=pT[pb][:, :].rearrange("p (a b) -> p a b", a=4)),
                            reads=[bpT[pb]], writes=[bhT[tt]])
                    else:
                        P.dve(lambda e, q=q, tt=tt, pb=pb: e.tensor_copy(
                            out=hT[:, q * 4:(q + 1) * 4, tt * 128:(tt + 1) * 128],
                            in_=pT[pb][:, :].rearrange("p (a b) -> p a b", a=4)),
                            reads=[bpT[pb]], writes=[bhT[tt]])
                ti += 1
            for cb in range(6):
                pb = cb % 2
                m = 128 if cb < 5 else 1
                for j in range(KC):
                    P.pe(lambda e, cb=cb, j=j, pb=pb, m=m: e.matmul(
                        pF[pb][0:m, :], lhsT=W[:, j, cb * 128:cb * 128 + m], rhs=hT[:, j, :],
                        start=(j == 0), stop=(j == KC - 1)),
                        reads=[bW] + bhT, writes=[bpF[pb]])
                zb = zi % 3
                zi += 1
                if cb % 2 == 0:
                    P.act(lambda e, zb=zb, pb=pb, m=m: e.copy(out=zst[zb][0:m, :], in_=pF[pb][0:m, :]),
                          reads=[bpF[pb]], writes=[bzst[zb]])
                else:
                    P.dve(lambda e, zb=zb, pb=pb, m=m: e.tensor_copy(out=zst[zb][0:m, :], in_=pF[pb][0:m, :]),
                          reads=[bpF[pb]], writes=[bzst[zb]])
                P.dma(zT[cb * 128:cb * 128 + m, s * 512:(s + 1) * 512], zst[zb][0:m, :],
                      reads=[bzst[zb]], writes=[bz])
            for tt in range(4):
                t0 = s * 512 + tt * 128
                pa = tt % 2
                for j in range(KC):
                    P.pe(lambda e, j=j, tt=tt, pa=pa: e.matmul(
                        pM[pa][:, :], lhsT=hT[:, j, tt * 128:(tt + 1) * 128], rhs=W[:, j, NFM:NFM + 512],
                        start=(j == 0), stop=(j == KC - 1)),
                        reads=[bW, bhT[tt]], writes=[bpM[pa]])
                zb = tt % 2
                if tt % 2 == 0:
                    P.act(lambda e, zb=zb, pa=pa: e.copy(out=zt2[zb][:, :], in_=pM[pa][:, :]),
                          reads=[bpM[pa]], writes=[bzt2[zb]])
                else:
                    P.dve(lambda e, zb=zb, pa=pa: e.tensor_copy(out=zt2[zb][:, :], in_=pM[pa][:, :]),
                          reads=[bpM[pa]], writes=[bzt2[zb]])
                P.dma(z[t0:t0 + 128, :], zt2[zb][:, :], reads=[bzt2[zb]], writes=[bz])
    P.barrier()


def bc_mid(ap2, n):
    p, a = ap2.shape
    return ap2.unsqueeze(2).to_broadcast([p, a, n])


def phase_b(P, nc, zT, z, bz, lb0, lb1, lsel, hgn, o_a):
    SEG = 1024
    NCH = SEG // 64
    with contextlib.ExitStack() as st:
        c = Ctx(P, nc, st)
        ident, bid = make_ident(P, c)
        lbt = c.sb("lbt", [128, 8]); blb = Buf()
        P.dma(lbt[:, 0:1], lb0, writes=[blb])
        P.dma(lbt[:, 1:2], lb1, writes=[blb])
        P.dma(lbt[:, 2:3], lsel, writes=[blb])
        P.dve(lambda e: e.tensor_tensor(out=lbt[:, 3:4], in0=lbt[:, 1:2], in1=lbt[:, 0:1], op=ALU.subtract),
              reads=[blb], writes=[blb])
        P.act(lambda e: e.activation(out=lbt[:, 4:5], in_=lbt[:, 3:4], func=AF.Sigmoid), reads=[blb], writes=[blb])
        P.dve(lambda e: e.tensor_tensor(out=lbt[:, 5:6], in0=lbt[:, 4:5], in1=lbt[:, 2:3], op=ALU.mult),
              reads=[blb], writes=[blb])
        P.dve(lambda e: e.tensor_scalar(out=lbt[:, 6:7], in0=lbt[:, 5:6], scalar1=-1.0, scalar2=1.0,
                                        op0=ALU.mult, op1=ALU.add), reads=[blb], writes=[blb])
        gnb = c.sb("gnb", [64, 128]); bgn = Buf()
        P.dma(gnb[:, :], hgn.partition_broadcast(64), writes=[bgn])
        mask01 = c.sb("mask01", [128, SEG]); bmk = Buf()
        P.pool(lambda e: e.memset(mask01[:, :], 1.0), writes=[bmk])
        P.pool(lambda e: e.memset(mask01[:, :].rearrange("p (n c) -> p n c", c=64)[:, :, 0:1], 0.0),
               reads=[bmk], writes=[bmk])
        tri = c.sb("tri", [64, 64]); btri = Buf()
        P.pool(lambda e: e.memset(tri[:, :], 1.0), writes=[btri])
        P.pool(lambda e: e.affine_select(out=tri[:, :], in_=tri[:, :], pattern=[[1, 64]],
                                         compare_op=ALU.is_ge, fill=0.0, base=0, channel_multiplier=-1),
               reads=[btri], writes=[btri])
        names = ["q", "f", "lf", "cum", "kk", "A", "E2", "qt", "kd"]
        T = {n: c.sb("t_" + n, [128, SEG]) for n in names}
        B = {n: Buf(n) for n in names}
        sm = c.sb("sm", [128, 4, NCH]); bsm = Buf()
        i_tm = c.sb("i_tm", [64, NCH, 128]); bi = Buf()
        g_tm = c.sb("g_tm", [64, NCH, 128]); bg = Buf()
        kd_tm = c.sb("kd_tm", [64, NCH, 128]); bkdt = [Buf() for _ in range(NCH // 4)]
        sT = c.sb("sT", [64, NCH, 64]); bsT = [Buf() for _ in range(NCH // 8)]
        Sall = c.sb("Sall", [128, NCH + 1, 128]); bS = [Buf() for _ in range(NCH + 1)]
        oseg = c.sb("oseg", [64, NCH, 128]); bo = [Buf() for _ in range(NCH // 4)]
        sq = c.sb("sq", [64, NCH, 128]); bsq = Buf()
        st2 = c.sb("st2", [64, 4, NCH]); bst2 = Buf()
        pK = c.ps("pK"); bpK = Buf()
        pS = c.ps("pS"); bpS = Buf()
        pU = [c.ps("pU0"), c.ps("pU1")]; bpU = [Buf(), Buf()]
        pO = [c.ps("pO0"), c.ps("pO1")]; bpO = [Buf(), Buf()]
        bout = Buf()

        P.dve(lambda e: e.memset(Sall[:, 0, :], 0.0), writes=[bS[0]])
        v3 = lambda t: t[:, :].rearrange("p (n c) -> p n c", c=64)
        for seg in range(S // SEG):
            r0 = seg * SEG
            P.dma(T["q"][:, :], zT[0:128, r0:r0 + SEG], reads=[bz], writes=[B["q"]])
            P.dma(T["f"][:, :], zT[128:256, r0:r0 + SEG], reads=[bz], writes=[B["f"]])
            P.dma(i_tm[:, :, :], z[r0:r0 + SEG, 0:128].rearrange("(n p) v -> p n v", p=64), reads=[bz], writes=[bi])
            P.dma(g_tm[:, :, :], z[r0:r0 + SEG, 128:256].rearrange("(n p) v -> p n v", p=64), reads=[bz],
                  writes=[bg])
            if seg > 0:
                P.dve(lambda e: e.tensor_copy(out=Sall[:, 0, :], in_=Sall[:, NCH, :]),
                      reads=[bS[NCH]], writes=[bS[0]])
            P.act(lambda e: e.activation(out=T["f"][:, :], in_=T["f"][:, :], func=AF.Sigmoid),
                  reads=[B["f"]], writes=[B["f"]])
            P.dve(lambda e: e.tensor_scalar(out=T["f"][:, :], in0=T["f"][:, :], scalar1=lbt[:, 6:7],
                                            scalar2=lbt[:, 5:6], op0=ALU.mult, op1=ALU.add),
                  reads=[B["f"], blb], writes=[B["f"]])
            P.act(lambda e: e.activation(out=T["lf"][:, :], in_=T["f"][:, :], func=AF.Ln),
                  reads=[B["f"]], writes=[B["lf"]])
            P.pool(lambda e: e.tensor_scalar(out=T["kk"][:, :], in0=T["f"][:, :], scalar1=-1.0, scalar2=1.0,
                                             op0=ALU.mult, op1=ALU.add),
                   reads=[B["f"]], writes=[B["kk"]])
            P.dve(lambda e: e.tensor_tensor_scan(out=T["cum"][:, :], data0=mask01[:, :], data1=T["lf"][:, :],
                                                 initial=0.0, op0=ALU.mult, op1=ALU.add),
                  reads=[B["lf"], bmk], writes=[B["cum"]])
            cum3 = v3(T["cum"])
            last = cum3[:, :, 63]
            mid = cum3[:, :, 31]
            P.act(lambda e: e.activation(out=sm[:, 0, :], in_=mid, func=AF.Exp, scale=-1.0),
                  reads=[B["cum"]], writes=[bsm])
            P.dve(lambda e: e.tensor_tensor(out=sm[:, 3, :], in0=last, in1=mid, op=ALU.subtract),
                  reads=[B["cum"]], writes=[bsm])
            P.act(lambda e: e.activation(out=sm[:, 1, :], in_=sm[:, 3, :], func=AF.Exp), reads=[bsm], writes=[bsm])
            P.act(lambda e: e.activation(out=sm[:, 2, :], in_=last, func=AF.Exp), reads=[B["cum"]], writes=[bsm])
            P.act(lambda e: e.activation(out=T["A"][:, :], in_=T["cum"][:, :], func=AF.Exp),
                  reads=[B["cum"]], writes=[B["A"]])
            P.dve(lambda e: e.tensor_tensor(out=T["A"][:, :], in0=T["A"][:, :], in1=T["q"][:, :], op=ALU.mult),
                  reads=[B["A"], B["q"]], writes=[B["A"]])
            P.dve(lambda e: e.tensor_tensor(out=v3(T["E2"]), in0=bc_mid(mid, 64), in1=cum3, op=ALU.subtract),
                  reads=[B["cum"]], writes=[B["E2"]])
            P.act(lambda e: e.activation(out=T["E2"][:, :], in_=T["E2"][:, :], func=AF.Exp),
                  reads=[B["E2"]], writes=[B["E2"]])
            P.pool(lambda e: e.tensor_tensor(out=T["E2"][:, :], in0=T["E2"][:, :], in1=T["kk"][:, :], op=ALU.mult),
                   reads=[B["E2"], B["kk"]], writes=[B["E2"]])
            P.dve(lambda e: e.tensor_tensor(out=v3(T["qt"]), in0=v3(T["A"]), in1=bc_mid(sm[:, 0, :], 64),
                                            op=ALU.mult),
                  reads=[B["A"], bsm], writes=[B["qt"]])
            P.pool(lambda e: e.tensor_tensor(out=v3(T["kd"]), in0=v3(T["E2"]), in1=bc_mid(sm[:, 1, :], 64),
                                             op=ALU.mult),
                   reads=[B["E2"], bsm], writes=[B["kd"]])
            for q4 in range(NCH // 4):
                for jj in range(4):
                    n = q4 * 4 + jj
                    P.pe(lambda e, n=n, jj=jj: e.transpose(out=pK[0:64, jj * 128:(jj + 1) * 128],
                                                           in_=T["kd"][:, n * 64:(n + 1) * 64],
                                                           identity=ident[:, :]),
                         reads=[B["kd"], bid], writes=[bpK])
                P.act(lambda e, q4=q4: e.copy(out=kd_tm[:, q4 * 4:(q4 + 1) * 4, :],
                                              in_=pK[0:64, :].rearrange("p (a b) -> p a b", a=4)),
                      reads=[bpK], writes=[bkdt[q4]])
            for q8 in range(NCH // 8):
                for jj in range(8):
                    n = q8 * 8 + jj
                    P.pe(lambda e, n=n, jj=jj: e.matmul(pS[0:64, jj * 64:(jj + 1) * 64],
                                                        lhsT=T["E2"][:, n * 64:(n + 1) * 64],
                                                        rhs=T["qt"][:, n * 64:(n + 1) * 64], start=True, stop=True),
                         reads=[B["E2"], B["qt"]], writes=[bpS])
                P.dve(lambda e, q8=q8: e.tensor_tensor(
                    out=sT[:, q8 * 8:(q8 + 1) * 8, :],
                    in0=pS[0:64, :].rearrange("p (a b) -> p a b", a=8),
                    in1=tri[:, :].unsqueeze(1).to_broadcast([64, 8, 64]), op=ALU.mult),
                    reads=[bpS, btri], writes=[bsT[q8]])
            for q4 in range(NCH // 4):
                ub = q4 % 2
                for jj in range(4):
                    n = q4 * 4 + jj
                    P.pe(lambda e, n=n, jj=jj, ub=ub: e.matmul(pU[ub][:, jj * 128:(jj + 1) * 128],
                                                               lhsT=kd_tm[:, n, :], rhs=i_tm[:, n, :],
                                                               start=True, stop=True),
                         reads=[bkdt[q4], bi], writes=[bpU[ub]])
                for jj in range(4):
                    n = q4 * 4 + jj
                    P.dve(lambda e, n=n, jj=jj, ub=ub: e.scalar_tensor_tensor(
                        out=Sall[:, n + 1, :], in0=Sall[:, n, :], scalar=sm[:, 2, n:n + 1],
                        in1=pU[ub][:, jj * 128:(jj + 1) * 128], op0=ALU.mult, op1=ALU.add),
                        reads=[bS[n], bsm, bpU[ub]], writes=[bS[n + 1]])
            for q4 in range(NCH // 4):
                ob = q4 % 2
                for jj in range(4):
                    n = q4 * 4 + jj
                    P.pe(lambda e, n=n, jj=jj, ob=ob: e.matmul(pO[ob][0:64, jj * 128:(jj + 1) * 128],
                                                               lhsT=T["A"][:, n * 64:(n + 1) * 64],
                                                               rhs=Sall[:, n, :], start=True, stop=False),
                         reads=[B["A"], bS[n]], writes=[bpO[ob]])
                    P.pe(lambda e, n=n, jj=jj, ob=ob: e.matmul(pO[ob][0:64, jj * 128:(jj + 1) * 128],
                                                               lhsT=sT[:, n, :], rhs=i_tm[:, n, :],
                                                               start=False, stop=True),
                         reads=[bsT[n // 8], bi], writes=[bpO[ob]])
                P.act(lambda e, q4=q4, ob=ob: e.copy(out=oseg[:, q4 * 4:(q4 + 1) * 4, :],
                                                     in_=pO[ob][0:64, :].rearrange("p (a b) -> p a b", a=4)),
                      reads=[bpO[ob]], writes=[bo[q4]])
            P.pool(lambda e: e.tensor_tensor(out=sq[:, :, :], in0=oseg[:, :, :], in1=oseg[:, :, :], op=ALU.mult),
                   reads=bo, writes=[bsq])
            P.dve(lambda e: e.tensor_reduce(out=st2[:, 0, :], in_=sq[:, :, :], axis=AX.X, op=ALU.add),
                  reads=[bsq], writes=[bst2])
            P.dve(lambda e: e.tensor_scalar(out=st2[:, 1, :], in0=st2[:, 0, :], scalar1=1.0 / 128, scalar2=EPS,
                                            op0=ALU.mult, op1=ALU.add), reads=[bst2], writes=[bst2])
            P.act(lambda e: e.activation(out=st2[:, 2, :], in_=st2[:, 1, :], func=AF.Sqrt), reads=[bst2],
                  writes=[bst2])
            P.dve(lambda e: e.reciprocal(out=st2[:, 3, :], in_=st2[:, 2, :]), reads=[bst2], writes=[bst2])
            P.dve(lambda e: e.tensor_tensor(out=oseg[:, :, :], in0=oseg[:, :, :], in1=bc_mid(st2[:, 3, :], 128),
                                            op=ALU.mult), reads=bo + [bst2], writes=bo)
            P.pool(lambda e: e.tensor_tensor(out=oseg[:, :, :], in0=oseg[:, :, :],
                                             in1=gnb[:, :].unsqueeze(1).to_broadcast([64, NCH, 128]), op=ALU.mult),
                   reads=bo + [bgn], writes=bo)
            P.act(lambda e: e.activation(out=g_tm[:, :, :], in_=g_tm[:, :, :], func=AF.Silu), reads=[bg],
                  writes=[bg])
            P.dve(lambda e: e.tensor_tensor(out=oseg[:, :, :], in0=oseg[:, :, :], in1=g_tm[:, :, :], op=ALU.mult),
                  reads=bo + [bg], writes=bo)
            P.dma(o_a[r0:r0 + SEG, :].rearrange("(n p) v -> p n v", p=64), oseg[:, :, :], reads=bo, writes=[bout])
    P.barrier()


def phase_c(P, nc, zT, bz, s5p, Bre, Bim, Cre, Cim, dsk, yT):
    L = 512
    NB = S // L
    PI = math.pi
    with contextlib.ExitStack() as st:
        c = Ctx(P, nc, st)
        par = c.sb("par", [128, 3, 4]); bpar = Buf()
        P.dma(par[:, :, :], s5p, writes=[bpar])
        wB = c.sb("wB", [128, 2, 4, 128]); bwB = Buf()
        P.dma(wB[:, 0, :, :], Bre, writes=[bwB])
        P.dma(wB[:, 1, :, :], Bim, writes=[bwB])
        wC = c.sb("wC", [128, 2, 4, 128]); bwC = Buf()
        P.dma(wC[:, 0, :, :], Cre, writes=[bwC])
        P.dma(wC[:, 1, :, :], Cim, writes=[bwC])
        P.dve(lambda e: e.tensor_scalar(out=wC[:, 1, :, :], in0=wC[:, 1, :, :], scalar1=-1.0, scalar2=None,
                                        op0=ALU.mult), reads=[bwC], writes=[bwC])
        dk = c.sb("dk", [128, 1]); bdk = Buf()
        P.dma(dk[:, :], dsk, writes=[bdk])
        NV = 40
        sc = c.sb("sc", [128, NV, 4]); bsc = Buf()
        names = {}

        vb = {}

        def V(n):
            if n not in names:
                names[n] = len(names)
                vb[n] = Buf(n)
                assert names[n] < NV
            return sc[:, names[n], :]

        def VB(*ns):
            for n in ns:
                V(n)
            return [vb[n] for n in ns]

        def tt(o, a, b, op):
            P.dve(lambda e: e.tensor_tensor(out=V(o), in0=V(a), in1=V(b), op=op), reads=VB(a, b), writes=VB(o))

        def ts(o, a, s1, op0, s2=None, op1=None):
            if op1 is None:
                P.dve(lambda e: e.tensor_scalar(out=V(o), in0=V(a), scalar1=s1, scalar2=None, op0=op0),
                      reads=VB(a), writes=VB(o))
            else:
                P.dve(lambda e: e.tensor_scalar(out=V(o), in0=V(a), scalar1=s1, scalar2=s2, op0=op0, op1=op1),
                      reads=VB(a), writes=VB(o))

        def stt(o, a, s, b, op0, op1):
            P.dve(lambda e: e.scalar_tensor_tensor(out=V(o), in0=V(a), scalar=s, in1=V(b), op0=op0, op1=op1),
                  reads=VB(a, b), writes=VB(o))

        def act(o, a, f, scale=1.0):
            P.act(lambda e: e.activation(out=V(o), in_=V(a), func=f, scale=scale), reads=VB(a), writes=VB(o))

        for n_, k_ in (("ar", 0), ("ai", 1), ("ldt", 2)):
            P.dve(lambda e, n_=n_, k_=k_: e.tensor_copy(out=V(n_), in_=par[:, k_, :]), reads=[bpar], writes=VB(n_))
        act("dt", "ldt", AF.Exp)
        tt("m1", "ar", "dt", ALU.mult)
        act("mag", "m1", AF.Exp)
        tt("ang", "ai", "dt", ALU.mult)
        ts("kq", "ang", PI, ALU.is_gt)
        for m_ in range(1, 7):
            stt("kq", "ang", (2 * m_ + 1) * PI, "kq", ALU.is_gt, ALU.add)
        stt("y", "kq", -2.0 * PI, "ang", ALU.mult, ALU.add)
        ts("x8", "y", 0.125, ALU.mult)
        tt("x2", "x8", "x8", ALU.mult)
        ts("p", "x2", -1.0 / 5040, ALU.mult)
        stt("p", "p", 1.0 / 120, "x2", ALU.add, ALU.mult)
        stt("p", "p", -1.0 / 6, "x2", ALU.add, ALU.mult)
        stt("s", "p", 1.0, "x8", ALU.add, ALU.mult)
        ts("q", "x2", 1.0 / 40320, ALU.mult)
        stt("q", "q", -1.0 / 720, "x2", ALU.add, ALU.mult)
        stt("q", "q", 1.0 / 24, "x2", ALU.add, ALU.mult)
        stt("q", "q", -0.5, "x2", ALU.add, ALU.mult)
        ts("c", "q", 1.0, ALU.add)
        for _ in range(3):
            tt("cc", "c", "c", ALU.mult)
            tt("ss", "s", "s", ALU.mult)
            stt("s", "s", 2.0, "c", ALU.mult, ALU.mult)
            tt("c", "cc", "ss", ALU.subtract)
        tt("abr", "mag", "c", ALU.mult)
        tt("abi", "mag", "s", ALU.mult)
        ts("nr", "abr", -1.0, ALU.add)
        tt("d1", "ar", "ar", ALU.mult)
        tt("d2", "ai", "ai", ALU.mult)
        tt("den", "d1", "d2", ALU.add)
        P.dve(lambda e: e.reciprocal(out=V("rden"), in_=V("den")), reads=VB("den"), writes=VB("rden"))
        tt("t1", "nr", "ar", ALU.mult)
        tt("t2", "abi", "ai", ALU.mult)
        tt("t1", "t1", "t2", ALU.add)
        tt("zr", "t1", "rden", ALU.mult)
        tt("t1", "abi", "ar", ALU.mult)
        tt("t2", "nr", "ai", ALU.mult)
        tt("t1", "t1", "t2", ALU.subtract)
        tt("zi", "t1", "rden", ALU.mult)
        pw = c.sb("pw", [128, 2, 10, 4]); bpw = Buf()
        P.dve(lambda e: e.tensor_copy(out=pw[:, 0, 0, :], in_=V("c")), reads=VB('c', 's'), writes=[bpw])
        P.dve(lambda e: e.tensor_copy(out=pw[:, 1, 0, :], in_=V("s")), reads=VB('c', 's'), writes=[bpw])
        for k in range(9):
            P.dve(lambda e, k=k: e.tensor_tensor(out=V("cc"), in0=pw[:, 0, k, :], in1=pw[:, 0, k, :], op=ALU.mult),
                  reads=[bpw], writes=VB('cc', 'ss'))
            P.dve(lambda e, k=k: e.tensor_tensor(out=V("ss"), in0=pw[:, 1, k, :], in1=pw[:, 1, k, :], op=ALU.mult),
                  reads=[bpw], writes=VB('cc', 'ss'))
            P.dve(lambda e, k=k: e.scalar_tensor_tensor(out=pw[:, 1, k + 1, :], in0=pw[:, 1, k, :], scalar=2.0,
                                                        in1=pw[:, 0, k, :], op0=ALU.mult, op1=ALU.mult),
                  reads=[bpw] + VB('cc', 'ss'), writes=[bpw])
            P.dve(lambda e, k=k: e.tensor_tensor(out=pw[:, 0, k + 1, :], in0=V("cc"), in1=V("ss"), op=ALU.subtract),
                  reads=[bpw] + VB('cc', 'ss'), writes=[bpw])
        Ec = c.sb("Ec", [128, 4, L]); Es = c.sb("Es", [128, 4, L])
        Tr = c.sb("Tr", [128, 4, L]); Ti = c.sb("Ti", [128, 4, L]); Rr = c.sb("Rr", [128, 4, L])
        btab = [Buf() for _ in range(4)]
        tmpa = c.sb("tmpa", [128, L]); btmp = Buf()
        for j in range(4):
            bt = btab[j]
            P.pool(lambda e, j=j: e.memset(Ec[:, j, 0:1], 1.0), writes=[bt])
            P.pool(lambda e, j=j: e.memset(Es[:, j, 0:1], 0.0), writes=[bt])
            P.pool(lambda e, j=j: e.memset(Rr[:, j, :], 1.0), writes=[bt])
            P.dve(lambda e, j=j: e.tensor_scalar(out=Rr[:, j, :], in0=Rr[:, j, :], scalar1=V("mag")[:, j:j + 1],
                                                 scalar2=None, op0=ALU.mult), reads=[bt] + VB('mag'), writes=[bt])
            for k in range(9):
                n = 1 << k
                ck = pw[:, 0, k, j:j + 1]
                sk = pw[:, 1, k, j:j + 1]
                P.dve(lambda e, j=j, n=n, sk=sk: e.tensor_scalar(out=tmpa[:, 0:n], in0=Es[:, j, 0:n], scalar1=sk,
                                                                 scalar2=None, op0=ALU.mult),
                      reads=[bt, bpw], writes=[btmp])
                P.dve(lambda e, j=j, n=n, ck=ck: e.scalar_tensor_tensor(out=Ec[:, j, n:2 * n], in0=Ec[:, j, 0:n],
                                                                        scalar=ck, in1=tmpa[:, 0:n],
                                                                        op0=ALU.mult, op1=ALU.subtract),
                      reads=[bt, bpw, btmp], writes=[bt])
                P.dve(lambda e, j=j, n=n, ck=ck: e.tensor_scalar(out=tmpa[:, 0:n], in0=Es[:, j, 0:n], scalar1=ck,
                                                                 scalar2=None, op0=ALU.mult),
                      reads=[bt, bpw], writes=[btmp])
                P.dve(lambda e, j=j, n=n, sk=sk: e.scalar_tensor_tensor(out=Es[:, j, n:2 * n], in0=Ec[:, j, 0:n],
                                                                        scalar=sk, in1=tmpa[:, 0:n],
                                                                        op0=ALU.mult, op1=ALU.add),
                      reads=[bt, bpw, btmp], writes=[bt])
            zr = V("zr")[:, j:j + 1]
            zi = V("zi")[:, j:j + 1]
            P.dve(lambda e, j=j, zi=zi: e.tensor_scalar(out=tmpa[:, :], in0=Es[:, j, :], scalar1=zi, scalar2=None,
                                                        op0=ALU.mult), reads=[bt] + VB('zr', 'zi'), writes=[btmp])
            P.dve(lambda e, j=j, zr=zr: e.scalar_tensor_tensor(out=Tr[:, j, :], in0=Ec[:, j, :], scalar=zr,
                                                               in1=tmpa[:, :], op0=ALU.mult, op1=ALU.add),
                  reads=[bt, btmp] + VB('zr', 'zi'), writes=[bt])
            P.dve(lambda e, j=j, zr=zr: e.tensor_scalar(out=tmpa[:, :], in0=Es[:, j, :], scalar1=zr, scalar2=None,
                                                        op0=ALU.mult), reads=[bt] + VB('zr', 'zi'), writes=[btmp])
            P.dve(lambda e, j=j, zi=zi: e.scalar_tensor_tensor(out=Ti[:, j, :], in0=Ec[:, j, :], scalar=zi,
                                                               in1=tmpa[:, :], op0=ALU.mult, op1=ALU.subtract),
                  reads=[bt, btmp] + VB('zr', 'zi'), writes=[bt])
        uT = [c.sb("uT%d" % i, [128, L]) for i in range(2)]; bu = [Buf(), Buf()]
        brs = c.sb("brs", [128, L]); bis = c.sb("bis", [128, L]); bbs = Buf()
        m1 = c.sb("m1", [128, L]); m2 = c.sb("m2", [128, L]); m3 = c.sb("m3", [128, L]); m4 = c.sb("m4", [128, L])
        bm = [Buf() for _ in range(4)]
        vr = c.sb("vr", [128, L]); vi = c.sb("vi", [128, L]); bv = [Buf(), Buf()]
        wr = c.sb("wr", [128, L]); wi = c.sb("wi", [128, L]); bw = [Buf(), Buf()]
        xr = c.sb("xr", [128, 4, L]); xi = c.sb("xi", [128, 4, L]); bx = [[Buf(), Buf()] for _ in range(4)]
        ini = c.sb("ini", [128, 4, 4]); bini = [Buf() for _ in range(4)]
        yo = [c.sb("yo%d" % i, [128, L]) for i in range(2)]; byo = [Buf(), Buf()]
        g1 = c.sb("g1", [128, L]); g2 = c.sb("g2", [128, L]); bg = [Buf(), Buf()]
        pB = [c.ps("pBr"), c.ps("pBi")]; bpB = [Buf(), Buf()]
        pY = [c.ps("pY0"), c.ps("pY1")]; bpY = [Buf(), Buf()]
        bout = Buf()
        GC = math.sqrt(2.0 / math.pi)
        for b in range(NB):
            ub = b % 2
            P.dma(uT[ub][:, :], zT[256:384, b * L:(b + 1) * L], reads=[bz], writes=[bu[ub]])
            for j in range(4):
                bt = btab[j]
                P.pe(lambda e, j=j, ub=ub: e.matmul(pB[0][:, :], lhsT=wB[:, 0, j, :], rhs=uT[ub][:, :],
                                                    start=True, stop=True), reads=[bwB, bu[ub]], writes=[bpB[0]])
                P.pe(lambda e, j=j, ub=ub: e.matmul(pB[1][:, :], lhsT=wB[:, 1, j, :], rhs=uT[ub][:, :],
                                                    start=True, stop=True), reads=[bwB, bu[ub]], writes=[bpB[1]])
                P.act(lambda e: e.copy(out=brs[:, :], in_=pB[0][:, :]), reads=[bpB[0]], writes=[bbs])
                P.act(lambda e: e.copy(out=bis[:, :], in_=pB[1][:, :]), reads=[bpB[1]], writes=[bbs])
                P.dve(lambda e, j=j: e.tensor_tensor(out=m1[:, :], in0=Tr[:, j, :], in1=brs[:, :], op=ALU.mult),
                      reads=[bt, bbs], writes=[bm[0]])
                P.pool(lambda e, j=j: e.tensor_tensor(out=m2[:, :], in0=Ti[:, j, :], in1=bis[:, :], op=ALU.mult),
                       reads=[bt, bbs], writes=[bm[1]])
                P.dve(lambda e, j=j: e.tensor_tensor(out=m3[:, :], in0=Tr[:, j, :], in1=bis[:, :], op=ALU.mult),
                      reads=[bt, bbs], writes=[bm[2]])
                P.pool(lambda e, j=j: e.tensor_tensor(out=m4[:, :], in0=Ti[:, j, :], in1=brs[:, :], op=ALU.mult),
                       reads=[bt, bbs], writes=[bm[3]])
                P.pool(lambda e: e.tensor_tensor(out=vr[:, :], in0=m1[:, :], in1=m2[:, :], op=ALU.subtract),
                       reads=[bm[0], bm[1]], writes=[bv[0]])
                P.pool(lambda e: e.tensor_tensor(out=vi[:, :], in0=m3[:, :], in1=m4[:, :], op=ALU.add),
                       reads=[bm[2], bm[3]], writes=[bv[1]])
                if b == 0:
                    P.dve(lambda e, j=j: e.memset(ini[:, j, :], 0.0), writes=[bini[j]])
                else:
                    c0 = pw[:, 0, 0, j:j + 1]
                    s0 = pw[:, 1, 0, j:j + 1]
                    xl = xr[:, j, L - 1:L]
                    yl = xi[:, j, L - 1:L]
                    P.dve(lambda e, j=j, s0=s0, yl=yl: e.tensor_tensor(out=ini[:, j, 2:3], in0=yl, in1=s0,
                                                                       op=ALU.mult),
                          reads=[bx[j][1], bpw], writes=[bini[j]])
                    P.dve(lambda e, j=j, c0=c0, xl=xl: e.scalar_tensor_tensor(out=ini[:, j, 0:1], in0=xl, scalar=c0,
                                                                              in1=ini[:, j, 2:3], op0=ALU.mult,
                                                                              op1=ALU.subtract),
                          reads=[bx[j][0], bpw, bini[j]], writes=[bini[j]])
                    P.dve(lambda e, j=j, c0=c0, yl=yl: e.tensor_tensor(out=ini[:, j, 3:4], in0=yl, in1=c0,
                                                                       op=ALU.mult),
                          reads=[bx[j][1], bpw], writes=[bini[j]])
                    P.dve(lambda e, j=j, s0=s0, xl=xl: e.scalar_tensor_tensor(out=ini[:, j, 1:2], in0=xl, scalar=s0,
                                                                              in1=ini[:, j, 3:4], op0=ALU.mult,
                                                                              op1=ALU.add),
                          reads=[bx[j][0], bpw, bini[j]], writes=[bini[j]])
                P.dve(lambda e, j=j: e.tensor_tensor_scan(out=wr[:, :], data0=Rr[:, j, :], data1=vr[:, :],
                                                          initial=ini[:, j, 0:1], op0=ALU.mult, op1=ALU.add),
                      reads=[bt, bv[0], bini[j]], writes=[bw[0]])
                P.dve(lambda e, j=j: e.tensor_tensor_scan(out=wi[:, :], data0=Rr[:, j, :], data1=vi[:, :],
                                                          initial=ini[:, j, 1:2], op0=ALU.mult, op1=ALU.add),
                      reads=[bt, bv[1], bini[j]], writes=[bw[1]])
                P.dve(lambda e, j=j: e.tensor_tensor(out=m1[:, :], in0=Ec[:, j, :], in1=wr[:, :], op=ALU.mult),
                      reads=[bt, bw[0]], writes=[bm[0]])
                P.pool(lambda e, j=j: e.tensor_tensor(out=m2[:, :], in0=Es[:, j, :], in1=wi[:, :], op=ALU.mult),
                       reads=[bt, bw[1]], writes=[bm[1]])
                P.dve(lambda e, j=j: e.tensor_tensor(out=m3[:, :], in0=Es[:, j, :], in1=wr[:, :], op=ALU.mult),
                      reads=[bt, bw[0]], writes=[bm[2]])
                P.pool(lambda e, j=j: e.tensor_tensor(out=m4[:, :], in0=Ec[:, j, :], in1=wi[:, :], op=ALU.mult),
                       reads=[bt, bw[1]], writes=[bm[3]])
                P.pool(lambda e, j=j: e.tensor_tensor(out=xr[:, j, :], in0=m1[:, :], in1=m2[:, :], op=ALU.subtract),
                       reads=[bm[0], bm[1]], writes=[bx[j][0]])
                P.pool(lambda e, j=j: e.tensor_tensor(out=xi[:, j, :], in0=m3[:, :], in1=m4[:, :], op=ALU.add),
                       reads=[bm[2], bm[3]], writes=[bx[j][1]])
            yb = b % 2
            for j in range(4):
                P.pe(lambda e, j=j, yb=yb: e.matmul(pY[yb][:, :], lhsT=wC[:, 0, j, :], rhs=xr[:, j, :],
                                                    start=(j == 0), stop=False),
                     reads=[bwC, bx[j][0]], writes=[bpY[yb]])
                P.pe(lambda e, j=j, yb=yb: e.matmul(pY[yb][:, :], lhsT=wC[:, 1, j, :], rhs=xi[:, j, :],
                                                    start=False, stop=(j == 3)),
                     reads=[bwC, bx[j][1]], writes=[bpY[yb]])
            P.dve(lambda e, yb=yb, ub=ub: e.scalar_tensor_tensor(out=yo[yb][:, :], in0=uT[ub][:, :], scalar=dk[:, 0:1],
                                                                 in1=pY[yb][:, :], op0=ALU.mult, op1=ALU.add),
                  reads=[bu[ub], bdk, bpY[yb]], writes=[byo[yb]])
            P.pool(lambda e, yb=yb: e.tensor_tensor(out=g1[:, :], in0=yo[yb][:, :], in1=yo[yb][:, :], op=ALU.mult),
                   reads=[byo[yb]], writes=[bg[0]])
            P.pool(lambda e: e.tensor_scalar(out=g1[:, :], in0=g1[:, :], scalar1=0.044715, scalar2=1.0,
                                             op0=ALU.mult, op1=ALU.add), reads=[bg[0]], writes=[bg[0]])
            P.pool(lambda e, yb=yb: e.tensor_tensor(out=g1[:, :], in0=g1[:, :], in1=yo[yb][:, :], op=ALU.mult),
                   reads=[bg[0], byo[yb]], writes=[bg[0]])
            P.act(lambda e: e.activation(out=g2[:, :], in_=g1[:, :], func=AF.Sigmoid, scale=2.0 * GC),
                  reads=[bg[0]], writes=[bg[1]])
            P.dve(lambda e, yb=yb: e.tensor_tensor(out=yo[yb][:, :], in0=yo[yb][:, :], in1=g2[:, :], op=ALU.mult),
                  reads=[byo[yb], bg[1]], writes=[byo[yb]])
            P.dma(yT[:, b * L:(b + 1) * L], yo[yb][:, :], reads=[byo[yb]], writes=[bout])
    P.barrier()


def phase_d(P, nc, zT, z, bz, bf, o_c):
    NKB = S // 128
    NQG = S // 512
    SCALE = 128 ** -0.5
    with contextlib.ExitStack() as st:
        c = Ctx(P, nc, st)
        qT = c.sb("qT", [128, S]); bq = Buf()
        kT = c.sb("kT", [128, S]); bk = Buf()
        va = c.sb("va", [128, NKB, 129]); bva = Buf()
        rowA = c.sb("rowA", [1, S]); brA = Buf()
        rowB = c.sb("rowB", [1, S]); brB = Buf()
        onesr = c.sb("onesr", [1, 512]); bon = Buf()
        cst = c.sb("cst", [1, 4]); bcst = Buf()
        ccol = c.sb("ccol", [128, NKB]); bcc = Buf()
        tri = c.sb("tri", [128, 128]); btri = Buf()
        PT = [c.sb("PT%d" % i, [128, 512]) for i in range(3)]; bPT = [Buf() for _ in range(3)]
        fg = [c.sb("fg%d" % i, [128, 128]) for i in range(2)]; bfg = [Buf(), Buf()]
        ot = [c.sb("ot%d" % i, [128, 128]) for i in range(2)]; bot = [Buf(), Buf()]
        rl = c.sb("rl", [128, 8]); brl = Buf()
        pST = [c.ps("pST0"), c.ps("pST1")]; bpST = [Buf(), Buf()]
        pO = [c.ps("pO%d" % i) for i in range(4)]; bpO = [Buf() for _ in range(4)]
        pC = c.ps("pC"); bpC = Buf()
        bout = Buf()

        P.dma(qT[:, :], zT[384:512, :], reads=[bz], writes=[bq])
        P.dma(kT[:, :], zT[512:640, :], reads=[bz], writes=[bk])
        P.dma(va[:, :, 0:128], z[:, 256:384].rearrange("(n p) v -> p n v", p=128), reads=[bz], writes=[bva])
        P.pool(lambda e: e.memset(va[:, :, 128:129], 1.0), writes=[bva])
        P.dma(rowA[:, :], zT[640:641, :], reads=[bz], writes=[brA])
        P.dma(cst[:, 0:1], bf, writes=[bcst])
        P.dve(lambda e: e.tensor_scalar(out=cst[:, 1:2], in0=cst[:, 0:1], scalar1=-1.0, scalar2=None, op0=ALU.mult),
              reads=[bcst], writes=[bcst])
        P.pool(lambda e: e.memset(onesr[:, :], 1.0), writes=[bon])
        P.pool(lambda e: e.memset(tri[:, :], 1.0), writes=[btri])
        P.pool(lambda e: e.affine_select(out=tri[:, :], in_=tri[:, :], pattern=[[1, 128]],
                                         compare_op=ALU.is_ge, fill=0.0, base=0, channel_multiplier=-1),
               reads=[btri], writes=[btri])
        P.act(lambda e: e.activation(out=rowA[:, :], in_=rowA[:, :], func=AF.Exp, scale=-1.0, bias=cst[:, 1:2]),
              reads=[brA, bcst], writes=[brA])
        P.act(lambda e: e.activation(out=rowA[:, :], in_=rowA[:, :], func=AF.Ln, bias=1.0),
              reads=[brA], writes=[brA])
        for h2 in range(S // 512):
            sl = slice(h2 * 512, (h2 + 1) * 512)
            init = 0.0 if h2 == 0 else rowB[:, h2 * 512 - 1:h2 * 512]
            P.dve(lambda e, sl=sl, init=init: e.tensor_tensor_scan(out=rowB[:, sl], data0=onesr[:, :],
                                                                   data1=rowA[:, sl], initial=init,
                                                                   op0=ALU.mult, op1=ALU.add),
                  reads=[brA, brB, bon], writes=[brB])
        for kb in range(NKB):
            P.pe(lambda e, kb=kb: e.matmul(pC[:, kb:kb + 1], lhsT=rowB[0:1, kb * 128:(kb + 1) * 128],
                                           rhs=onesr[0:1, 0:1], start=True, stop=True),
                 reads=[brB, bon], writes=[bpC])
        P.dve(lambda e: e.tensor_copy(out=ccol[:, :], in_=pC[:, 0:NKB]), reads=[bpC], writes=[bcc])
        P.dve(lambda e: e.tensor_scalar(out=rowB[:, :], in0=rowB[:, :], scalar1=-1.0 / SCALE, scalar2=None,
                                        op0=ALU.mult), reads=[brB], writes=[brB])
        it = 0
        for Q in range(NQG):
            oset = (Q % 2) * 2
            for ob_ in (oset, oset + 1):
                P.dve(lambda e, ob_=ob_: e.memset(pO[ob_][:, 0:258], 0.0), writes=[bpO[ob_]])
            for kb in range(4 * Q + 4):
                jlo = max(0, kb - 4 * Q)
                q0 = Q * 512 + jlo * 128
                n = 512 - jlo * 128
                sb_ = it % 2
                pb = it % 3
                it += 1
                P.pe(lambda e, kb=kb, q0=q0, n=n, sb_=sb_: e.matmul(pST[sb_][:, 0:n],
                                                                  lhsT=kT[:, kb * 128:(kb + 1) * 128],
                                                                  rhs=qT[:, q0:q0 + n], start=True, stop=False),
                     reads=[bk, bq], writes=[bpST[sb_]])
                P.pe(lambda e, q0=q0, n=n, sb_=sb_: e.matmul(pST[sb_][:, 0:n], lhsT=onesr[0:1, 0:128],
                                                           rhs=rowB[0:1, q0:q0 + n], start=False, stop=True),
                     reads=[bon, brB], writes=[bpST[sb_]])
                P.act(lambda e, kb=kb, n=n, sb_=sb_, pb=pb: e.activation(out=PT[pb][:, 0:n], in_=pST[sb_][:, 0:n],
                                                                       func=AF.Exp, scale=SCALE,
                                                                       bias=ccol[:, kb:kb + 1]),
                      reads=[bpST[sb_], bcc], writes=[bPT[pb]])
                if kb >= 4 * Q:
                    P.pool(lambda e, pb=pb: e.tensor_tensor(out=PT[pb][:, 0:128], in0=PT[pb][:, 0:128],
                                                            in1=tri[:, :], op=ALU.mult),
                           reads=[bPT[pb], btri], writes=[bPT[pb]])
                for jq in range(jlo, 4):
                    qb = 4 * Q + jq
                    off = (jq - jlo) * 128
                    ob = oset + jq // 2
                    oc = (jq % 2) * 129
                    P.pe(lambda e, kb=kb, pb=pb, off=off, ob=ob, oc=oc, qb=qb: e.matmul(
                        pO[ob][:, oc:oc + 129], lhsT=PT[pb][:, off:off + 128], rhs=va[:, kb, :],
                        start=False, stop=True, skip_group_check=True),
                        reads=[bPT[pb], bva], writes=[bpO[ob]])
            for jq in range(4):
                qb = 4 * Q + jq
                ob = oset + jq // 2
                oc = (jq % 2) * 129
                fb = qb % 2
                P.dma(fg[fb][:, :], z[qb * 128:(qb + 1) * 128, 384:512], reads=[bz], writes=[bfg[fb]])
                P.dve(lambda e, ob=ob, oc=oc, jq=jq: e.reciprocal(out=rl[:, jq:jq + 1],
                                                                  in_=pO[ob][:, oc + 128:oc + 129]),
                      reads=[bpO[ob]], writes=[brl])
                P.dve(lambda e, ob=ob, oc=oc, jq=jq, fb=fb: e.tensor_scalar(out=ot[fb][:, :],
                                                                           in0=pO[ob][:, oc:oc + 128],
                                                                           scalar1=rl[:, jq:jq + 1], scalar2=None,
                                                                           op0=ALU.mult),
                      reads=[bpO[ob], brl], writes=[bot[fb]])
                P.act(lambda e, fb=fb: e.activation(out=fg[fb][:, :], in_=fg[fb][:, :], func=AF.Silu),
                      reads=[bfg[fb]], writes=[bfg[fb]])
                P.pool(lambda e, fb=fb: e.tensor_tensor(out=ot[fb][:, :], in0=ot[fb][:, :], in1=fg[fb][:, :],
                                                        op=ALU.mult),
                       reads=[bot[fb], bfg[fb]], writes=[bot[fb]])
                P.dma(o_c[qb * 128:(qb + 1) * 128, :], ot[fb][:, :], reads=[bot[fb]], writes=[bout])
    P.barrier()


T2 = 1024
HW = 512
NW2 = 7168


def build_k2(P, nc, x, gcol, w2, bgc, yT, o_a, o_c, wglu, wa, wb, wc, wout, fg, xo, xf):
    KC = D // 128
    with contextlib.ExitStack() as st:
        c = Ctx(P, nc, st)
        ident, bid = make_ident(P, c)
        gc = c.sb("gc", [128, KC]); bgc_ = Buf()
        P.dma(gc[:, :], gcol, writes=[bgc_])
        bg = c.sb("bg", [128, 48]); bbg = Buf()
        P.dma(bg[:, :], bgc, writes=[bbg])
        fgb = c.sb("fgb", [128, D]); bfgb = Buf()
        P.dma(fgb[:, :], fg.partition_broadcast(128), writes=[bfgb])
        xs = c.sb("xs", [128, 4, D]); bxs = [Buf() for _ in range(4)]
        hb = c.sb("hb", [128, D]); bhb = Buf()
        stg = [c.sb("stg%d" % i, [128, 1024]) for i in range(2)]; bstg = [Buf(), Buf()]
        hT = c.sb("hT", [128, KC, HW]); bhT = Buf()
        oT = c.sb("oT", [128, 8, HW]); boT = Buf()
        ymT = c.sb("ymT", [128, KC, HW]); bym = [Buf() for _ in range(KC)]
        stat = c.sb("stat", [128, 8]); bstat = Buf()
        w2s = [c.sb("w2s%d" % i, [128, KC, 128]) for i in range(2)]; bw2s = [Buf(), Buf()]
        wbs = [c.sb("wbs%d" % i, [128, 8, 128]) for i in range(2)]; bwbs = [Buf(), Buf()]
        wgs = [c.sb("wgs%d" % i, [128, 2, 8, 128]) for i in range(2)]; bwgs = [Buf(), Buf()]
        wos = [c.sb("wos%d" % i, [128, KC, 128]) for i in range(2)]; bwos = [Buf(), Buf()]
        e1 = [c.sb("e1%d" % i, [128, HW]) for i in range(2)]; be1 = [Buf(), Buf()]
        e2 = [c.sb("e2%d" % i, [128, HW]) for i in range(2)]; be2 = [Buf(), Buf()]
        e3 = [c.sb("e3%d" % i, [128, HW]) for i in range(2)]; be3 = [Buf(), Buf()]
        banks = [c.ps("bk%d" % i) for i in range(8)]; bbk = [Buf() for _ in range(8)]
        bki = [0]
        bout = Buf()

        def nb():
            i = bki[0] % 8
            bki[0] += 1
            return banks[i], bbk[i]

        cnt = dict(w2=0, wb=0, wg=0, wo=0, e=0)
        w2v = w2.rearrange("(c p) n -> p c n", p=128)
        wgv = wglu.rearrange("(c p) n -> p c n", p=128)
        wov = wout.rearrange("(c p) n -> p c n", p=128)

        def evac_T(ps_t, bps, dst, bdst, use_act):
            if use_act:
                P.act(lambda e: e.copy(out=dst, in_=ps_t[:, :].rearrange("p (a b) -> p a b", a=4)),
                      reads=[bps], writes=[bdst])
            else:
                P.dve(lambda e: e.tensor_copy(out=dst, in_=ps_t[:, :].rearrange("p (a b) -> p a b", a=4)),
                      reads=[bps], writes=[bdst])

        for hf in range(T2 // HW):
            tok0 = hf * HW
            for tt in range(4):
                t0 = tok0 + tt * 128
                P.dma(xs[:, tt, :], x[t0:t0 + 128, :], writes=[bxs[tt]])
                P.act(lambda e, tt=tt: e.activation(out=hb[:, :], in_=xs[:, tt, :], func=AF.Square,
                                                    accum_out=stat[:, 0:1]),
                      reads=[bxs[tt]], writes=[bhb, bstat])
                P.dve(lambda e: e.tensor_scalar(out=stat[:, 1:2], in0=stat[:, 0:1], scalar1=1.0 / D, scalar2=EPS,
                                                op0=ALU.mult, op1=ALU.add), reads=[bstat], writes=[bstat])
                P.act(lambda e: e.activation(out=stat[:, 2:3], in_=stat[:, 1:2], func=AF.Sqrt),
                      reads=[bstat], writes=[bstat])
                P.dve(lambda e: e.reciprocal(out=stat[:, 3:4], in_=stat[:, 2:3]), reads=[bstat], writes=[bstat])
                P.dve(lambda e, tt=tt: e.tensor_scalar(out=hb[:, :], in0=xs[:, tt, :], scalar1=stat[:, 3:4],
                                                       scalar2=None, op0=ALU.mult),
                      reads=[bxs[tt], bstat], writes=[bhb])
                for q in range(KC // 4):
                    pt, bpt = nb()
                    for jj in range(4):
                        j = q * 4 + jj
                        P.pe(lambda e, j=j, jj=jj, pt=pt: e.transpose(out=pt[:, jj * 128:(jj + 1) * 128],
                                                                      in_=hb[:, j * 128:(j + 1) * 128],
                                                                      identity=ident[:, :]),
                             reads=[bhb, bid], writes=[bpt])
                    for jj in range(4):
                        j = q * 4 + jj
                        if jj % 2 == 0:
                            P.act(lambda e, j=j, jj=jj, tt=tt, pt=pt: e.activation(
                                out=hT[:, j, tt * 128:(tt + 1) * 128], in_=pt[:, jj * 128:(jj + 1) * 128],
                                func=AF.Identity, scale=gc[:, j:j + 1]),
                                reads=[bpt, bgc_], writes=[bhT])
                        else:
                            P.dve(lambda e, j=j, jj=jj, tt=tt, pt=pt: e.tensor_scalar(
                                out=hT[:, j, tt * 128:(tt + 1) * 128], in0=pt[:, jj * 128:(jj + 1) * 128],
                                scalar1=gc[:, j:j + 1], scalar2=None, op0=ALU.mult),
                                reads=[bpt, bgc_], writes=[bhT])
            for bi_, br in enumerate(("b", "a", "c")):
                if br == "b":
                    P.dma(ymT[:, 0:8, :], yT[:, tok0:tok0 + HW].rearrange("(c p) t -> p c t", p=128),
                          writes=bym[0:8])
                    for eb in range(8):
                        sg = cnt["wg"] % 2; cnt["wg"] += 1
                        s2 = cnt["w2"] % 2; cnt["w2"] += 1
                        P.dma(wgs[sg][:, 0, :, :], wgv[:, :, eb * 128:(eb + 1) * 128], writes=[bwgs[sg]])
                        P.dma(wgs[sg][:, 1, :, :], wgv[:, :, 1024 + eb * 128:1024 + (eb + 1) * 128],
                              writes=[bwgs[sg]])
                        P.dma(w2s[s2][:, :, :], w2v[:, :, eb * 128:(eb + 1) * 128], writes=[bw2s[s2]])
                        pa, bpa = nb()
                        pb_, bpb = nb()
                        pc_, bpc = nb()
                        for kc in range(8):
                            P.pe(lambda e, kc=kc, sg=sg, pa=pa: e.matmul(pa[:, :], lhsT=wgs[sg][:, 0, kc, :],
                                                                        rhs=ymT[:, kc, :], start=(kc == 0),
                                                                        stop=(kc == 7)),
                                 reads=[bwgs[sg]] + bym[0:8], writes=[bpa])
                        for kc in range(8):
                            P.pe(lambda e, kc=kc, sg=sg, pb_=pb_: e.matmul(pb_[:, :], lhsT=wgs[sg][:, 1, kc, :],
                                                                          rhs=ymT[:, kc, :], start=(kc == 0),
                                                                          stop=(kc == 7)),
                                 reads=[bwgs[sg]] + bym[0:8], writes=[bpb])
                        for kc in range(KC):
                            P.pe(lambda e, kc=kc, s2=s2, pc_=pc_: e.matmul(pc_[:, :], lhsT=w2s[s2][:, kc, :],
                                                                          rhs=hT[:, kc, :], start=(kc == 0),
                                                                          stop=(kc == KC - 1)),
                                 reads=[bw2s[s2], bhT], writes=[bpc])
                        ei = cnt["e"] % 2; cnt["e"] += 1
                        P.act(lambda e, ei=ei, pb_=pb_: e.activation(out=e1[ei][:, :], in_=pb_[:, :],
                                                                    func=AF.Sigmoid),
                              reads=[bpb], writes=[be1[ei]])
                        P.act(lambda e, ei=ei, pc_=pc_: e.activation(out=e2[ei][:, :], in_=pc_[:, :], func=AF.Silu),
                              reads=[bpc], writes=[be2[ei]])
                        P.dve(lambda e, ei=ei, pa=pa: e.tensor_tensor(out=e3[ei][:, :], in0=pa[:, :],
                                                                     in1=e1[ei][:, :], op=ALU.mult),
                              reads=[bpa, be1[ei]], writes=[be3[ei]])
                        P.pool(lambda e, ei=ei, eb=eb: e.tensor_tensor(out=oT[:, eb, :], in0=e3[ei][:, :],
                                                                      in1=e2[ei][:, :], op=ALU.mult),
                               reads=[be3[ei], be2[ei]], writes=[boT])
                else:
                    src = o_a if br == "a" else o_c
                    for tt in range(4):
                        t0 = tok0 + tt * 128
                        sb_ = tt % 2
                        P.dma(stg[sb_][:, :], src[t0:t0 + 128, :], writes=[bstg[sb_]])
                        for q in range(2):
                            pt, bpt = nb()
                            for jj in range(4):
                                j = q * 4 + jj
                                P.pe(lambda e, j=j, jj=jj, pt=pt, sb_=sb_: e.transpose(
                                    out=pt[:, jj * 128:(jj + 1) * 128], in_=stg[sb_][:, j * 128:(j + 1) * 128],
                                    identity=ident[:, :]), reads=[bstg[sb_], bid], writes=[bpt])
                            evac_T(pt, bpt, oT[:, q * 4:(q + 1) * 4, tt * 128:(tt + 1) * 128], boT, q == 0)
                wbr = dict(a=wa, b=wb, c=wc)[br]
                wbv = wbr.rearrange("(c p) n -> p c n", p=128)
                gofs = dict(a=0, b=1, c=2)[br]
                for db in range(KC):
                    s2 = cnt["w2"] % 2; cnt["w2"] += 1
                    sb2 = cnt["wb"] % 2; cnt["wb"] += 1
                    col0 = 1024 + gofs * 2048 + db * 128
                    P.dma(w2s[s2][:, :, :], w2v[:, :, col0:col0 + 128], writes=[bw2s[s2]])
                    P.dma(wbs[sb2][:, :, :], wbv[:, :, db * 128:(db + 1) * 128], writes=[bwbs[sb2]])
                    pg, bpg = nb()
                    pm, bpm = nb()
                    for kc in range(KC):
                        P.pe(lambda e, kc=kc, s2=s2, pg=pg: e.matmul(pg[:, :], lhsT=w2s[s2][:, kc, :],
                                                                    rhs=hT[:, kc, :], start=(kc == 0),
                                                                    stop=(kc == KC - 1)),
                             reads=[bw2s[s2], bhT], writes=[bpg])
                    for kc in range(8):
                        P.pe(lambda e, kc=kc, sb2=sb2, pm=pm: e.matmul(pm[:, :], lhsT=wbs[sb2][:, kc, :],
                                                                      rhs=oT[:, kc, :], start=(kc == 0),
                                                                      stop=(kc == 7)),
                             reads=[bwbs[sb2], boT], writes=[bpm])
                    ei = cnt["e"] % 2; cnt["e"] += 1
                    bcol = gofs * 16 + db
                    P.act(lambda e, ei=ei, pg=pg, bcol=bcol: e.activation(out=e1[ei][:, :], in_=pg[:, :],
                                                                         func=AF.Sigmoid,
                                                                         bias=bg[:, bcol:bcol + 1]),
                          reads=[bpg, bbg], writes=[be1[ei]])
                    if bi_ == 0:
                        P.dve(lambda e, ei=ei, pm=pm, db=db: e.tensor_tensor(out=ymT[:, db, :], in0=pm[:, :],
                                                                            in1=e1[ei][:, :], op=ALU.mult),
                              reads=[bpm, be1[ei]], writes=[bym[db]])
                    else:
                        P.dve(lambda e, ei=ei, pm=pm: e.tensor_tensor(out=e3[ei][:, :], in0=pm[:, :],
                                                                     in1=e1[ei][:, :], op=ALU.mult),
                              reads=[bpm, be1[ei]], writes=[be3[ei]])
                        P.pool(lambda e, ei=ei, db=db: e.tensor_tensor(out=ymT[:, db, :], in0=ymT[:, db, :],
                                                                      in1=e3[ei][:, :], op=ALU.add),
                               reads=[bym[db], be3[ei]], writes=[bym[db]])
            for ob in range(KC):
                so = cnt["wo"] % 2; cnt["wo"] += 1
                P.dma(wos[so][:, :, :], wov[:, :, ob * 128:(ob + 1) * 128], writes=[bwos[so]])
                po, bpo = nb()
                for kc in range(KC):
                    P.pe(lambda e, kc=kc, so=so, po=po: e.matmul(po[:, :], lhsT=wos[so][:, kc, :], rhs=ymT[:, kc, :],
                                                                start=(kc == 0), stop=(kc == KC - 1)),
                         reads=[bwos[so], bym[kc]], writes=[bpo])
                ei = cnt["e"] % 2; cnt["e"] += 1
                P.act(lambda e, ei=ei, po=po: e.copy(out=e1[ei][:, :], in_=po[:, :]), reads=[bpo], writes=[be1[ei]])
                pt, bpt = nb()
                for tt in range(4):
                    P.pe(lambda e, tt=tt, ei=ei, pt=pt: e.transpose(out=pt[:, tt * 128:(tt + 1) * 128],
                                                                   in_=e1[ei][:, tt * 128:(tt + 1) * 128],
                                                                   identity=ident[:, :]),
                         reads=[be1[ei], bid], writes=[bpt])
                P.dve(lambda e, ob=ob, pt=pt: e.tensor_tensor(out=xs[:, :, ob * 128:(ob + 1) * 128],
                                                             in0=xs[:, :, ob * 128:(ob + 1) * 128],
                                                             in1=pt[:, :].rearrange("p (a b) -> p a b", a=4),
                                                             op=ALU.add),
                      reads=bxs + [bpt], writes=bxs)
            for tt in range(4):
                t0 = tok0 + tt * 128
                P.dma(xo[t0:t0 + 128, :], xs[:, tt, :], reads=[bxs[tt]], writes=[bout])
                P.act(lambda e, tt=tt: e.activation(out=hb[:, :], in_=xs[:, tt, :], func=AF.Square,
                                                    accum_out=stat[:, 4:5]),
                      reads=[bxs[tt]], writes=[bhb, bstat])
                P.dve(lambda e: e.tensor_scalar(out=stat[:, 5:6], in0=stat[:, 4:5], scalar1=1.0 / D, scalar2=EPS,
                                                op0=ALU.mult, op1=ALU.add), reads=[bstat], writes=[bstat])
                P.act(lambda e: e.activation(out=stat[:, 6:7], in_=stat[:, 5:6], func=AF.Sqrt),
                      reads=[bstat], writes=[bstat])
                P.dve(lambda e: e.reciprocal(out=stat[:, 7:8], in_=stat[:, 6:7]), reads=[bstat], writes=[bstat])
                P.dve(lambda e, tt=tt: e.scalar_tensor_tensor(out=hb[:, :], in0=xs[:, tt, :], scalar=stat[:, 7:8],
                                                              in1=fgb[:, :], op0=ALU.mult, op1=ALU.mult),
                      reads=[bxs[tt], bstat, bfgb], writes=[bhb])
                P.dma(xf[t0:t0 + 128, :], hb[:, :], reads=[bhb], writes=[bout])
    P.barrier()


OFFS = [0, 1024, 2048, 3072, 4096, 5120, 6144, 7168, 8192, 9216, 9224, 10248, 16392]
_CACHE = {}


def _head_cols(c):
    hq, hf, hi, hgate, su, sgate, fq, fk, fv, ff, fgate, mg = OFFS[:12]
    r = lambda o: list(range(o + 128 * c, o + 128 * c + 128))
    return np.array(r(hq) + r(hf) + r(su) + r(fq) + r(fk) + [ff + c] + r(hi) + r(hgate) + r(fv) + r(fgate))


def _build_k1():
    nc = bass.Bass("TRN2", target_bir_lowering=False)
    i_ = lambda n, s: nc.dram_tensor(n, s, F32, kind="ExternalInput").ap()
    o_ = lambda n, s: nc.dram_tensor(n, s, F32, kind="ExternalOutput").ap()
    x = i_("x", [S, D]); g = i_("g", [1, D]); w = i_("w", [D, NCOL])
    lb0 = i_("lb0", [128, 1]); lb1 = i_("lb1", [128, 1]); lsel = i_("lsel", [128, 1]); hgn = i_("hgn", [1, 128])
    s5p = i_("s5p", [128, 3, 4]); Bre = i_("Bre", [128, 4, 128]); Bim = i_("Bim", [128, 4, 128])
    Cre = i_("Cre", [128, 4, 128]); Cim = i_("Cim", [128, 4, 128]); dsk = i_("dsk", [128, 1]); bf = i_("bf", [1, 1])
    zT = nc.dram_tensor("zT", [NFM, S], F32).ap(); z = nc.dram_tensor("z", [S, NTM], F32).ap()
    o_a = o_("o_a", [S, 128]); yT = o_("yT", [128, S]); o_c = o_("o_c", [S, 128])
    P = Prog(nc)
    bz = Buf("z")
    phase_a(P, nc, x, g, w, zT, z, bz)
    phase_b(P, nc, zT, z, bz, lb0, lb1, lsel, hgn, o_a)
    phase_c(P, nc, zT, bz, s5p, Bre, Bim, Cre, Cim, dsk, yT)
    phase_d(P, nc, zT, z, bz, bf, o_c)
    P.emit()
    return nc


def _build_k2():
    nc = bass.Bass("TRN2", target_bir_lowering=False)
    i_ = lambda n, s: nc.dram_tensor(n, s, F32, kind="ExternalInput").ap()
    o_ = lambda n, s: nc.dram_tensor(n, s, F32, kind="ExternalOutput").ap()
    x = i_("x", [T2, 2048]); gcol = i_("gcol", [128, 16]); w2 = i_("w2", [2048, 7168]); bgc = i_("bgc", [128, 48])
    yT = i_("yT", [1024, T2]); o_a = i_("o_a", [T2, 1024]); o_c = i_("o_c", [T2, 1024])
    wglu = i_("wglu", [1024, 2048]); wa = i_("wa", [1024, 2048]); wb = i_("wb", [1024, 2048])
    wc = i_("wc", [1024, 2048]); wout = i_("wout", [2048, 2048]); fg = i_("fg", [1, 2048])
    xo = o_("xo", [T2, 2048]); xf = o_("xf", [T2, 2048])
    P = Prog(nc)
    build_k2(P, nc, x, gcol, w2, bgc, yT, o_a, o_c, wglu, wa, wb, wc, wout, fg, xo, xf)
    P.emit()
    return nc


def _k1_inputs(inp, l, c, x):
    sl = slice(128 * c, 128 * c + 128)
    f = np.float32
    m = dict(x=x, g=np.ascontiguousarray(inp["norm_g"][l][None, :]),
             w=np.ascontiguousarray(inp["w_in"][l][:, _head_cols(c)]),
             lb0=np.ascontiguousarray(inp["hg_lb"][0, sl][:, None]),
             lb1=np.ascontiguousarray(inp["hg_lb"][1, sl][:, None]),
             lsel=np.full((128, 1), float(l), f), hgn=np.ascontiguousarray(inp["hg_norm_g"][l, sl][None, :]))
    s5p = np.zeros((128, 3, 4), f)
    Bre = np.zeros((128, 4, 128), f); Bim = np.zeros((128, 4, 128), f)
    Cre = np.zeros((128, 4, 128), f); Cim = np.zeros((128, 4, 128), f)
    for j in range(4):
        for gg in range(2):
            gl = 2 * j + gg
            g = 8 * c + gl
            ps = slice(64 * gg, 64 * gg + 64)
            s5p[ps, 0, j] = inp["s5_a_re"][l, g]
            s5p[ps, 1, j] = inp["s5_a_im"][l, g]
            s5p[ps, 2, j] = inp["s5_log_dt"][l, g]
            cs = slice(16 * gl, 16 * gl + 16)
            Bre[cs, j, ps] = inp["s5_b_re"][l, g].T
            Bim[cs, j, ps] = inp["s5_b_im"][l, g].T
            Cre[ps, j, cs] = inp["s5_c_re"][l, g].T
            Cim[ps, j, cs] = inp["s5_c_im"][l, g].T
    m.update(s5p=s5p, Bre=Bre, Bim=Bim, Cre=Cre, Cim=Cim, dsk=np.ascontiguousarray(inp["s5_d"][l, sl][:, None]),
             bf=np.ascontiguousarray(inp["fox_bf"][l, c].reshape(1, 1)))
    return m


def _k2_inputs(inp, l, tok, x, yT_full, o_a, o_c):
    sg, mg = OFFS[5], OFFS[11]
    w = inp["w_in"][l]
    w2 = np.ascontiguousarray(np.concatenate([w[:, sg:sg + 1024], w[:, mg:mg + 6144]], axis=1))
    return dict(x=np.ascontiguousarray(x[tok]), gcol=np.ascontiguousarray(inp["norm_g"][l].reshape(16, 128).T),
                w2=w2, bgc=np.ascontiguousarray(inp["b_gate"][l].reshape(48, 128).T),
                yT=np.ascontiguousarray(yT_full[:, tok]), o_a=np.ascontiguousarray(o_a[tok]),
                o_c=np.ascontiguousarray(o_c[tok]), wglu=np.ascontiguousarray(inp["s5_w_glu"][l]),
                wa=np.ascontiguousarray(inp["w_br_a"][l]), wb=np.ascontiguousarray(inp["w_br_b"][l]),
                wc=np.ascontiguousarray(inp["w_br_c"][l]), wout=np.ascontiguousarray(inp["w_out"][l]),
                fg=np.ascontiguousarray(inp["final_g"][None, :]))


def kernel(**inputs):
    inp = {k: np.asarray(v, dtype=np.float32) for k, v in inputs.items()}
    x = np.ascontiguousarray(inp["x"][0])
    xf = None
    for l in range(2):
        nc1 = _build_k1()
        res = run_bass_kernel_spmd(nc1, [_k1_inputs(inp, l, c, x) for c in range(8)], core_ids=list(range(8)))
        o_a = np.concatenate([r["o_a"] for r in res.results], axis=1)
        yT = np.concatenate([r["yT"] for r in res.results], axis=0)
        o_c = np.concatenate([r["o_c"] for r in res.results], axis=1)
        nc2 = _build_k2()
        res = run_bass_kernel_spmd(nc2, [_k2_inputs(inp, l, slice(1024 * c, 1024 * c + 1024), x, yT, o_a, o_c)
                                         for c in range(8)], core_ids=list(range(8)))
        x = np.concatenate([r["xo"] for r in res.results], axis=0)
        xf = np.concatenate([r["xf"] for r in res.results], axis=0)
    return xf[None].astype(np.float32)
```

```python
import contextlib, math
from concourse.bass_utils import run_bass_kernel_spmd
import numpy as np
import concourse.bass as bass
import concourse.mybir as mybir

F32 = mybir.dt.float32
BF16 = mybir.dt.bfloat16
ALU = mybir.AluOpType
AF = mybir.ActivationFunctionType
AX = mybir.AxisListType


class Buf:
    __slots__ = ("name", "w", "r")

    def __init__(self, name=""):
        self.name = name
        self.w = None
        self.r = []


class Prog:
    ENGS = ("pe", "dve", "act", "pool", "sp")
    NDMA = 12

    def __init__(self, nc):
        self.nc = nc
        self.ops = []
        self.floor = {}
        self.since = set()

    def add(self, eng, fn, reads=(), writes=(), dma=False):
        idx = len(self.ops)
        deps = set()
        for b in reads:
            if b.w is not None:
                deps.add(b.w)
        for b in writes:
            if b.w is not None:
                deps.add(b.w)
            deps.update(b.r)
        for b in reads:
            b.r.append(idx)
        for b in writes:
            b.w = idx
            b.r = []
        if eng in self.floor:
            deps |= self.floor.pop(eng)
        deps.discard(idx)
        self.ops.append(dict(eng=eng, fn=fn, deps=deps, dma=dma))
        if dma:
            self.since.add(idx)
        return idx

    def barrier(self):
        last = {}
        for i, o in enumerate(self.ops):
            if not o["dma"]:
                last[o["eng"]] = i
        fl = set(last.values()) | self.since
        for e in self.ENGS:
            self.floor[e] = set(fl) | self.floor.get(e, set())
        self.since = set()

    def pe(self, fn, reads=(), writes=()):
        return self.add("pe", fn, reads, writes)

    def dve(self, fn, reads=(), writes=()):
        return self.add("dve", fn, reads, writes)

    def act(self, fn, reads=(), writes=()):
        return self.add("act", fn, reads, writes)

    def pool(self, fn, reads=(), writes=()):
        return self.add("pool", fn, reads, writes)

    def cc(self, fn, reads=(), writes=()):
        i = self.add("pool", fn, reads, writes, dma=True)
        self.ops[i]["cc"] = True
        return i

    def core(self, e):
        if not hasattr(self, "_core"):
            self._core = {}
        k = id(e)
        if k not in self._core:
            self._core[k] = e.partition_id()
        return self._core[k]

    def dma(self, out, in_, reads=(), writes=(), q="sp"):
        return self.add(q, lambda e: e.dma_start(out=out, in_=in_), reads, writes, dma=True)

    def emit(self):
        nc = self.nc
        ops = self.ops
        per = {e: [] for e in self.ENGS}
        seq = {}
        for i, o in enumerate(ops):
            seq[i] = len(per[o["eng"]])
            per[o["eng"]].append(i)
        dmacount = {e: 0 for e in self.ENGS}
        dmainfo = {}
        ncc = 0
        for i, o in enumerate(ops):
            if o.get("cc"):
                dmainfo[i] = ("cc", ncc, 1)
                ncc += 1
            elif o["dma"]:
                j = dmacount[o["eng"]]
                dmacount[o["eng"]] += 1
                dmainfo[i] = (o["eng"], j % self.NDMA, 16 * (j // self.NDMA + 1))
        import contextlib
        waits = {}
        signal = set()
        for ename in self.ENGS:
            waited = {}
            for i in per[ename]:
                o = ops[i]
                need = {}
                for d in o["deps"]:
                    od = ops[d]
                    if od["dma"]:
                        q, k, v = dmainfo[d]
                        sk = ("d", q, k)
                        need[sk] = max(need.get(sk, 0), v)
                    elif od["eng"] == ename:
                        if ename == "pe":
                            continue
                        if seq[i] - seq[d] <= 3:
                            sk = ("e", ename)
                            need[sk] = max(need.get(sk, 0), seq[d] + 1)
                    else:
                        sk = ("e", od["eng"])
                        need[sk] = max(need.get(sk, 0), seq[d] + 1)
                if o["dma"] and not o.get("cc"):
                    q, k, v = dmainfo[i]
                    if v > 16:
                        sk = ("d", q, k)
                        need[sk] = max(need.get(sk, 0), v - 16)
                wl = []
                for sk, v in need.items():
                    if waited.get(sk, 0) >= v:
                        continue
                    waited[sk] = v
                    wl.append((sk, v))
                    if sk[0] == "e":
                        signal.add((sk[1], v - 1))
                waits[i] = wl
        cnt = {}
        for ename in self.ENGS:
            c = 0
            arr = []
            for sq_ in range(len(per[ename])):
                if (ename, sq_) in signal:
                    c += 1
                arr.append(c)
            cnt[ename] = arr
        self.n_signal = len(signal)
        with contextlib.ExitStack() as st:
            esem = {e: st.enter_context(nc.semaphore("s_" + e)) for e in self.ENGS}
            dsem = {}
            for e in self.ENGS:
                if dmacount[e]:
                    dsem[e] = [st.enter_context(nc.semaphore("d_%s_%d" % (e, k)))
                               for k in range(min(self.NDMA, dmacount[e]))]
            dsem["cc"] = [st.enter_context(nc.semaphore("cc_%d" % k)) for k in range(ncc)]
            block = st.enter_context(nc.Block())

            def body(ename):
                def f(eng):
                    for i in per[ename]:
                        o = ops[i]
                        for sk, v in waits[i]:
                            if sk[0] == "e":
                                eng.wait_ge(esem[sk[1]], cnt[sk[1]][v - 1])
                            else:
                                eng.wait_ge(dsem[sk[1]][sk[2]], v)
                        ins = o["fn"](eng)
                        if o.get("cc"):
                            q, k, v = dmainfo[i]
                            ins.then_inc(dsem[q][k], 1)
                        elif o["dma"]:
                            q, k, v = dmainfo[i]
                            ins.then_inc(dsem[q][k], 16)
                        elif (ename, seq[i]) in signal:
                            ins.then_inc(esem[ename], 1)
                    if ename in dsem:
                        n = dmacount[ename]
                        for k in range(len(dsem[ename])):
                            c_ = (n - k + self.NDMA - 1) // self.NDMA
                            if c_ > 0:
                                eng.wait_ge(dsem[ename][k], 16 * c_)
                return f

            if per["sp"]:
                block.sync(body("sp"))
            if per["pe"]:
                block.tensor(body("pe"))
            if per["dve"]:
                block.vector(body("dve"))
            if per["act"]:
                block.scalar(body("act"))
            if per["pool"]:
                block.gpsimd(body("pool"))

D = 2048


S = 8192
NFM = 641
NTM = 512
NCOL = NFM + NTM
EPS = 1e-6


class Ctx:
    cnt = [0]

    def __init__(self, P, nc, st):
        self.P, self.nc, self.st = P, nc, st
        Ctx.cnt[0] += 1
        self.pfx = "c%d_" % Ctx.cnt[0]

    def sb(self, name, shape, dt=None):
        return self.st.enter_context(self.nc.sbuf_tensor(self.pfx + name, shape, dt or F32))

    def ps(self, name):
        return self.st.enter_context(self.nc.psum_tensor(self.pfx + name, [128, 512], F32))


def make_ident(P, c, n=128):
    ident = c.sb("ident", [128, 128]); bid = Buf()
    P.pool(lambda e: e.memset(ident[:, :], 0.0), writes=[bid])
    P.pool(lambda e: e.affine_select(out=ident[:, :], in_=ident[:, :], pattern=[[-1, 128]],
                                     compare_op=ALU.not_equal, fill=1.0, base=0, channel_multiplier=1),
           reads=[bid], writes=[bid])
    return ident, bid


def phase_a(P, nc, x, g, w, zT, z, bz):
    with contextlib.ExitStack() as st:
        c = Ctx(P, nc, st)
        KC = D // 128
        W32 = c.sb("W32", [128, KC, NCOL]); bW32 = Buf("W32")
        W = c.sb("W", [128, KC, NCOL], BF16); bW = Buf("W")
        gb = c.sb("gb", [128, D]); bgb = Buf()
        ident, bid = make_ident(P, c)
        xt = [c.sb("xt%d" % i, [128, D]) for i in range(2)]; bxt = [Buf(), Buf()]
        stat = [c.sb("stat%d" % i, [128, 4]) for i in range(2)]; bstat = [Buf(), Buf()]
        h = [c.sb("h%d" % i, [128, D]) for i in range(2)]; bh = [Buf(), Buf()]
        hT = c.sb("hT", [128, KC, 512], BF16); bhT = [Buf() for _ in range(4)]
        zst = [c.sb("zst%d" % i, [128, 512]) for i in range(3)]; bzst = [Buf() for _ in range(3)]
        zt2 = [c.sb("zt2%d" % i, [128, NTM]) for i in range(2)]; bzt2 = [Buf(), Buf()]
        pT = [c.ps("pT%d" % i) for i in range(2)]; bpT = [Buf(), Buf()]
        pF = [c.ps("pF%d" % i) for i in range(2)]; bpF = [Buf(), Buf()]
        pM = [c.ps("pM%d" % i) for i in range(2)]; bpM = [Buf() for _ in range(2)]

        P.dma(W32[:, :, :], w.rearrange("(c p) n -> p c n", p=128), writes=[bW32])
        for q_ in range(4):
            sl_ = slice(q_ * 4, q_ * 4 + 4)
            if q_ % 2 == 0:
                P.pool(lambda e, sl_=sl_: e.tensor_copy(out=W[:, sl_, :], in_=W32[:, sl_, :]), reads=[bW32], writes=[bW])
            else:
                P.act(lambda e, sl_=sl_: e.copy(out=W[:, sl_, :], in_=W32[:, sl_, :]), reads=[bW32], writes=[bW])
        P.dma(gb[:, :], g.partition_broadcast(128), writes=[bgb])
        nst = S // 512
        ti = 0
        zi = 0
        for s in range(nst):
            for tt in range(4):
                t0 = s * 512 + tt * 128
                b = ti % 2
                P.dma(xt[b][:, :], x[t0:t0 + 128, :], writes=[bxt[b]])
                P.act(lambda e, b=b: e.activation(out=h[b][:, :], in_=xt[b][:, :], func=AF.Square,
                                                  accum_out=stat[b][:, 0:1]),
                      reads=[bxt[b]], writes=[bh[b], bstat[b]])
                P.dve(lambda e, b=b: e.tensor_scalar(out=stat[b][:, 1:2], in0=stat[b][:, 0:1], scalar1=1.0 / D,
                                                     scalar2=EPS, op0=ALU.mult, op1=ALU.add),
                      reads=[bstat[b]], writes=[bstat[b]])
                P.act(lambda e, b=b: e.activation(out=stat[b][:, 2:3], in_=stat[b][:, 1:2], func=AF.Sqrt),
                      reads=[bstat[b]], writes=[bstat[b]])
                P.dve(lambda e, b=b: e.reciprocal(out=stat[b][:, 3:4], in_=stat[b][:, 2:3]),
                      reads=[bstat[b]], writes=[bstat[b]])
                P.dve(lambda e, b=b: e.scalar_tensor_tensor(out=h[b][:, :], in0=xt[b][:, :],
                                                            scalar=stat[b][:, 3:4], in1=gb[:, :],
                                                            op0=ALU.mult, op1=ALU.mult),
                      reads=[bxt[b], bstat[b], bgb], writes=[bh[b]])
                for q in range(KC // 4):
                    pb = (ti * (KC // 4) + q) % 2
                    for jj in range(4):
                        j = q * 4 + jj
                        P.pe(lambda e, b=b, j=j, jj=jj, pb=pb: e.transpose(
                            out=pT[pb][:, jj * 128:(jj + 1) * 128], in_=h[b][:, j * 128:(j + 1) * 128],
                            identity=ident[:, :]),
                            reads=[bh[b], bid], writes=[bpT[pb]])
                    if q % 2 == 0:
                        P.act(lambda e, q=q, tt=tt, pb=pb: e.copy(
                            out=hT[:, q * 4:(q + 1) * 4, tt * 128:(tt + 1) * 128],
                            in_=pT[pb][:, :].rearrange("p (a b) -> p a b", a=4)),
                            reads=[bpT[pb]], writes=[bhT[tt]])
                    else:
                        P.dve(lambda e, q=q, tt=tt, pb=pb: e.tensor_copy(
                            out=hT[:, q * 4:(q + 1) * 4, tt * 128:(tt + 1) * 128],
                            in_=pT[pb][:, :].rearrange("p (a b) -> p a b", a=4)),
                            reads=[bpT[pb]], writes=[bhT[tt]])
                ti += 1
            for cb in range(6):
                pb = cb % 2
                m = 128 if cb < 5 else 1
                for j in range(KC):
                    P.pe(lambda e, cb=cb, j=j, pb=pb, m=m: e.matmul(
                        pF[pb][0:m, :], lhsT=W[:, j, cb * 128:cb * 128 + m], rhs=hT[:, j, :],
                        start=(j == 0), stop=(j == KC - 1)),
                        reads=[bW] + bhT, writes=[bpF[pb]])
                zb = zi % 3
                zi += 1
                if cb % 2 == 0:
                    P.act(lambda e, zb=zb, pb=pb, m=m: e.copy(out=zst[zb][0:m, :], in_=pF[pb][0:m, :]),
                          reads=[bpF[pb]], writes=[bzst[zb]])
                else:
                    P.dve(lambda e, zb=zb, pb=pb, m=m: e.tensor_copy(out=zst[zb][0:m, :], in_=pF[pb][0:m, :]),
                          reads=[bpF[pb]], writes=[bzst[zb]])
                P.dma(zT[cb * 128:cb * 128 + m, s * 512:(s + 1) * 512], zst[zb][0:m, :],
                      reads=[bzst[zb]], writes=[bz])
            for tt in range(4):
                t0 = s * 512 + tt * 128
                pa = tt % 2
                for j in range(KC):
                    P.pe(lambda e, j=j, tt=tt, pa=pa: e.matmul(
                        pM[pa][:, :], lhsT=hT[:, j, tt * 128:(tt + 1) * 128], rhs=W[:, j, NFM:NFM + 512],
                        start=(j == 0), stop=(j == KC - 1)),
                        reads=[bW, bhT[tt]], writes=[bpM[pa]])
                zb = tt % 2
                if tt % 2 == 0:
                    P.act(lambda e, zb=zb, pa=pa: e.copy(out=zt2[zb][:, :], in_=pM[pa][:, :]),
                          reads=[bpM[pa]], writes=[bzt2[zb]])
                else:
                    P.dve(lambda e, zb=zb, pa=pa: e.tensor_copy(out=zt2[zb][:, :], in_=pM[pa][:, :]),
                          reads=[bpM[pa]], writes=[bzt2[zb]])
                P.dma(z[t0:t0 + 128, :], zt2[zb][:, :], reads=[bzt2[zb]], writes=[bz])
    P.barrier()


def bc_mid(ap2, n):
    p, a = ap2.shape
    return ap2.unsqueeze(2).to_broadcast([p, a, n])


def phase_b(P, nc, zT, z, bz, lb0, lb1, lsel, hgn, o_a):
    SEG = 1024
    NCH = SEG // 64
    with contextlib.ExitStack() as st:
        c = Ctx(P, nc, st)
        ident, bid = make_ident(P, c)
        lbt = c.sb("lbt", [128, 8]); blb = Buf()
        P.dma(lbt[:, 0:1], lb0, writes=[blb])
        P.dma(lbt[:, 1:2], lb1, writes=[blb])
        P.dma(lbt[:, 2:3], lsel, writes=[blb])
        P.dve(lambda e: e.tensor_tensor(out=lbt[:, 3:4], in0=lbt[:, 1:2], in1=lbt[:, 0:1], op=ALU.subtract),
              reads=[blb], writes=[blb])
        P.act(lambda e: e.activation(out=lbt[:, 4:5], in_=lbt[:, 3:4], func=AF.Sigmoid), reads=[blb], writes=[blb])
        P.dve(lambda e: e.tensor_tensor(out=lbt[:, 5:6], in0=lbt[:, 4:5], in1=lbt[:, 2:3], op=ALU.mult),
              reads=[blb], writes=[blb])
        P.dve(lambda e: e.tensor_scalar(out=lbt[:, 6:7], in0=lbt[:, 5:6], scalar1=-1.0, scalar2=1.0,
                                        op0=ALU.mult, op1=ALU.add), reads=[blb], writes=[blb])
        gnb = c.sb("gnb", [64, 128]); bgn = Buf()
        P.dma(gnb[:, :], hgn.partition_broadcast(64), writes=[bgn])
        mask01 = c.sb("mask01", [128, SEG]); bmk = Buf()
        P.pool(lambda e: e.memset(mask01[:, :], 1.0), writes=[bmk])
        P.pool(lambda e: e.memset(mask01[:, :].rearrange("p (n c) -> p n c", c=64)[:, :, 0:1], 0.0),
               reads=[bmk], writes=[bmk])
        tri = c.sb("tri", [64, 64]); btri = Buf()
        P.pool(lambda e: e.memset(tri[:, :], 1.0), writes=[btri])
        P.pool(lambda e: e.affine_select(out=tri[:, :], in_=tri[:, :], pattern=[[1, 64]],
                                         compare_op=ALU.is_ge, fill=0.0, base=0, channel_multiplier=-1),
               reads=[btri], writes=[btri])
        names = ["q", "f", "lf", "cum", "kk", "A", "E2", "qt", "kd"]
        T = {n: c.sb("t_" + n, [128, SEG]) for n in names}
        B = {n: Buf(n) for n in names}
        sm = c.sb("sm", [128, 4, NCH]); bsm = Buf()
        i_tm = c.sb("i_tm", [64, NCH, 128]); bi = Buf()
        g_tm = c.sb("g_tm", [64, NCH, 128]); bg = Buf()
        kd_tm = c.sb("kd_tm", [64, NCH, 128]); bkdt = [Buf() for _ in range(NCH // 4)]
        sT = c.sb("sT", [64, NCH, 64]); bsT = [Buf() for _ in range(NCH // 8)]
        Sall = c.sb("Sall", [128, NCH + 1, 128]); bS = [Buf() for _ in range(NCH + 1)]
        oseg = c.sb("oseg", [64, NCH, 128]); bo = [Buf() for _ in range(NCH // 4)]
        sq = c.sb("sq", [64, NCH, 128]); bsq = Buf()
        st2 = c.sb("st2", [64, 4, NCH]); bst2 = Buf()
        pK = c.ps("pK"); bpK = Buf()
        pS = c.ps("pS"); bpS = Buf()
        pU = [c.ps("pU0"), c.ps("pU1")]; bpU = [Buf(), Buf()]
        pO = [c.ps("pO0"), c.ps("pO1")]; bpO = [Buf(), Buf()]
        bout = Buf()

        P.dve(lambda e: e.memset(Sall[:, 0, :], 0.0), writes=[bS[0]])
        v3 = lambda t: t[:, :].rearrange("p (n c) -> p n c", c=64)
        for seg in range(S // SEG):
            r0 = seg * SEG
            P.dma(T["q"][:, :], zT[0:128, r0:r0 + SEG], reads=[bz], writes=[B["q"]])
            P.dma(T["f"][:, :], zT[128:256, r0:r0 + SEG], reads=[bz], writes=[B["f"]])
            P.dma(i_tm[:, :, :], z[r0:r0 + SEG, 0:128].rearrange("(n p) v -> p n v", p=64), reads=[bz], writes=[bi])
            P.dma(g_tm[:, :, :], z[r0:r0 + SEG, 128:256].rearrange("(n p) v -> p n v", p=64), reads=[bz],
                  writes=[bg])
            if seg > 0:
                P.dve(lambda e: e.tensor_copy(out=Sall[:, 0, :], in_=Sall[:, NCH, :]),
                      reads=[bS[NCH]], writes=[bS[0]])
            P.act(lambda e: e.activation(out=T["f"][:, :], in_=T["f"][:, :], func=AF.Sigmoid),
                  reads=[B["f"]], writes=[B["f"]])
            P.dve(lambda e: e.tensor_scalar(out=T["f"][:, :], in0=T["f"][:, :], scalar1=lbt[:, 6:7],
                                            scalar2=lbt[:, 5:6], op0=ALU.mult, op1=ALU.add),
                  reads=[B["f"], blb], writes=[B["f"]])
            P.act(lambda e: e.activation(out=T["lf"][:, :], in_=T["f"][:, :], func=AF.Ln),
                  reads=[B["f"]], writes=[B["lf"]])
            P.pool(lambda e: e.tensor_scalar(out=T["kk"][:, :], in0=T["f"][:, :], scalar1=-1.0, scalar2=1.0,
                                             op0=ALU.mult, op1=ALU.add),
                   reads=[B["f"]], writes=[B["kk"]])
            P.dve(lambda e: e.tensor_tensor_scan(out=T["cum"][:, :], data0=mask01[:, :], data1=T["lf"][:, :],
                                                 initial=0.0, op0=ALU.mult, op1=ALU.add),
                  reads=[B["lf"], bmk], writes=[B["cum"]])
            cum3 = v3(T["cum"])
            last = cum3[:, :, 63]
            mid = cum3[:, :, 31]
            P.act(lambda e: e.activation(out=sm[:, 0, :], in_=mid, func=AF.Exp, scale=-1.0),
                  reads=[B["cum"]], writes=[bsm])
            P.dve(lambda e: e.tensor_tensor(out=sm[:, 3, :], in0=last, in1=mid, op=ALU.subtract),
                  reads=[B["cum"]], writes=[bsm])
            P.act(lambda e: e.activation(out=sm[:, 1, :], in_=sm[:, 3, :], func=AF.Exp), reads=[bsm], writes=[bsm])
            P.act(lambda e: e.activation(out=sm[:, 2, :], in_=last, func=AF.Exp), reads=[B["cum"]], writes=[bsm])
            P.act(lambda e: e.activation(out=T["A"][:, :], in_=T["cum"][:, :], func=AF.Exp),
                  reads=[B["cum"]], writes=[B["A"]])
            P.dve(lambda e: e.tensor_tensor(out=T["A"][:, :], in0=T["A"][:, :], in1=T["q"][:, :], op=ALU.mult),
                  reads=[B["A"], B["q"]], writes=[B["A"]])
            P.dve(lambda e: e.tensor_tensor(out=v3(T["E2"]), in0=bc_mid(mid, 64), in1=cum3, op=ALU.subtract),
                  reads=[B["cum"]], writes=[B["E2"]])
            P.act(lambda e: e.activation(out=T["E2"][:, :], in_=T["E2"][:, :], func=AF.Exp),
                  reads=[B["E2"]], writes=[B["E2"]])
            P.pool(lambda e: e.tensor_tensor(out=T["E2"][:, :], in0=T["E2"][:, :], in1=T["kk"][:, :], op=ALU.mult),
                   reads=[B["E2"], B["kk"]], writes=[B["E2"]])
            P.dve(lambda e: e.tensor_tensor(out=v3(T["qt"]), in0=v3(T["A"]), in1=bc_mid(sm[:, 0, :], 64),
                                            op=ALU.mult),
                  reads=[B["A"], bsm], writes=[B["qt"]])
            P.pool(lambda e: e.tensor_tensor(out=v3(T["kd"]), in0=v3(T["E2"]), in1=bc_mid(sm[:, 1, :], 64),
                                             op=ALU.mult),
                   reads=[B["E2"], bsm], writes=[B["kd"]])
            for q4 in range(NCH // 4):
                for jj in range(4):
                    n = q4 * 4 + jj
                    P.pe(lambda e, n=n, jj=jj: e.transpose(out=pK[0:64, jj * 128:(jj + 1) * 128],
                                                           in_=T["kd"][:, n * 64:(n + 1) * 64],
                                                           identity=ident[:, :]),
                         reads=[B["kd"], bid], writes=[bpK])
                P.act(lambda e, q4=q4: e.copy(out=kd_tm[:, q4 * 4:(q4 + 1) * 4, :],
                                              in_=pK[0:64, :].rearrange("p (a b) -> p a b", a=4)),
                      reads=[bpK], writes=[bkdt[q4]])
            for q8 in range(NCH // 8):
                for jj in range(8):
                    n = q8 * 8 + jj
                    P.pe(lambda e, n=n, jj=jj: e.matmul(pS[0:64, jj * 64:(jj + 1) * 64],
                                                        lhsT=T["E2"][:, n * 64:(n + 1) * 64],
                                                        rhs=T["qt"][:, n * 64:(n + 1) * 64], start=True, stop=True),
                         reads=[B["E2"], B["qt"]], writes=[bpS])
                P.dve(lambda e, q8=q8: e.tensor_tensor(
                    out=sT[:, q8 * 8:(q8 + 1) * 8, :],
                    in0=pS[0:64, :].rearrange("p (a b) -> p a b", a=8),
                    in1=tri[:, :].unsqueeze(1).to_broadcast([64, 8, 64]), op=ALU.mult),
                    reads=[bpS, btri], writes=[bsT[q8]])
            for q4 in range(NCH // 4):
                ub = q4 % 2
                for jj in range(4):
                    n = q4 * 4 + jj
                    P.pe(lambda e, n=n, jj=jj, ub=ub: e.matmul(pU[ub][:, jj * 128:(jj + 1) * 128],
                                                               lhsT=kd_tm[:, n, :], rhs=i_tm[:, n, :],
                                                               start=True, stop=True),
                         reads=[bkdt[q4], bi], writes=[bpU[ub]])
                for jj in range(4):
                    n = q4 * 4 + jj
                    P.dve(lambda e, n=n, jj=jj, ub=ub: e.scalar_tensor_tensor(
                        out=Sall[:, n + 1, :], in0=Sall[:, n, :], scalar=sm[:, 2, n:n + 1],
                        in1=pU[ub][:, jj * 128:(jj + 1) * 128], op0=ALU.mult, op1=ALU.add),
                        reads=[bS[n], bsm, bpU[ub]], writes=[bS[n + 1]])
            for q4 in range(NCH // 4):
                ob = q4 % 2
                for jj in range(4):
                    n = q4 * 4 + jj
                    P.pe(lambda e, n=n, jj=jj, ob=ob: e.matmul(pO[ob][0:64, jj * 128:(jj + 1) * 128],
                                                               lhsT=T["A"][:, n * 64:(n + 1) * 64],
                                                               rhs=Sall[:, n, :], start=True, stop=False),
                         reads=[B["A"], bS[n]], writes=[bpO[ob]])
                    P.pe(lambda e, n=n, jj=jj, ob=ob: e.matmul(pO[ob][0:64, jj * 128:(jj + 1) * 128],
                                                               lhsT=sT[:, n, :], rhs=i_tm[:, n, :],
                                                               start=False, stop=True),
                         reads=[bsT[n // 8], bi], writes=[bpO[ob]])
                P.act(lambda e, q4=q4, ob=ob: e.copy(out=oseg[:, q4 * 4:(q4 + 1) * 4, :],
                                                     in_=pO[ob][0:64, :].rearrange("p (a b) -> p a b", a=4)),
                      reads=[bpO[ob]], writes=[bo[q4]])
            P.pool(lambda e: e.tensor_tensor(out=sq[:, :, :], in0=oseg[:, :, :], in1=oseg[:, :, :], op=ALU.mult),
                   reads=bo, writes=[bsq])
            P.dve(lambda e: e.tensor_reduce(out=st2[:, 0, :], in_=sq[:, :, :], axis=AX.X, op=ALU.add),
                  reads=[bsq], writes=[bst2])
            P.dve(lambda e: e.tensor_scalar(out=st2[:, 1, :], in0=st2[:, 0, :], scalar1=1.0 / 128, scalar2=EPS,
                                            op0=ALU.mult, op1=ALU.add), reads=[bst2], writes=[bst2])
            P.act(lambda e: e.activation(out=st2[:, 2, :], in_=st2[:, 1, :], func=AF.Sqrt), reads=[bst2],
                  writes=[bst2])
            P.dve(lambda e: e.reciprocal(out=st2[:, 3, :], in_=st2[:, 2, :]), reads=[bst2], writes=[bst2])
            P.dve(lambda e: e.tensor_tensor(out=oseg[:, :, :], in0=oseg[:, :, :], in1=bc_mid(st2[:, 3, :], 128),
                                            op=ALU.mult), reads=bo + [bst2], writes=bo)
            P.pool(lambda e: e.tensor_tensor(out=oseg[:, :, :], in0=oseg[:, :, :],
                                             in1=gnb[:, :].unsqueeze(1).to_broadcast([64, NCH, 128]), op=ALU.mult),
                   reads=bo + [bgn], writes=bo)
            P.act(lambda e: e.activation(out=g_tm[:, :, :], in_=g_tm[:, :, :], func=AF.Silu), reads=[bg],
                  writes=[bg])
            P.dve(lambda e: e.tensor_tensor(out=oseg[:, :, :], in0=oseg[:, :, :], in1=g_tm[:, :, :], op=ALU.mult),
                  reads=bo + [bg], writes=bo)
            P.dma(o_a[r0:r0 + SEG, :].rearrange("(n p) v -> p n v", p=64), oseg[:, :, :], reads=bo, writes=[bout])
    P.barrier()


def phase_c(P, nc, zT, bz, s5p, Bre, Bim, Cre, Cim, dsk, yT):
    L = 512
    NB = S // L
    PI = math.pi
    with contextlib.ExitStack() as st:
        c = Ctx(P, nc, st)
        par = c.sb("par", [128, 3, 4]); bpar = Buf()
        P.dma(par[:, :, :], s5p, writes=[bpar])
        wB = c.sb("wB", [128, 2, 4, 128]); bwB = Buf()
        P.dma(wB[:, 0, :, :], Bre, writes=[bwB])
        P.dma(wB[:, 1, :, :], Bim, writes=[bwB])
        wC = c.sb("wC", [128, 2, 4, 128]); bwC = Buf()
        P.dma(wC[:, 0, :, :], Cre, writes=[bwC])
        P.dma(wC[:, 1, :, :], Cim, writes=[bwC])
        P.dve(lambda e: e.tensor_scalar(out=wC[:, 1, :, :], in0=wC[:, 1, :, :], scalar1=-1.0, scalar2=None,
                                        op0=ALU.mult), reads=[bwC], writes=[bwC])
        dk = c.sb("dk", [128, 1]); bdk = Buf()
        P.dma(dk[:, :], dsk, writes=[bdk])
        NV = 40
        sc = c.sb("sc", [128, NV, 4]); bsc = Buf()
        names = {}

        vb = {}

        def V(n):
            if n not in names:
                names[n] = len(names)
                vb[n] = Buf(n)
                assert names[n] < NV
            return sc[:, names[n], :]

        def VB(*ns):
            for n in ns:
                V(n)
            return [vb[n] for n in ns]

        def tt(o, a, b, op):
            P.dve(lambda e: e.tensor_tensor(out=V(o), in0=V(a), in1=V(b), op=op), reads=VB(a, b), writes=VB(o))

        def ts(o, a, s1, op0, s2=None, op1=None):
            if op1 is None:
                P.dve(lambda e: e.tensor_scalar(out=V(o), in0=V(a), scalar1=s1, scalar2=None, op0=op0),
                      reads=VB(a), writes=VB(o))
            else:
                P.dve(lambda e: e.tensor_scalar(out=V(o), in0=V(a), scalar1=s1, scalar2=s2, op0=op0, op1=op1),
                      reads=VB(a), writes=VB(o))

        def stt(o, a, s, b, op0, op1):
            P.dve(lambda e: e.scalar_tensor_tensor(out=V(o), in0=V(a), scalar=s, in1=V(b), op0=op0, op1=op1),
                  reads=VB(a, b), writes=VB(o))

        def act(o, a, f, scale=1.0):
            P.act(lambda e: e.activation(out=V(o), in_=V(a), func=f, scale=scale), reads=VB(a), writes=VB(o))

        for n_, k_ in (("ar", 0), ("ai", 1), ("ldt", 2)):
            P.dve(lambda e, n_=n_, k_=k_: e.tensor_copy(out=V(n_), in_=par[:, k_, :]), reads=[bpar], writes=VB(n_))
        act("dt", "ldt", AF.Exp)
        tt("m1", "ar", "dt", ALU.mult)
        act("mag", "m1", AF.Exp)
        tt("ang", "ai", "dt", ALU.mult)
        ts("kq", "ang", PI, ALU.is_gt)
        for m_ in range(1, 7):
            stt("kq", "ang", (2 * m_ + 1) * PI, "kq", ALU.is_gt, ALU.add)
        stt("y", "kq", -2.0 * PI, "ang", ALU.mult, ALU.add)
        ts("x8", "y", 0.125, ALU.mult)
        tt("x2", "x8", "x8", ALU.mult)
        ts("p", "x2", -1.0 / 5040, ALU.mult)
        stt("p", "p", 1.0 / 120, "x2", ALU.add, ALU.mult)
        stt("p", "p", -1.0 / 6, "x2", ALU.add, ALU.mult)
        stt("s", "p", 1.0, "x8", ALU.add, ALU.mult)
        ts("q", "x2", 1.0 / 40320, ALU.mult)
        stt("q", "q", -1.0 / 720, "x2", ALU.add, ALU.mult)
        stt("q", "q", 1.0 / 24, "x2", ALU.add, ALU.mult)
        stt("q", "q", -0.5, "x2", ALU.add, ALU.mult)
        ts("c", "q", 1.0, ALU.add)
        for _ in range(3):
            tt("cc", "c", "c", ALU.mult)
            tt("ss", "s", "s", ALU.mult)
            stt("s", "s", 2.0, "c", ALU.mult, ALU.mult)
            tt("c", "cc", "ss", ALU.subtract)
        tt("abr", "mag", "c", ALU.mult)
        tt("abi", "mag", "s", ALU.mult)
        ts("nr", "abr", -1.0, ALU.add)
        tt("d1", "ar", "ar", ALU.mult)
        tt("d2", "ai", "ai", ALU.mult)
        tt("den", "d1", "d2", ALU.add)
        P.dve(lambda e: e.reciprocal(out=V("rden"), in_=V("den")), reads=VB("den"), writes=VB("rden"))
        tt("t1", "nr", "ar", ALU.mult)
        tt("t2", "abi", "ai", ALU.mult)
        tt("t1", "t1", "t2", ALU.add)
        tt("zr", "t1", "rden", ALU.mult)
        tt("t1", "abi", "ar", ALU.mult)
        tt("t2", "nr", "ai", ALU.mult)
        tt("t1", "t1", "t2", ALU.subtract)
        tt("zi", "t1", "rden", ALU.mult)
        pw = c.sb("pw", [128, 2, 10, 4]); bpw = Buf()
        P.dve(lambda e: e.tensor_copy(out=pw[:, 0, 0, :], in_=V("c")), reads=VB('c', 's'), writes=[bpw])
        P.dve(lambda e: e.tensor_copy(out=pw[:, 1, 0, :], in_=V("s")), reads=VB('c', 's'), writes=[bpw])
        for k in range(9):
            P.dve(lambda e, k=k: e.tensor_tensor(out=V("cc"), in0=pw[:, 0, k, :], in1=pw[:, 0, k, :], op=ALU.mult),
                  reads=[bpw], writes=VB('cc', 'ss'))
            P.dve(lambda e, k=k: e.tensor_tensor(out=V("ss"), in0=pw[:, 1, k, :], in1=pw[:, 1, k, :], op=ALU.mult),
                  reads=[bpw], writes=VB('cc', 'ss'))
            P.dve(lambda e, k=k: e.scalar_tensor_tensor(out=pw[:, 1, k + 1, :], in0=pw[:, 1, k, :], scalar=2.0,
                                                        in1=pw[:, 0, k, :], op0=ALU.mult, op1=ALU.mult),
                  reads=[bpw] + VB('cc', 'ss'), writes=[bpw])
            P.dve(lambda e, k=k: e.tensor_tensor(out=pw[:, 0, k + 1, :], in0=V("cc"), in1=V("ss"), op=ALU.subtract),
                  reads=[bpw] + VB('cc', 'ss'), writes=[bpw])
        Ec = c.sb("Ec", [128, 4, L]); Es = c.sb("Es", [128, 4, L])
        Tr = c.sb("Tr", [128, 4, L]); Ti = c.sb("Ti", [128, 4, L]); Rr = c.sb("Rr", [128, 4, L])
        btab = [Buf() for _ in range(4)]
        tmpa = c.sb("tmpa", [128, L]); btmp = Buf()
        for j in range(4):
            bt = btab[j]
            P.pool(lambda e, j=j: e.memset(Ec[:, j, 0:1], 1.0), writes=[bt])
            P.pool(lambda e, j=j: e.memset(Es[:, j, 0:1], 0.0), writes=[bt])
            P.pool(lambda e, j=j: e.memset(Rr[:, j, :], 1.0), writes=[bt])
            P.dve(lambda e, j=j: e.tensor_scalar(out=Rr[:, j, :], in0=Rr[:, j, :], scalar1=V("mag")[:, j:j + 1],
                                                 scalar2=None, op0=ALU.mult), reads=[bt] + VB('mag'), writes=[bt])
            for k in range(9):
                n = 1 << k
                ck = pw[:, 0, k, j:j + 1]
                sk = pw[:, 1, k, j:j + 1]
                P.dve(lambda e, j=j, n=n, sk=sk: e.tensor_scalar(out=tmpa[:, 0:n], in0=Es[:, j, 0:n], scalar1=sk,
                                                                 scalar2=None, op0=ALU.mult),
                      reads=[bt, bpw], writes=[btmp])
                P.dve(lambda e, j=j, n=n, ck=ck: e.scalar_tensor_tensor(out=Ec[:, j, n:2 * n], in0=Ec[:, j, 0:n],
                                                                        scalar=ck, in1=tmpa[:, 0:n],
                                                                        op0=ALU.mult, op1=ALU.subtract),
                      reads=[bt, bpw, btmp], writes=[bt])
                P.dve(lambda e, j=j, n=n, ck=ck: e.tensor_scalar(out=tmpa[:, 0:n], in0=Es[:, j, 0:n], scalar1=ck,
                                                                 scalar2=None, op0=ALU.mult),
                      reads=[bt, bpw], writes=[btmp])
                P.dve(lambda e, j=j, n=n, sk=sk: e.scalar_tensor_tensor(out=Es[:, j, n:2 * n], in0=Ec[:, j, 0:n],
                                                                        scalar=sk, in1=tmpa[:, 0:n],
                                                                        op0=ALU.mult, op1=ALU.add),
                      reads=[bt, bpw, btmp], writes=[bt])
            zr = V("zr")[:, j:j + 1]
            zi = V("zi")[:, j:j + 1]
            P.dve(lambda e, j=j, zi=zi: e.tensor_scalar(out=tmpa[:, :], in0=Es[:, j, :], scalar1=zi, scalar2=None,
                                                        op0=ALU.mult), reads=[bt] + VB('zr', 'zi'), writes=[btmp])
            P.dve(lambda e, j=j, zr=zr: e.scalar_tensor_tensor(out=Tr[:, j, :], in0=Ec[:, j, :], scalar=zr,
                                                               in1=tmpa[:, :], op0=ALU.mult, op1=ALU.add),
                  reads=[bt, btmp] + VB('zr', 'zi'), writes=[bt])
            P.dve(lambda e, j=j, zr=zr: e.tensor_scalar(out=tmpa[:, :], in0=Es[:, j, :], scalar1=zr, scalar2=None,
                                                        op0=ALU.mult), reads=[bt] + VB('zr', 'zi'), writes=[btmp])
            P.dve(lambda e, j=j, zi=zi: e.scalar_tensor_tensor(out=Ti[:, j, :], in0=Ec[:, j, :], scalar=zi,
                                                               in1=tmpa[:, :], op0=ALU.mult, op1=ALU.subtract),
                  reads=[bt, btmp] + VB('zr', 'zi'), writes=[bt])
        uT = [c.sb("uT%d" % i, [128, L]) for i in range(2)]; bu = [Buf(), Buf()]
        brs = c.sb("brs", [128, L]); bis = c.sb("bis", [128, L]); bbs = Buf()
        m1 = c.sb("m1", [128, L]); m2 = c.sb("m2", [128, L]); m3 = c.sb("m3", [128, L]); m4 = c.sb("m4", [128, L])
        bm = [Buf() for _ in range(4)]
        vr = c.sb("vr", [128, L]); vi = c.sb("vi", [128, L]); bv = [Buf(), Buf()]
        wr = c.sb("wr", [128, L]); wi = c.sb("wi", [128, L]); bw = [Buf(), Buf()]
        xr = c.sb("xr", [128, 4, L]); xi = c.sb("xi", [128, 4, L]); bx = [[Buf(), Buf()] for _ in range(4)]
        ini = c.sb("ini", [128, 4, 4]); bini = [Buf() for _ in range(4)]
        yo = [c.sb("yo%d" % i, [128, L]) for i in range(2)]; byo = [Buf(), Buf()]
        g1 = c.sb("g1", [128, L]); g2 = c.sb("g2", [128, L]); bg = [Buf(), Buf()]
        pB = [c.ps("pBr"), c.ps("pBi")]; bpB = [Buf(), Buf()]
        pY = [c.ps("pY0"), c.ps("pY1")]; bpY = [Buf(), Buf()]
        bout = Buf()
        GC = math.sqrt(2.0 / math.pi)
        for b in range(NB):
            ub = b % 2
            P.dma(uT[ub][:, :], zT[256:384, b * L:(b + 1) * L], reads=[bz], writes=[bu[ub]])
            for j in range(4):
                bt = btab[j]
                P.pe(lambda e, j=j, ub=ub: e.matmul(pB[0][:, :], lhsT=wB[:, 0, j, :], rhs=uT[ub][:, :],
                                                    start=True, stop=True), reads=[bwB, bu[ub]], writes=[bpB[0]])
                P.pe(lambda e, j=j, ub=ub: e.matmul(pB[1][:, :], lhsT=wB[:, 1, j, :], rhs=uT[ub][:, :],
                                                    start=True, stop=True), reads=[bwB, bu[ub]], writes=[bpB[1]])
                P.act(lambda e: e.copy(out=brs[:, :], in_=pB[0][:, :]), reads=[bpB[0]], writes=[bbs])
                P.act(lambda e: e.copy(out=bis[:, :], in_=pB[1][:, :]), reads=[bpB[1]], writes=[bbs])
                P.dve(lambda e, j=j: e.tensor_tensor(out=m1[:, :], in0=Tr[:, j, :], in1=brs[:, :], op=ALU.mult),
                      reads=[bt, bbs], writes=[bm[0]])
                P.pool(lambda e, j=j: e.tensor_tensor(out=m2[:, :], in0=Ti[:, j, :], in1=bis[:, :], op=ALU.mult),
                       reads=[bt, bbs], writes=[bm[1]])
                P.dve(lambda e, j=j: e.tensor_tensor(out=m3[:, :], in0=Tr[:, j, :], in1=bis[:, :], op=ALU.mult),
                      reads=[bt, bbs], writes=[bm[2]])
                P.pool(lambda e, j=j: e.tensor_tensor(out=m4[:, :], in0=Ti[:, j, :], in1=brs[:, :], op=ALU.mult),
                       reads=[bt, bbs], writes=[bm[3]])
                P.pool(lambda e: e.tensor_tensor(out=vr[:, :], in0=m1[:, :], in1=m2[:, :], op=ALU.subtract),
                       reads=[bm[0], bm[1]], writes=[bv[0]])
                P.pool(lambda e: e.tensor_tensor(out=vi[:, :], in0=m3[:, :], in1=m4[:, :], op=ALU.add),
                       reads=[bm[2], bm[3]], writes=[bv[1]])
                if b == 0:
                    P.dve(lambda e, j=j: e.memset(ini[:, j, :], 0.0), writes=[bini[j]])
                else:
                    c0 = pw[:, 0, 0, j:j + 1]
                    s0 = pw[:, 1, 0, j:j + 1]
                    xl = xr[:, j, L - 1:L]
                    yl = xi[:, j, L - 1:L]
                    P.dve(lambda e, j=j, s0=s0, yl=yl: e.tensor_tensor(out=ini[:, j, 2:3], in0=yl, in1=s0,
                                                                       op=ALU.mult),
                          reads=[bx[j][1], bpw], writes=[bini[j]])
                    P.dve(lambda e, j=j, c0=c0, xl=xl: e.scalar_tensor_tensor(out=ini[:, j, 0:1], in0=xl, scalar=c0,
                                                                              in1=ini[:, j, 2:3], op0=ALU.mult,
                                                                              op1=ALU.subtract),
                          reads=[bx[j][0], bpw, bini[j]], writes=[bini[j]])
                    P.dve(lambda e, j=j, c0=c0, yl=yl: e.tensor_tensor(out=ini[:, j, 3:4], in0=yl, in1=c0,
                                                                       op=ALU.mult),
                          reads=[bx[j][1], bpw], writes=[bini[j]])
                    P.dve(lambda e, j=j, s0=s0, xl=xl: e.scalar_tensor_tensor(out=ini[:, j, 1:2], in0=xl, scalar=s0,
                                                                              in1=ini[:, j, 3:4], op0=ALU.mult,
                                                                              op1=ALU.add),
                          reads=[bx[j][0], bpw, bini[j]], writes=[bini[j]])
                P.dve(lambda e, j=j: e.tensor_tensor_scan(out=wr[:, :], data0=Rr[:, j, :], data1=vr[:, :],
                                                          initial=ini[:, j, 0:1], op0=ALU.mult, op1=ALU.add),
                      reads=[bt, bv[0], bini[j]], writes=[bw[0]])
                P.dve(lambda e, j=j: e.tensor_tensor_scan(out=wi[:, :], data0=Rr[:, j, :], data1=vi[:, :],
                                                          initial=ini[:, j, 1:2], op0=ALU.mult, op1=ALU.add),
                      reads=[bt, bv[1], bini[j]], writes=[bw[1]])
                P.dve(lambda e, j=j: e.tensor_tensor(out=m1[:, :], in0=Ec[:, j, :], in1=wr[:, :], op=ALU.mult),
                      reads=[bt, bw[0]], writes=[bm[0]])
                P.pool(lambda e, j=j: e.tensor_tensor(out=m2[:, :], in0=Es[:, j, :], in1=wi[:, :], op=ALU.mult),
                       reads=[bt, bw[1]], writes=[bm[1]])
                P.dve(lambda e, j=j: e.tensor_tensor(out=m3[:, :], in0=Es[:, j, :], in1=wr[:, :], op=ALU.mult),
                      reads=[bt, bw[0]], writes=[bm[2]])
                P.pool(lambda e, j=j: e.tensor_tensor(out=m4[:, :], in0=Ec[:, j, :], in1=wi[:, :], op=ALU.mult),
                       reads=[bt, bw[1]], writes=[bm[3]])
                P.pool(lambda e, j=j: e.tensor_tensor(out=xr[:, j, :], in0=m1[:, :], in1=m2[:, :], op=ALU.subtract),
                       reads=[bm[0], bm[1]], writes=[bx[j][0]])
                P.pool(lambda e, j=j: e.tensor_tensor(out=xi[:, j, :], in0=m3[:, :], in1=m4[:, :], op=ALU.add),
                       reads=[bm[2], bm[3]], writes=[bx[j][1]])
            yb = b % 2
            for j in range(4):
                P.pe(lambda e, j=j, yb=yb: e.matmul(pY[yb][:, :], lhsT=wC[:, 0, j, :], rhs=xr[:, j, :],
                                                    start=(j == 0), stop=False),
                     reads=[bwC, bx[j][0]], writes=[bpY[yb]])
                P.pe(lambda e, j=j, yb=yb: e.matmul(pY[yb][:, :], lhsT=wC[:, 1, j, :], rhs=xi[:, j, :],
                                                    start=False, stop=(j == 3)),
                     reads=[bwC, bx[j][1]], writes=[bpY[yb]])
            P.dve(lambda e, yb=yb, ub=ub: e.scalar_tensor_tensor(out=yo[yb][:, :], in0=uT[ub][:, :], scalar=dk[:, 0:1],
                                                                 in1=pY[yb][:, :], op0=ALU.mult, op1=ALU.add),
                  reads=[bu[ub], bdk, bpY[yb]], writes=[byo[yb]])
            P.pool(lambda e, yb=yb: e.tensor_tensor(out=g1[:, :], in0=yo[yb][:, :], in1=yo[yb][:, :], op=ALU.mult),
                   reads=[byo[yb]], writes=[bg[0]])
            P.pool(lambda e: e.tensor_scalar(out=g1[:, :], in0=g1[:, :], scalar1=0.044715, scalar2=1.0,
                                             op0=ALU.mult, op1=ALU.add), reads=[bg[0]], writes=[bg[0]])
            P.pool(lambda e, yb=yb: e.tensor_tensor(out=g1[:, :], in0=g1[:, :], in1=yo[yb][:, :], op=ALU.mult),
                   reads=[bg[0], byo[yb]], writes=[bg[0]])
            P.act(lambda e: e.activation(out=g2[:, :], in_=g1[:, :], func=AF.Sigmoid, scale=2.0 * GC),
                  reads=[bg[0]], writes=[bg[1]])
            P.dve(lambda e, yb=yb: e.tensor_tensor(out=yo[yb][:, :], in0=yo[yb][:, :], in1=g2[:, :], op=ALU.mult),
                  reads=[byo[yb], bg[1]], writes=[byo[yb]])
            P.dma(yT[:, b * L:(b + 1) * L], yo[yb][:, :], reads=[byo[yb]], writes=[bout])
    P.barrier()


def phase_d(P, nc, zT, z, bz, bf, o_c):
    NKB = S // 128
    NQG = S // 512
    SCALE = 128 ** -0.5
    with contextlib.ExitStack() as st:
        c = Ctx(P, nc, st)
        qT = c.sb("qT", [128, S]); bq = Buf()
        kT = c.sb("kT", [128, S]); bk = Buf()
        va = c.sb("va", [128, NKB, 129]); bva = Buf()
        rowA = c.sb("rowA", [1, S]); brA = Buf()
        rowB = c.sb("rowB", [1, S]); brB = Buf()
        onesr = c.sb("onesr", [1, 512]); bon = Buf()
        cst = c.sb("cst", [1, 4]); bcst = Buf()
        ccol = c.sb("ccol", [128, NKB]); bcc = Buf()
        tri = c.sb("tri", [128, 128]); btri = Buf()
        PT = [c.sb("PT%d" % i, [128, 512]) for i in range(3)]; bPT = [Buf() for _ in range(3)]
        fg = [c.sb("fg%d" % i, [128, 128]) for i in range(2)]; bfg = [Buf(), Buf()]
        ot = [c.sb("ot%d" % i, [128, 128]) for i in range(2)]; bot = [Buf(), Buf()]
        rl = c.sb("rl", [128, 8]); brl = Buf()
        pST = [c.ps("pST0"), c.ps("pST1")]; bpST = [Buf(), Buf()]
        pO = [c.ps("pO%d" % i) for i in range(4)]; bpO = [Buf() for _ in range(4)]
        pC = c.ps("pC"); bpC = Buf()
        bout = Buf()

        P.dma(qT[:, :], zT[384:512, :], reads=[bz], writes=[bq])
        P.dma(kT[:, :], zT[512:640, :], reads=[bz], writes=[bk])
        P.dma(va[:, :, 0:128], z[:, 256:384].rearrange("(n p) v -> p n v", p=128), reads=[bz], writes=[bva])
        P.pool(lambda e: e.memset(va[:, :, 128:129], 1.0), writes=[bva])
        P.dma(rowA[:, :], zT[640:641, :], reads=[bz], writes=[brA])
        P.dma(cst[:, 0:1], bf, writes=[bcst])
        P.dve(lambda e: e.tensor_scalar(out=cst[:, 1:2], in0=cst[:, 0:1], scalar1=-1.0, scalar2=None, op0=ALU.mult),
              reads=[bcst], writes=[bcst])
        P.pool(lambda e: e.memset(onesr[:, :], 1.0), writes=[bon])
        P.pool(lambda e: e.memset(tri[:, :], 1.0), writes=[btri])
        P.pool(lambda e: e.affine_select(out=tri[:, :], in_=tri[:, :], pattern=[[1, 128]],
                                         compare_op=ALU.is_ge, fill=0.0, base=0, channel_multiplier=-1),
               reads=[btri], writes=[btri])
        P.act(lambda e: e.activation(out=rowA[:, :], in_=rowA[:, :], func=AF.Exp, scale=-1.0, bias=cst[:, 1:2]),
              reads=[brA, bcst], writes=[brA])
        P.act(lambda e: e.activation(out=rowA[:, :], in_=rowA[:, :], func=AF.Ln, bias=1.0),
              reads=[brA], writes=[brA])
        for h2 in range(S // 512):
            sl = slice(h2 * 512, (h2 + 1) * 512)
            init = 0.0 if h2 == 0 else rowB[:, h2 * 512 - 1:h2 * 512]
            P.dve(lambda e, sl=sl, init=init: e.tensor_tensor_scan(out=rowB[:, sl], data0=onesr[:, :],
                                                                   data1=rowA[:, sl], initial=init,
                                                                   op0=ALU.mult, op1=ALU.add),
                  reads=[brA, brB, bon], writes=[brB])
        for kb in range(NKB):
            P.pe(lambda e, kb=kb: e.matmul(pC[:, kb:kb + 1], lhsT=rowB[0:1, kb * 128:(kb + 1) * 128],
                                           rhs=onesr[0:1, 0:1], start=True, stop=True),
                 reads=[brB, bon], writes=[bpC])
        P.dve(lambda e: e.tensor_copy(out=ccol[:, :], in_=pC[:, 0:NKB]), reads=[bpC], writes=[bcc])
        P.dve(lambda e: e.tensor_scalar(out=rowB[:, :], in0=rowB[:, :], scalar1=-1.0 / SCALE, scalar2=None,
                                        op0=ALU.mult), reads=[brB], writes=[brB])
        it = 0
        for Q in range(NQG):
            oset = (Q % 2) * 2
            for ob_ in (oset, oset + 1):
                P.dve(lambda e, ob_=ob_: e.memset(pO[ob_][:, 0:258], 0.0), writes=[bpO[ob_]])
            for kb in range(4 * Q + 4):
                jlo = max(0, kb - 4 * Q)
                q0 = Q * 512 + jlo * 128
                n = 512 - jlo * 128
                sb_ = it % 2
                pb = it % 3
                it += 1
                P.pe(lambda e, kb=kb, q0=q0, n=n, sb_=sb_: e.matmul(pST[sb_][:, 0:n],
                                                                  lhsT=kT[:, kb * 128:(kb + 1) * 128],
                                                                  rhs=qT[:, q0:q0 + n], start=True, stop=False),
                     reads=[bk, bq], writes=[bpST[sb_]])
                P.pe(lambda e, q0=q0, n=n, sb_=sb_: e.matmul(pST[sb_][:, 0:n], lhsT=onesr[0:1, 0:128],
                                                           rhs=rowB[0:1, q0:q0 + n], start=False, stop=True),
                     reads=[bon, brB], writes=[bpST[sb_]])
                P.act(lambda e, kb=kb, n=n, sb_=sb_, pb=pb: e.activation(out=PT[pb][:, 0:n], in_=pST[sb_][:, 0:n],
                                                                       func=AF.Exp, scale=SCALE,
                                                                       bias=ccol[:, kb:kb + 1]),
                      reads=[bpST[sb_], bcc], writes=[bPT[pb]])
                if kb >= 4 * Q:
                    P.pool(lambda e, pb=pb: e.tensor_tensor(out=PT[pb][:, 0:128], in0=PT[pb][:, 0:128],
                                                            in1=tri[:, :], op=ALU.mult),
                           reads=[bPT[pb], btri], writes=[bPT[pb]])
                for jq in range(jlo, 4):
                    qb = 4 * Q + jq
                    off = (jq - jlo) * 128
                    ob = oset + jq // 2
                    oc = (jq % 2) * 129
                    P.pe(lambda e, kb=kb, pb=pb, off=off, ob=ob, oc=oc, qb=qb: e.matmul(
                        pO[ob][:, oc:oc + 129], lhsT=PT[pb][:, off:off + 128], rhs=va[:, kb, :],
                        start=False, stop=True, skip_group_check=True),
                        reads=[bPT[pb], bva], writes=[bpO[ob]])
            for jq in range(4):
                qb = 4 * Q + jq
                ob = oset + jq // 2
                oc = (jq % 2) * 129
                fb = qb % 2
                P.dma(fg[fb][:, :], z[qb * 128:(qb + 1) * 128, 384:512], reads=[bz], writes=[bfg[fb]])
                P.dve(lambda e, ob=ob, oc=oc, jq=jq: e.reciprocal(out=rl[:, jq:jq + 1],
                                                                  in_=pO[ob][:, oc + 128:oc + 129]),
                      reads=[bpO[ob]], writes=[brl])
                P.dve(lambda e, ob=ob, oc=oc, jq=jq, fb=fb: e.tensor_scalar(out=ot[fb][:, :],
                                                                           in0=pO[ob][:, oc:oc + 128],
                                                                           scalar1=rl[:, jq:jq + 1], scalar2=None,
                                                                           op0=ALU.mult),
                      reads=[bpO[ob], brl], writes=[bot[fb]])
                P.act(lambda e, fb=fb: e.activation(out=fg[fb][:, :], in_=fg[fb][:, :], func=AF.Silu),
                      reads=[bfg[fb]], writes=[bfg[fb]])
                P.pool(lambda e, fb=fb: e.tensor_tensor(out=ot[fb][:, :], in0=ot[fb][:, :], in1=fg[fb][:, :],
                                                        op=ALU.mult),
                       reads=[bot[fb], bfg[fb]], writes=[bot[fb]])
                P.dma(o_c[qb * 128:(qb + 1) * 128, :], ot[fb][:, :], reads=[bot[fb]], writes=[bout])
    P.barrier()


T2 = 1024
HW = 512
NW2 = 7168


def build_k2(P, nc, x, gcol, w2, bgc, yT, o_a, o_c, wglu, wa, wb, wc, wout, fg, xo, xf, T2=1024, rd_src=(), rd_x=(), wr_xo=()):
    KC = D // 128
    with contextlib.ExitStack() as st:
        c = Ctx(P, nc, st)
        ident, bid = make_ident(P, c)
        gc = c.sb("gc", [128, KC]); bgc_ = Buf()
        P.dma(gc[:, :], gcol, writes=[bgc_])
        bg = c.sb("bg", [128, 48]); bbg = Buf()
        P.dma(bg[:, :], bgc, writes=[bbg])
        fgb = c.sb("fgb", [128, D]); bfgb = Buf()
        P.dma(fgb[:, :], fg.partition_broadcast(128), writes=[bfgb])
        xs = c.sb("xs", [128, 4, D]); bxs = [Buf() for _ in range(4)]
        hb = c.sb("hb", [128, D]); bhb = Buf()
        stg0 = c.sb("stg0", [128, 1024]); bstg0 = Buf()
        stg = [stg0, stg0]; bstg = [bstg0, bstg0]
        hT = c.sb("hT", [128, KC, HW], BF16); bhT = Buf()
        oT = c.sb("oT", [128, 8, HW], BF16); boT = Buf()
        ymT = c.sb("ymT", [128, KC, HW]); bym = [Buf() for _ in range(KC)]
        yTb = c.sb("yTb", [128, 8, HW], BF16); byTb = Buf()
        mTb = hT; bmTb = [bhT for _ in range(KC)]
        w2b = [c.sb("w2b%d" % i, [128, KC, 128], BF16) for i in range(2)]; bw2b = [Buf(), Buf()]
        wbb = [c.sb("wbb%d" % i, [128, 8, 128], BF16) for i in range(2)]; bwbb = [Buf(), Buf()]
        wgb = [c.sb("wgb%d" % i, [128, 2, 8, 128], BF16) for i in range(2)]; bwgb = [Buf(), Buf()]
        wob = w2b; bwob = bw2b
        castc = [0]

        def cast(dst, bdst, src, bsrc):
            i = castc[0]; castc[0] += 1
            if i % 2 == 0:
                P.pool(lambda e: e.tensor_copy(out=dst, in_=src), reads=[bsrc], writes=[bdst])
            else:
                P.act(lambda e: e.copy(out=dst, in_=src), reads=[bsrc], writes=[bdst])
        stat = c.sb("stat", [128, 8]); bstat = Buf()
        w2s = [c.sb("w2s%d" % i, [128, KC, 128]) for i in range(2)]; bw2s = [Buf(), Buf()]
        wbs = [c.sb("wbs%d" % i, [128, 8, 128]) for i in range(2)]; bwbs = [Buf(), Buf()]
        wgs = [c.sb("wgs%d" % i, [128, 2, 8, 128]) for i in range(2)]; bwgs = [Buf(), Buf()]
        wos = w2s; bwos = bw2s
        e1 = [c.sb("e1%d" % i, [128, HW]) for i in range(2)]; be1 = [Buf(), Buf()]
        e2 = [c.sb("e2%d" % i, [128, HW]) for i in range(2)]; be2 = [Buf(), Buf()]
        e3 = [c.sb("e3%d" % i, [128, HW]) for i in range(2)]; be3 = [Buf(), Buf()]
        banks = [c.ps("bk%d" % i) for i in range(8)]; bbk = [Buf() for _ in range(8)]
        bki = [0]
        bout = Buf()

        def nb():
            i = bki[0] % 8
            bki[0] += 1
            return banks[i], bbk[i]

        cnt = dict(w2=0, wb=0, wg=0, wo=0, e=0)
        w2v = w2.rearrange("(c p) n -> p c n", p=128)
        wgv = wglu.rearrange("(c p) n -> p c n", p=128)
        wov = wout.rearrange("(c p) n -> p c n", p=128)

        def evac_T(ps_t, bps, dst, bdst, use_act):
            if use_act:
                P.act(lambda e: e.copy(out=dst, in_=ps_t[:, :].rearrange("p (a b) -> p a b", a=4)),
                      reads=[bps], writes=[bdst])
            else:
                P.dve(lambda e: e.tensor_copy(out=dst, in_=ps_t[:, :].rearrange("p (a b) -> p a b", a=4)),
                      reads=[bps], writes=[bdst])

        for hf in range(T2 // HW):
            tok0 = hf * HW
            for tt in range(4):
                t0 = tok0 + tt * 128
                P.dma(xs[:, tt, :], x[t0:t0 + 128, :], reads=rd_x, writes=[bxs[tt]])
                P.act(lambda e, tt=tt: e.activation(out=hb[:, :], in_=xs[:, tt, :], func=AF.Square,
                                                    accum_out=stat[:, 0:1]),
                      reads=[bxs[tt]], writes=[bhb, bstat])
                P.dve(lambda e: e.tensor_scalar(out=stat[:, 1:2], in0=stat[:, 0:1], scalar1=1.0 / D, scalar2=EPS,
                                                op0=ALU.mult, op1=ALU.add), reads=[bstat], writes=[bstat])
                P.act(lambda e: e.activation(out=stat[:, 2:3], in_=stat[:, 1:2], func=AF.Sqrt),
                      reads=[bstat], writes=[bstat])
                P.dve(lambda e: e.reciprocal(out=stat[:, 3:4], in_=stat[:, 2:3]), reads=[bstat], writes=[bstat])
                P.dve(lambda e, tt=tt: e.tensor_scalar(out=hb[:, :], in0=xs[:, tt, :], scalar1=stat[:, 3:4],
                                                       scalar2=None, op0=ALU.mult),
                      reads=[bxs[tt], bstat], writes=[bhb])
                for q in range(KC // 4):
                    pt, bpt = nb()
                    for jj in range(4):
                        j = q * 4 + jj
                        P.pe(lambda e, j=j, jj=jj, pt=pt: e.transpose(out=pt[:, jj * 128:(jj + 1) * 128],
                                                                      in_=hb[:, j * 128:(j + 1) * 128],
                                                                      identity=ident[:, :]),
                             reads=[bhb, bid], writes=[bpt])
                    for jj in range(4):
                        j = q * 4 + jj
                        if jj % 2 == 0:
                            P.act(lambda e, j=j, jj=jj, tt=tt, pt=pt: e.activation(
                                out=hT[:, j, tt * 128:(tt + 1) * 128], in_=pt[:, jj * 128:(jj + 1) * 128],
                                func=AF.Identity, scale=gc[:, j:j + 1]),
                                reads=[bpt, bgc_], writes=[bhT])
                        else:
                            P.dve(lambda e, j=j, jj=jj, tt=tt, pt=pt: e.tensor_scalar(
                                out=hT[:, j, tt * 128:(tt + 1) * 128], in0=pt[:, jj * 128:(jj + 1) * 128],
                                scalar1=gc[:, j:j + 1], scalar2=None, op0=ALU.mult),
                                reads=[bpt, bgc_], writes=[bhT])
            for bi_, br in enumerate(("b", "a", "c")):
                if br == "b":
                    P.add("sp", lambda e, tok0=tok0: e.dma_start(out=ymT[:, 0:8, :], in_=yT(e, tok0)),
                          reads=rd_src, writes=bym[0:8], dma=True)
                    P.dve(lambda e: e.tensor_copy(out=yTb[:, :, :], in_=ymT[:, 0:8, :]), reads=bym[0:8], writes=[byTb])
                    for eb in range(8):
                        sg = cnt["wg"] % 2; cnt["wg"] += 1
                        s2 = cnt["w2"] % 2; cnt["w2"] += 1
                        P.dma(wgs[sg][:, 0, :, :], wgv[:, :, eb * 128:(eb + 1) * 128], writes=[bwgs[sg]])
                        P.dma(wgs[sg][:, 1, :, :], wgv[:, :, 1024 + eb * 128:1024 + (eb + 1) * 128],
                              writes=[bwgs[sg]])
                        P.dma(w2s[s2][:, :, :], w2v[:, :, eb * 128:(eb + 1) * 128], writes=[bw2s[s2]])
                        cast(wgb[sg][:, :, :, :], bwgb[sg], wgs[sg][:, :, :, :], bwgs[sg])
                        cast(w2b[s2][:, :, :], bw2b[s2], w2s[s2][:, :, :], bw2s[s2])
                        pa, bpa = nb()
                        pb_, bpb = nb()
                        pc_, bpc = nb()
                        for kc in range(8):
                            P.pe(lambda e, kc=kc, sg=sg, pa=pa: e.matmul(pa[:, :], lhsT=wgb[sg][:, 0, kc, :],
                                                                        rhs=yTb[:, kc, :], start=(kc == 0),
                                                                        stop=(kc == 7)),
                                 reads=[bwgb[sg], byTb], writes=[bpa])
                        for kc in range(8):
                            P.pe(lambda e, kc=kc, sg=sg, pb_=pb_: e.matmul(pb_[:, :], lhsT=wgb[sg][:, 1, kc, :],
                                                                          rhs=yTb[:, kc, :], start=(kc == 0),
                                                                          stop=(kc == 7)),
                                 reads=[bwgb[sg], byTb], writes=[bpb])
                        for kc in range(KC):
                            P.pe(lambda e, kc=kc, s2=s2, pc_=pc_: e.matmul(pc_[:, :], lhsT=w2b[s2][:, kc, :],
                                                                          rhs=hT[:, kc, :], start=(kc == 0),
                                                                          stop=(kc == KC - 1)),
                                 reads=[bw2b[s2], bhT], writes=[bpc])
                        ei = cnt["e"] % 2; cnt["e"] += 1
                        P.act(lambda e, ei=ei, pb_=pb_: e.activation(out=e1[ei][:, :], in_=pb_[:, :],
                                                                    func=AF.Sigmoid),
                              reads=[bpb], writes=[be1[ei]])
                        P.act(lambda e, ei=ei, pc_=pc_: e.activation(out=e2[ei][:, :], in_=pc_[:, :], func=AF.Silu),
                              reads=[bpc], writes=[be2[ei]])
                        P.dve(lambda e, ei=ei, pa=pa: e.tensor_tensor(out=e3[ei][:, :], in0=pa[:, :],
                                                                     in1=e1[ei][:, :], op=ALU.mult),
                              reads=[bpa, be1[ei]], writes=[be3[ei]])
                        P.pool(lambda e, ei=ei, eb=eb: e.tensor_tensor(out=oT[:, eb, :], in0=e3[ei][:, :],
                                                                      in1=e2[ei][:, :], op=ALU.mult),
                               reads=[be3[ei], be2[ei]], writes=[boT])
                else:
                    src = o_a if br == "a" else o_c
                    for tt in range(4):
                        t0 = tok0 + tt * 128
                        sb_ = tt % 2
                        P.add("sp", lambda e, t0=t0, sb_=sb_, src=src: e.dma_start(
                            out=stg[sb_][:, :].rearrange("p (r v) -> p r v", r=8), in_=src(e, t0)),
                            reads=rd_src, writes=[bstg[sb_]], dma=True)
                        for q in range(2):
                            pt, bpt = nb()
                            for jj in range(4):
                                j = q * 4 + jj
                                P.pe(lambda e, j=j, jj=jj, pt=pt, sb_=sb_: e.transpose(
                                    out=pt[:, jj * 128:(jj + 1) * 128], in_=stg[sb_][:, j * 128:(j + 1) * 128],
                                    identity=ident[:, :]), reads=[bstg[sb_], bid], writes=[bpt])
                            evac_T(pt, bpt, oT[:, q * 4:(q + 1) * 4, tt * 128:(tt + 1) * 128], boT, q == 0)
                wbr = dict(a=wa, b=wb, c=wc)[br]
                wbv = wbr.rearrange("(c p) n -> p c n", p=128)
                gofs = dict(a=0, b=1, c=2)[br]
                for db in range(KC):
                    s2 = cnt["w2"] % 2; cnt["w2"] += 1
                    sb2 = cnt["wb"] % 2; cnt["wb"] += 1
                    col0 = 1024 + gofs * 2048 + db * 128
                    P.dma(w2s[s2][:, :, :], w2v[:, :, col0:col0 + 128], writes=[bw2s[s2]])
                    P.dma(wbs[sb2][:, :, :], wbv[:, :, db * 128:(db + 1) * 128], writes=[bwbs[sb2]])
                    cast(w2b[s2][:, :, :], bw2b[s2], w2s[s2][:, :, :], bw2s[s2])
                    cast(wbb[sb2][:, :, :], bwbb[sb2], wbs[sb2][:, :, :], bwbs[sb2])
                    pg, bpg = nb()
                    pm, bpm = nb()
                    for kc in range(KC):
                        P.pe(lambda e, kc=kc, s2=s2, pg=pg: e.matmul(pg[:, :], lhsT=w2b[s2][:, kc, :],
                                                                    rhs=hT[:, kc, :], start=(kc == 0),
                                                                    stop=(kc == KC - 1)),
                             reads=[bw2b[s2], bhT], writes=[bpg])
                    for kc in range(8):
                        P.pe(lambda e, kc=kc, sb2=sb2, pm=pm: e.matmul(pm[:, :], lhsT=wbb[sb2][:, kc, :],
                                                                      rhs=oT[:, kc, :], start=(kc == 0),
                                                                      stop=(kc == 7)),
                             reads=[bwbb[sb2], boT], writes=[bpm])
                    ei = cnt["e"] % 2; cnt["e"] += 1
                    bcol = gofs * 16 + db
                    P.act(lambda e, ei=ei, pg=pg, bcol=bcol: e.activation(out=e1[ei][:, :], in_=pg[:, :],
                                                                         func=AF.Sigmoid,
                                                                         bias=bg[:, bcol:bcol + 1]),
                          reads=[bpg, bbg], writes=[be1[ei]])
                    if bi_ == 0:
                        P.dve(lambda e, ei=ei, pm=pm, db=db: e.tensor_tensor(out=ymT[:, db, :], in0=pm[:, :],
                                                                            in1=e1[ei][:, :], op=ALU.mult),
                              reads=[bpm, be1[ei]], writes=[bym[db]])
                    else:
                        P.dve(lambda e, ei=ei, pm=pm: e.tensor_tensor(out=e3[ei][:, :], in0=pm[:, :],
                                                                     in1=e1[ei][:, :], op=ALU.mult),
                              reads=[bpm, be1[ei]], writes=[be3[ei]])
                        P.pool(lambda e, ei=ei, db=db: e.tensor_tensor(out=ymT[:, db, :], in0=ymT[:, db, :],
                                                                      in1=e3[ei][:, :], op=ALU.add),
                               reads=[bym[db], be3[ei]], writes=[bym[db]])
            for kc in range(KC):
                cast(mTb[:, kc, :], bmTb[kc], ymT[:, kc, :], bym[kc])
            for ob in range(KC):
                so = cnt["w2"] % 2; cnt["w2"] += 1
                P.dma(wos[so][:, :, :], wov[:, :, ob * 128:(ob + 1) * 128], writes=[bwos[so]])
                cast(wob[so][:, :, :], bwob[so], wos[so][:, :, :], bwos[so])
                po, bpo = nb()
                for kc in range(KC):
                    P.pe(lambda e, kc=kc, so=so, po=po: e.matmul(po[:, :], lhsT=wob[so][:, kc, :], rhs=mTb[:, kc, :],
                                                                start=(kc == 0), stop=(kc == KC - 1)),
                         reads=[bwob[so], bmTb[kc]], writes=[bpo])
                ei = cnt["e"] % 2; cnt["e"] += 1
                P.act(lambda e, ei=ei, po=po: e.copy(out=e1[ei][:, :], in_=po[:, :]), reads=[bpo], writes=[be1[ei]])
                pt, bpt = nb()
                for tt in range(4):
                    P.pe(lambda e, tt=tt, ei=ei, pt=pt: e.transpose(out=pt[:, tt * 128:(tt + 1) * 128],
                                                                   in_=e1[ei][:, tt * 128:(tt + 1) * 128],
                                                                   identity=ident[:, :]),
                         reads=[be1[ei], bid], writes=[bpt])
                P.dve(lambda e, ob=ob, pt=pt: e.tensor_tensor(out=xs[:, :, ob * 128:(ob + 1) * 128],
                                                             in0=xs[:, :, ob * 128:(ob + 1) * 128],
                                                             in1=pt[:, :].rearrange("p (a b) -> p a b", a=4),
                                                             op=ALU.add),
                      reads=bxs + [bpt], writes=bxs)
            for tt in range(4):
                t0 = tok0 + tt * 128
                if xo is not None:
                    P.dma(xo[t0:t0 + 128, :], xs[:, tt, :], reads=[bxs[tt]], writes=[bout] + list(wr_xo))
                if xf is None:
                    continue
                P.act(lambda e, tt=tt: e.activation(out=hb[:, :], in_=xs[:, tt, :], func=AF.Square,
                                                    accum_out=stat[:, 4:5]),
                      reads=[bxs[tt]], writes=[bhb, bstat])
                P.dve(lambda e: e.tensor_scalar(out=stat[:, 5:6], in0=stat[:, 4:5], scalar1=1.0 / D, scalar2=EPS,
                                                op0=ALU.mult, op1=ALU.add), reads=[bstat], writes=[bstat])
                P.act(lambda e: e.activation(out=stat[:, 6:7], in_=stat[:, 5:6], func=AF.Sqrt),
                      reads=[bstat], writes=[bstat])
                P.dve(lambda e: e.reciprocal(out=stat[:, 7:8], in_=stat[:, 6:7]), reads=[bstat], writes=[bstat])
                P.dve(lambda e, tt=tt: e.scalar_tensor_tensor(out=hb[:, :], in0=xs[:, tt, :], scalar=stat[:, 7:8],
                                                              in1=fgb[:, :], op0=ALU.mult, op1=ALU.mult),
                      reads=[bxs[tt], bstat, bfgb], writes=[bhb])
                P.dma(xf[t0:t0 + 128, :], hb[:, :], reads=[bhb], writes=[bout])
    P.barrier()


NCORE = 8


def build_fused():
    D = 2048
    T2 = S // NCORE
    nc = bass.Bass("TRN2", target_bir_lowering=False)
    i_ = lambda n, s: nc.dram_tensor(n, s, F32, kind="ExternalInput").ap()
    o_ = lambda n, s: nc.dram_tensor(n, s, F32, kind="ExternalOutput").ap()
    t_ = lambda n, s: nc.dram_tensor(n, s, F32)
    xs = i_("xs", [T2, D])
    fg = i_("fg", [1, D])
    L = []
    for l in range(2):
        s = "_%d" % l
        L.append(dict(
            g=i_("g" + s, [1, D]), w=i_("w" + s, [D, NCOL]), lb0=i_("lb0" + s, [128, 1]), lb1=i_("lb1" + s, [128, 1]),
            lsel=i_("lsel" + s, [128, 1]), hgn=i_("hgn" + s, [1, 128]), s5p=i_("s5p" + s, [128, 3, 4]),
            Bre=i_("Bre" + s, [128, 4, 128]), Bim=i_("Bim" + s, [128, 4, 128]), Cre=i_("Cre" + s, [128, 4, 128]),
            Cim=i_("Cim" + s, [128, 4, 128]), dsk=i_("dsk" + s, [128, 1]), bf=i_("bf" + s, [1, 1]),
            gcol=i_("gcol" + s, [128, 16]), w2=i_("w2" + s, [D, 7168]), bgc=i_("bgc" + s, [128, 48]),
            wglu=i_("wglu" + s, [1024, 2048]), wa=i_("wa" + s, [1024, 2048]), wb=i_("wb" + s, [1024, 2048]),
            wc=i_("wc" + s, [1024, 2048]), wout=i_("wout" + s, [2048, 2048])))
    out = o_("out", [T2, D])
    xsb = t_("xsb", [T2, D]); x_all = t_("x_all", [S, D])
    zT = t_("zT", [NFM, S]).ap(); z = t_("z", [S, NTM]).ap()
    ib_oa = t_("ib_oa", [S, 128]); ib_y = t_("ib_y", [128, S]); ib_oc = t_("ib_oc", [S, 128])
    g_oa = t_("g_oa", [NCORE * S, 128]); g_y = t_("g_y", [NCORE * 128, S]); g_oc = t_("g_oc", [NCORE * S, 128])
    P = Prog(nc)
    RG = [list(range(NCORE))]

    def allgather(src, dst):
        P.cc(lambda e: e.collective_compute("AllGather", ALU.bypass, replica_groups=RG,
                                            ins=[src.ap().opt()], outs=[dst.ap().opt()]))

    P.dma(xsb.ap(), xs)
    P.barrier()
    allgather(xsb, x_all)
    P.barrier()
    for l in range(2):
        a = L[l]
        bz = Buf("z")
        phase_a(P, nc, x_all.ap(), a["g"], a["w"], zT, z, bz)
        phase_b(P, nc, zT, z, bz, a["lb0"], a["lb1"], a["lsel"], a["hgn"], ib_oa.ap())
        phase_c(P, nc, zT, bz, a["s5p"], a["Bre"], a["Bim"], a["Cre"], a["Cim"], a["dsk"], ib_y.ap())
        phase_d(P, nc, zT, z, bz, a["bf"], ib_oc.ap())
        allgather(ib_oa, g_oa)
        allgather(ib_y, g_y)
        allgather(ib_oc, g_oc)
        P.barrier()
        g_oa_v = g_oa.ap().rearrange("(r t) v -> t r v", r=NCORE)
        g_oc_v = g_oc.ap().rearrange("(r t) v -> t r v", r=NCORE)
        g_y_ap = g_y.ap()
        ysrc = lambda e, tok0: g_y_ap[:, bass.ds(P.core(e) * T2 + tok0, HW)].rearrange("(c p) t -> p c t", p=128)
        oasrc = lambda e, t0: g_oa_v[bass.ds(P.core(e) * T2 + t0, 128), :, :]
        ocsrc = lambda e, t0: g_oc_v[bass.ds(P.core(e) * T2 + t0, 128), :, :]
        xin = xs if l == 0 else xsb.ap()
        build_k2(P, nc, xin, a["gcol"], a["w2"], a["bgc"], ysrc, oasrc, ocsrc, a["wglu"], a["wa"], a["wb"],
                 a["wc"], a["wout"], fg, xsb.ap() if l == 0 else None, out if l == 1 else None, T2=T2)
        if l == 0:
            allgather(xsb, x_all)
            P.barrier()
    P.emit()
    return nc


OFFS = [0, 1024, 2048, 3072, 4096, 5120, 6144, 7168, 8192, 9216, 9224, 10248, 16392]


def _head_cols(c):
    hq, hf, hi, hgate, su, sgate, fq, fk, fv, ff, fgate, mg = OFFS[:12]
    r = lambda o: list(range(o + 128 * c, o + 128 * c + 128))
    return np.array(r(hq) + r(hf) + r(su) + r(fq) + r(fk) + [ff + c] + r(hi) + r(hgate) + r(fv) + r(fgate))


def _layer_inputs(inp, l, c, shared):
    sl = slice(128 * c, 128 * c + 128)
    f = np.float32
    ca = np.ascontiguousarray
    m = dict(g=ca(inp["norm_g"][l][None, :]), w=ca(inp["w_in"][l][:, _head_cols(c)]),
             lb0=ca(inp["hg_lb"][0, sl][:, None]), lb1=ca(inp["hg_lb"][1, sl][:, None]),
             lsel=np.full((128, 1), float(l), f), hgn=ca(inp["hg_norm_g"][l, sl][None, :]))
    s5p = np.zeros((128, 3, 4), f)
    Bre = np.zeros((128, 4, 128), f); Bim = np.zeros((128, 4, 128), f)
    Cre = np.zeros((128, 4, 128), f); Cim = np.zeros((128, 4, 128), f)
    for j in range(4):
        for gg in range(2):
            gl = 2 * j + gg
            g = 8 * c + gl
            ps = slice(64 * gg, 64 * gg + 64)
            s5p[ps, 0, j] = inp["s5_a_re"][l, g]
            s5p[ps, 1, j] = inp["s5_a_im"][l, g]
            s5p[ps, 2, j] = inp["s5_log_dt"][l, g]
            cs = slice(16 * gl, 16 * gl + 16)
            Bre[cs, j, ps] = inp["s5_b_re"][l, g].T
            Bim[cs, j, ps] = inp["s5_b_im"][l, g].T
            Cre[ps, j, cs] = inp["s5_c_re"][l, g].T
            Cim[ps, j, cs] = inp["s5_c_im"][l, g].T
    m.update(s5p=s5p, Bre=Bre, Bim=Bim, Cre=Cre, Cim=Cim, dsk=ca(inp["s5_d"][l, sl][:, None]),
             bf=ca(inp["fox_bf"][l, c].reshape(1, 1)))
    m.update(shared[l])
    return {k + "_%d" % l: v for k, v in m.items()}


def _shared(inp, l):
    ca = np.ascontiguousarray
    sg, mg = OFFS[5], OFFS[11]
    w = inp["w_in"][l]
    return dict(gcol=ca(inp["norm_g"][l].reshape(16, 128).T),
                w2=ca(np.concatenate([w[:, sg:sg + 1024], w[:, mg:mg + 6144]], axis=1)),
                bgc=ca(inp["b_gate"][l].reshape(48, 128).T), wglu=ca(inp["s5_w_glu"][l]),
                wa=ca(inp["w_br_a"][l]), wb=ca(inp["w_br_b"][l]), wc=ca(inp["w_br_c"][l]), wout=ca(inp["w_out"][l]))


def _core_inputs(inp, c, shared, S_):
    T2_ = S_ // 8
    m = dict(xs=np.ascontiguousarray(inp["x"][0, T2_ * c:T2_ * (c + 1)]), fg=np.ascontiguousarray(inp["final_g"][None, :]))
    for l in range(2):
        m.update(_layer_inputs(inp, l, c, shared))
    return m


def kernel(**inputs):
    inp = {k: np.asarray(v, dtype=np.float32) for k, v in inputs.items()}
    shared = [_shared(inp, l) for l in range(2)]
    nc = build_fused()
    maps = [_core_inputs(inp, c, shared, 8192) for c in range(8)]
    res = run_bass_kernel_spmd(nc, maps, core_ids=list(range(8)))
    out = np.concatenate([r["out"] for r in res.results], axis=0)
    return out[None].astype(np.float32)
```

```python
import contextlib, math
from concourse.bass_utils import run_bass_kernel_spmd
import numpy as np
import concourse.bass as bass
import concourse.mybir as mybir

F32 = mybir.dt.float32
BF16 = mybir.dt.bfloat16
ALU = mybir.AluOpType
AF = mybir.ActivationFunctionType
AX = mybir.AxisListType


class Buf:
    __slots__ = ("name", "w", "r")

    def __init__(self, name=""):
        self.name = name
        self.w = None
        self.r = []


class Prog:
    ENGS = ("pe", "dve", "act", "pool", "sp")
    NDMA = 12

    def __init__(self, nc):
        self.nc = nc
        self.ops = []
        self.floor = {}
        self.since = set()

    def add(self, eng, fn, reads=(), writes=(), dma=False):
        idx = len(self.ops)
        deps = set()
        for b in reads:
            if b.w is not None:
                deps.add(b.w)
        for b in writes:
            if b.w is not None:
                deps.add(b.w)
            deps.update(b.r)
        for b in reads:
            b.r.append(idx)
        for b in writes:
            b.w = idx
            b.r = []
        if eng in self.floor:
            deps |= self.floor.pop(eng)
        deps.discard(idx)
        self.ops.append(dict(eng=eng, fn=fn, deps=deps, dma=dma))
        if dma:
            self.since.add(idx)
        return idx

    def barrier(self):
        last = {}
        for i, o in enumerate(self.ops):
            if not o["dma"]:
                last[o["eng"]] = i
        fl = set(last.values()) | self.since
        for e in self.ENGS:
            self.floor[e] = set(fl) | self.floor.get(e, set())
        self.since = set()

    def pe(self, fn, reads=(), writes=()):
        return self.add("pe", fn, reads, writes)

    def dve(self, fn, reads=(), writes=()):
        return self.add("dve", fn, reads, writes)

    def act(self, fn, reads=(), writes=()):
        return self.add("act", fn, reads, writes)

    def pool(self, fn, reads=(), writes=()):
        return self.add("pool", fn, reads, writes)

    def cc(self, fn, reads=(), writes=()):
        i = self.add("pool", fn, reads, writes, dma=True)
        self.ops[i]["cc"] = True
        return i

    def core(self, e):
        if not hasattr(self, "_core"):
            self._core = {}
        k = id(e)
        if k not in self._core:
            self._core[k] = e.partition_id()
        return self._core[k]

    def dma(self, out, in_, reads=(), writes=(), q="sp"):
        return self.add(q, lambda e: e.dma_start(out=out, in_=in_), reads, writes, dma=True)

    def emit(self):
        nc = self.nc
        ops = self.ops
        per = {e: [] for e in self.ENGS}
        seq = {}
        for i, o in enumerate(ops):
            seq[i] = len(per[o["eng"]])
            per[o["eng"]].append(i)
        dmacount = {e: 0 for e in self.ENGS}
        dmainfo = {}
        ncc = 0
        for i, o in enumerate(ops):
            if o.get("cc"):
                dmainfo[i] = ("cc", ncc, 1)
                ncc += 1
            elif o["dma"]:
                j = dmacount[o["eng"]]
                dmacount[o["eng"]] += 1
                dmainfo[i] = (o["eng"], j % self.NDMA, 16 * (j // self.NDMA + 1))
        import contextlib
        waits = {}
        signal = set()
        for ename in self.ENGS:
            waited = {}
            for i in per[ename]:
                o = ops[i]
                need = {}
                for d in o["deps"]:
                    od = ops[d]
                    if od["dma"]:
                        q, k, v = dmainfo[d]
                        sk = ("d", q, k)
                        need[sk] = max(need.get(sk, 0), v)
                    elif od["eng"] == ename:
                        if ename == "pe":
                            continue
                        if seq[i] - seq[d] <= 3:
                            sk = ("e", ename)
                            need[sk] = max(need.get(sk, 0), seq[d] + 1)
                    else:
                        sk = ("e", od["eng"])
                        need[sk] = max(need.get(sk, 0), seq[d] + 1)
                if o["dma"] and not o.get("cc"):
                    q, k, v = dmainfo[i]
                    if v > 16:
                        sk = ("d", q, k)
                        need[sk] = max(need.get(sk, 0), v - 16)
                wl = []
                for sk, v in need.items():
                    if waited.get(sk, 0) >= v:
                        continue
                    waited[sk] = v
                    wl.append((sk, v))
                    if sk[0] == "e":
                        signal.add((sk[1], v - 1))
                waits[i] = wl
        cnt = {}
        for ename in self.ENGS:
            c = 0
            arr = []
            for sq_ in range(len(per[ename])):
                if (ename, sq_) in signal:
                    c += 1
                arr.append(c)
            cnt[ename] = arr
        self.n_signal = len(signal)
        with contextlib.ExitStack() as st:
            esem = {e: st.enter_context(nc.semaphore("s_" + e)) for e in self.ENGS}
            dsem = {}
            for e in self.ENGS:
                if dmacount[e]:
                    dsem[e] = [st.enter_context(nc.semaphore("d_%s_%d" % (e, k)))
                               for k in range(min(self.NDMA, dmacount[e]))]
            dsem["cc"] = [st.enter_context(nc.semaphore("cc_%d" % k)) for k in range(ncc)]
            block = st.enter_context(nc.Block())

            def body(ename):
                def f(eng):
                    for i in per[ename]:
                        o = ops[i]
                        for sk, v in waits[i]:
                            if sk[0] == "e":
                                eng.wait_ge(esem[sk[1]], cnt[sk[1]][v - 1])
                            else:
                                eng.wait_ge(dsem[sk[1]][sk[2]], v)
                        ins = o["fn"](eng)
                        if o.get("cc"):
                            q, k, v = dmainfo[i]
                            ins.then_inc(dsem[q][k], 1)
                        elif o["dma"]:
                            q, k, v = dmainfo[i]
                            ins.then_inc(dsem[q][k], 16)
                        elif (ename, seq[i]) in signal:
                            ins.then_inc(esem[ename], 1)
                    if ename in dsem:
                        n = dmacount[ename]
                        for k in range(len(dsem[ename])):
                            c_ = (n - k + self.NDMA - 1) // self.NDMA
                            if c_ > 0:
                                eng.wait_ge(dsem[ename][k], 16 * c_)
                return f

            if per["sp"]:
                block.sync(body("sp"))
            if per["pe"]:
                block.tensor(body("pe"))
            if per["dve"]:
                block.vector(body("dve"))
            if per["act"]:
                block.scalar(body("act"))
            if per["pool"]:
                block.gpsimd(body("pool"))

D = 2048


S = 8192
NFM = 641
NTM = 512
NCOL = NFM + NTM
EPS = 1e-6


class Ctx:
    cnt = [0]

    def __init__(self, P, nc, st):
        self.P, self.nc, self.st = P, nc, st
        Ctx.cnt[0] += 1
        self.pfx = "c%d_" % Ctx.cnt[0]

    def sb(self, name, shape, dt=None):
        return self.st.enter_context(self.nc.sbuf_tensor(self.pfx + name, shape, dt or F32))

    def ps(self, name):
        return self.st.enter_context(self.nc.psum_tensor(self.pfx + name, [128, 512], F32))


def make_ident(P, c, n=128):
    ident = c.sb("ident", [128, 128]); bid = Buf()
    P.pool(lambda e: e.memset(ident[:, :], 0.0), writes=[bid])
    P.pool(lambda e: e.affine_select(out=ident[:, :], in_=ident[:, :], pattern=[[-1, 128]],
                                     compare_op=ALU.not_equal, fill=1.0, base=0, channel_multiplier=1),
           reads=[bid], writes=[bid])
    return ident, bid


def phase_a(P, nc, x, g, w, zT, z, bz):
    with contextlib.ExitStack() as st:
        c = Ctx(P, nc, st)
        KC = D // 128
        W32 = c.sb("W32", [128, KC, NCOL]); bW32 = Buf("W32")
        W = c.sb("W", [128, KC, NCOL], BF16); bW = Buf("W")
        gb = c.sb("gb", [128, D]); bgb = Buf()
        ident, bid = make_ident(P, c)
        xt = [c.sb("xt%d" % i, [128, D]) for i in range(2)]; bxt = [Buf(), Buf()]
        stat = [c.sb("stat%d" % i, [128, 4]) for i in range(2)]; bstat = [Buf(), Buf()]
        h = [c.sb("h%d" % i, [128, D]) for i in range(2)]; bh = [Buf(), Buf()]
        hT = c.sb("hT", [128, KC, 512], BF16); bhT = [Buf() for _ in range(4)]
        zst = [c.sb("zst%d" % i, [128, 512]) for i in range(3)]; bzst = [Buf() for _ in range(3)]
        zt2 = [c.sb("zt2%d" % i, [128, NTM]) for i in range(2)]; bzt2 = [Buf(), Buf()]
        pT = [c.ps("pT%d" % i) for i in range(2)]; bpT = [Buf(), Buf()]
        pF = [c.ps("pF%d" % i) for i in range(2)]; bpF = [Buf(), Buf()]
        pM = [c.ps("pM%d" % i) for i in range(2)]; bpM = [Buf() for _ in range(2)]

        P.dma(W32[:, :, :], w.rearrange("(c p) n -> p c n", p=128), writes=[bW32])
        for q_ in range(4):
            sl_ = slice(q_ * 4, q_ * 4 + 4)
            if q_ % 2 == 0:
                P.pool(lambda e, sl_=sl_: e.tensor_copy(out=W[:, sl_, :], in_=W32[:, sl_, :]), reads=[bW32], writes=[bW])
            else:
                P.act(lambda e, sl_=sl_: e.copy(out=W[:, sl_, :], in_=W32[:, sl_, :]), reads=[bW32], writes=[bW])
        P.dma(gb[:, :], g.partition_broadcast(128), writes=[bgb])
        nst = S // 512
        ti = 0
        zi = 0
        for s in range(nst):
            for tt in range(4):
                t0 = s * 512 + tt * 128
                b = ti % 2
                P.dma(xt[b][:, :], x[t0:t0 + 128, :], writes=[bxt[b]])
                P.act(lambda e, b=b: e.activation(out=h[b][:, :], in_=xt[b][:, :], func=AF.Square,
                                                  accum_out=stat[b][:, 0:1]),
                      reads=[bxt[b]], writes=[bh[b], bstat[b]])
                P.dve(lambda e, b=b: e.tensor_scalar(out=stat[b][:, 1:2], in0=stat[b][:, 0:1], scalar1=1.0 / D,
                                                     scalar2=EPS, op0=ALU.mult, op1=ALU.add),
                      reads=[bstat[b]], writes=[bstat[b]])
                P.act(lambda e, b=b: e.activation(out=stat[b][:, 2:3], in_=stat[b][:, 1:2], func=AF.Sqrt),
                      reads=[bstat[b]], writes=[bstat[b]])
                P.dve(lambda e, b=b: e.reciprocal(out=stat[b][:, 3:4], in_=stat[b][:, 2:3]),
                      reads=[bstat[b]], writes=[bstat[b]])
                P.dve(lambda e, b=b: e.scalar_tensor_tensor(out=h[b][:, :], in0=xt[b][:, :],
                                                            scalar=stat[b][:, 3:4], in1=gb[:, :],
                                                            op0=ALU.mult, op1=ALU.mult),
                      reads=[bxt[b], bstat[b], bgb], writes=[bh[b]])
                for q in range(KC // 4):
                    pb = (ti * (KC // 4) + q) % 2
                    for jj in range(4):
                        j = q * 4 + jj
                        P.pe(lambda e, b=b, j=j, jj=jj, pb=pb: e.transpose(
                            out=pT[pb][:, jj * 128:(jj + 1) * 128], in_=h[b][:, j * 128:(j + 1) * 128],
                            identity=ident[:, :]),
                            reads=[bh[b], bid], writes=[bpT[pb]])
                    if q % 2 == 0:
                        P.act(lambda e, q=q, tt=tt, pb=pb: e.copy(
                            out=hT[:, q * 4:(q + 1) * 4, tt * 128:(tt + 1) * 128],
                            in_=pT[pb][:, :].rearrange("p (a b) -> p a b", a=4)),
                            reads=[bpT[pb]], writes=[bhT[tt]])
                    else:
                        P.dve(lambda e, q=q, tt=tt, pb=pb: e.tensor_copy(
                            out=hT[:, q * 4:(q + 1) * 4, tt * 128:(tt + 1) * 128],
                            in_=pT[pb][:, :].rearrange("p (a b) -> p a b", a=4)),
                            reads=[bpT[pb]], writes=[bhT[tt]])
                ti += 1
            for cb in range(6):
                pb = cb % 2
                m = 128 if cb < 5 else 1
                for j in range(KC):
                    P.pe(lambda e, cb=cb, j=j, pb=pb, m=m: e.matmul(
                        pF[pb][0:m, :], lhsT=W[:, j, cb * 128:cb * 128 + m], rhs=hT[:, j, :],
                        start=(j == 0), stop=(j == KC - 1)),
                        reads=[bW] + bhT, writes=[bpF[pb]])
                zb = zi % 3
                zi += 1
                if cb % 2 == 0:
                    P.act(lambda e, zb=zb, pb=pb, m=m: e.copy(out=zst[zb][0:m, :], in_=pF[pb][0:m, :]),
                          reads=[bpF[pb]], writes=[bzst[zb]])
                else:
                    P.dve(lambda e, zb=zb, pb=pb, m=m: e.tensor_copy(out=zst[zb][0:m, :], in_=pF[pb][0:m, :]),
                          reads=[bpF[pb]], writes=[bzst[zb]])
                P.dma(zT[cb * 128:cb * 128 + m, s * 512:(s + 1) * 512], zst[zb][0:m, :],
                      reads=[bzst[zb]], writes=[bz])
            for tt in range(4):
                t0 = s * 512 + tt * 128
                pa = tt % 2
                for j in range(KC):
                    P.pe(lambda e, j=j, tt=tt, pa=pa: e.matmul(
                        pM[pa][:, :], lhsT=hT[:, j, tt * 128:(tt + 1) * 128], rhs=W[:, j, NFM:NFM + 512],
                        start=(j == 0), stop=(j == KC - 1)),
                        reads=[bW, bhT[tt]], writes=[bpM[pa]])
                zb = tt % 2
                if tt % 2 == 0:
                    P.act(lambda e, zb=zb, pa=pa: e.copy(out=zt2[zb][:, :], in_=pM[pa][:, :]),
                          reads=[bpM[pa]], writes=[bzt2[zb]])
                else:
                    P.dve(lambda e, zb=zb, pa=pa: e.tensor_copy(out=zt2[zb][:, :], in_=pM[pa][:, :]),
                          reads=[bpM[pa]], writes=[bzt2[zb]])
                P.dma(z[t0:t0 + 128, :], zt2[zb][:, :], reads=[bzt2[zb]], writes=[bz])
    P.barrier()


def bc_mid(ap2, n):
    p, a = ap2.shape
    return ap2.unsqueeze(2).to_broadcast([p, a, n])


def phase_b(P, nc, zT, z, bz, lb0, lb1, lsel, hgn, o_a):
    SEG = 1024
    NCH = SEG // 64
    with contextlib.ExitStack() as st:
        c = Ctx(P, nc, st)
        ident, bid = make_ident(P, c)
        lbt = c.sb("lbt", [128, 8]); blb = Buf()
        P.dma(lbt[:, 0:1], lb0, writes=[blb])
        P.dma(lbt[:, 1:2], lb1, writes=[blb])
        P.dma(lbt[:, 2:3], lsel, writes=[blb])
        P.dve(lambda e: e.tensor_tensor(out=lbt[:, 3:4], in0=lbt[:, 1:2], in1=lbt[:, 0:1], op=ALU.subtract),
              reads=[blb], writes=[blb])
        P.act(lambda e: e.activation(out=lbt[:, 4:5], in_=lbt[:, 3:4], func=AF.Sigmoid), reads=[blb], writes=[blb])
        P.dve(lambda e: e.tensor_tensor(out=lbt[:, 5:6], in0=lbt[:, 4:5], in1=lbt[:, 2:3], op=ALU.mult),
              reads=[blb], writes=[blb])
        P.dve(lambda e: e.tensor_scalar(out=lbt[:, 6:7], in0=lbt[:, 5:6], scalar1=-1.0, scalar2=1.0,
                                        op0=ALU.mult, op1=ALU.add), reads=[blb], writes=[blb])
        gnb = c.sb("gnb", [64, 128]); bgn = Buf()
        P.dma(gnb[:, :], hgn.partition_broadcast(64), writes=[bgn])
        mask01 = c.sb("mask01", [128, SEG]); bmk = Buf()
        P.pool(lambda e: e.memset(mask01[:, :], 1.0), writes=[bmk])
        P.pool(lambda e: e.memset(mask01[:, :].rearrange("p (n c) -> p n c", c=64)[:, :, 0:1], 0.0),
               reads=[bmk], writes=[bmk])
        tri = c.sb("tri", [64, 64]); btri = Buf()
        P.pool(lambda e: e.memset(tri[:, :], 1.0), writes=[btri])
        P.pool(lambda e: e.affine_select(out=tri[:, :], in_=tri[:, :], pattern=[[1, 64]],
                                         compare_op=ALU.is_ge, fill=0.0, base=0, channel_multiplier=-1),
               reads=[btri], writes=[btri])
        names = ["q", "f", "lf", "cum", "kk", "A", "E2", "qt", "kd"]
        T = {n: c.sb("t_" + n, [128, SEG]) for n in names}
        B = {n: Buf(n) for n in names}
        sm = c.sb("sm", [128, 4, NCH]); bsm = Buf()
        i_tm = c.sb("i_tm", [64, NCH, 128]); bi = Buf()
        g_tm = c.sb("g_tm", [64, NCH, 128]); bg = Buf()
        kd_tm = c.sb("kd_tm", [64, NCH, 128]); bkdt = [Buf() for _ in range(NCH // 4)]
        sT = c.sb("sT", [64, NCH, 64]); bsT = [Buf() for _ in range(NCH // 8)]
        Sall = c.sb("Sall", [128, NCH + 1, 128]); bS = [Buf() for _ in range(NCH + 1)]
        oseg = c.sb("oseg", [64, NCH, 128]); bo = [Buf() for _ in range(NCH // 4)]
        sq = c.sb("sq", [64, NCH, 128]); bsq = Buf()
        st2 = c.sb("st2", [64, 4, NCH]); bst2 = Buf()
        pK = c.ps("pK"); bpK = Buf()
        pS = c.ps("pS"); bpS = Buf()
        pU = [c.ps("pU0"), c.ps("pU1")]; bpU = [Buf(), Buf()]
        pO = [c.ps("pO0"), c.ps("pO1")]; bpO = [Buf(), Buf()]
        bout = Buf()

        P.dve(lambda e: e.memset(Sall[:, 0, :], 0.0), writes=[bS[0]])
        v3 = lambda t: t[:, :].rearrange("p (n c) -> p n c", c=64)
        for seg in range(S // SEG):
            r0 = seg * SEG
            P.dma(T["q"][:, :], zT[0:128, r0:r0 + SEG], reads=[bz], writes=[B["q"]])
            P.dma(T["f"][:, :], zT[128:256, r0:r0 + SEG], reads=[bz], writes=[B["f"]])
            P.dma(i_tm[:, :, :], z[r0:r0 + SEG, 0:128].rearrange("(n p) v -> p n v", p=64), reads=[bz], writes=[bi])
            P.dma(g_tm[:, :, :], z[r0:r0 + SEG, 128:256].rearrange("(n p) v -> p n v", p=64), reads=[bz],
                  writes=[bg])
            if seg > 0:
                P.dve(lambda e: e.tensor_copy(out=Sall[:, 0, :], in_=Sall[:, NCH, :]),
                      reads=[bS[NCH]], writes=[bS[0]])
            P.act(lambda e: e.activation(out=T["f"][:, :], in_=T["f"][:, :], func=AF.Sigmoid),
                  reads=[B["f"]], writes=[B["f"]])
            P.dve(lambda e: e.tensor_scalar(out=T["f"][:, :], in0=T["f"][:, :], scalar1=lbt[:, 6:7],
                                            scalar2=lbt[:, 5:6], op0=ALU.mult, op1=ALU.add),
                  reads=[B["f"], blb], writes=[B["f"]])
            P.act(lambda e: e.activation(out=T["lf"][:, :], in_=T["f"][:, :], func=AF.Ln),
                  reads=[B["f"]], writes=[B["lf"]])
            P.pool(lambda e: e.tensor_scalar(out=T["kk"][:, :], in0=T["f"][:, :], scalar1=-1.0, scalar2=1.0,
                                             op0=ALU.mult, op1=ALU.add),
                   reads=[B["f"]], writes=[B["kk"]])
            P.dve(lambda e: e.tensor_tensor_scan(out=T["cum"][:, :], data0=mask01[:, :], data1=T["lf"][:, :],
                                                 initial=0.0, op0=ALU.mult, op1=ALU.add),
                  reads=[B["lf"], bmk], writes=[B["cum"]])
            cum3 = v3(T["cum"])
            last = cum3[:, :, 63]
            mid = cum3[:, :, 31]
            P.act(lambda e: e.activation(out=sm[:, 0, :], in_=mid, func=AF.Exp, scale=-1.0),
                  reads=[B["cum"]], writes=[bsm])
            P.dve(lambda e: e.tensor_tensor(out=sm[:, 3, :], in0=last, in1=mid, op=ALU.subtract),
                  reads=[B["cum"]], writes=[bsm])
            P.act(lambda e: e.activation(out=sm[:, 1, :], in_=sm[:, 3, :], func=AF.Exp), reads=[bsm], writes=[bsm])
            P.act(lambda e: e.activation(out=sm[:, 2, :], in_=last, func=AF.Exp), reads=[B["cum"]], writes=[bsm])
            P.act(lambda e: e.activation(out=T["A"][:, :], in_=T["cum"][:, :], func=AF.Exp),
                  reads=[B["cum"]], writes=[B["A"]])
            P.dve(lambda e: e.tensor_tensor(out=T["A"][:, :], in0=T["A"][:, :], in1=T["q"][:, :], op=ALU.mult),
                  reads=[B["A"], B["q"]], writes=[B["A"]])
            P.dve(lambda e: e.tensor_tensor(out=v3(T["E2"]), in0=bc_mid(mid, 64), in1=cum3, op=ALU.subtract),
                  reads=[B["cum"]], writes=[B["E2"]])
            P.act(lambda e: e.activation(out=T["E2"][:, :], in_=T["E2"][:, :], func=AF.Exp),
                  reads=[B["E2"]], writes=[B["E2"]])
            P.pool(lambda e: e.tensor_tensor(out=T["E2"][:, :], in0=T["E2"][:, :], in1=T["kk"][:, :], op=ALU.mult),
                   reads=[B["E2"], B["kk"]], writes=[B["E2"]])
            P.dve(lambda e: e.tensor_tensor(out=v3(T["qt"]), in0=v3(T["A"]), in1=bc_mid(sm[:, 0, :], 64),
                                            op=ALU.mult),
                  reads=[B["A"], bsm], writes=[B["qt"]])
            P.pool(lambda e: e.tensor_tensor(out=v3(T["kd"]), in0=v3(T["E2"]), in1=bc_mid(sm[:, 1, :], 64),
                                             op=ALU.mult),
                   reads=[B["E2"], bsm], writes=[B["kd"]])
            for q4 in range(NCH // 4):
                for jj in range(4):
                    n = q4 * 4 + jj
                    P.pe(lambda e, n=n, jj=jj: e.transpose(out=pK[0:64, jj * 128:(jj + 1) * 128],
                                                           in_=T["kd"][:, n * 64:(n + 1) * 64],
                                                           identity=ident[:, :]),
                         reads=[B["kd"], bid], writes=[bpK])
                P.act(lambda e, q4=q4: e.copy(out=kd_tm[:, q4 * 4:(q4 + 1) * 4, :],
                                              in_=pK[0:64, :].rearrange("p (a b) -> p a b", a=4)),
                      reads=[bpK], writes=[bkdt[q4]])
            for q8 in range(NCH // 8):
                for jj in range(8):
                    n = q8 * 8 + jj
                    P.pe(lambda e, n=n, jj=jj: e.matmul(pS[0:64, jj * 64:(jj + 1) * 64],
                                                        lhsT=T["E2"][:, n * 64:(n + 1) * 64],
                                                        rhs=T["qt"][:, n * 64:(n + 1) * 64], start=True, stop=True),
                         reads=[B["E2"], B["qt"]], writes=[bpS])
                P.dve(lambda e, q8=q8: e.tensor_tensor(
                    out=sT[:, q8 * 8:(q8 + 1) * 8, :],
                    in0=pS[0:64, :].rearrange("p (a b) -> p a b", a=8),
                    in1=tri[:, :].unsqueeze(1).to_broadcast([64, 8, 64]), op=ALU.mult),
                    reads=[bpS, btri], writes=[bsT[q8]])
            for q4 in range(NCH // 4):
                ub = q4 % 2
                for jj in range(4):
                    n = q4 * 4 + jj
                    P.pe(lambda e, n=n, jj=jj, ub=ub: e.matmul(pU[ub][:, jj * 128:(jj + 1) * 128],
                                                               lhsT=kd_tm[:, n, :], rhs=i_tm[:, n, :],
                                                               start=True, stop=True),
                         reads=[bkdt[q4], bi], writes=[bpU[ub]])
                for jj in range(4):
                    n = q4 * 4 + jj
                    P.dve(lambda e, n=n, jj=jj, ub=ub: e.scalar_tensor_tensor(
                        out=Sall[:, n + 1, :], in0=Sall[:, n, :], scalar=sm[:, 2, n:n + 1],
                        in1=pU[ub][:, jj * 128:(jj + 1) * 128], op0=ALU.mult, op1=ALU.add),
                        reads=[bS[n], bsm, bpU[ub]], writes=[bS[n + 1]])
            for q4 in range(NCH // 4):
                ob = q4 % 2
                for jj in range(4):
                    n = q4 * 4 + jj
                    P.pe(lambda e, n=n, jj=jj, ob=ob: e.matmul(pO[ob][0:64, jj * 128:(jj + 1) * 128],
                                                               lhsT=T["A"][:, n * 64:(n + 1) * 64],
                                                               rhs=Sall[:, n, :], start=True, stop=False),
                         reads=[B["A"], bS[n]], writes=[bpO[ob]])
                    P.pe(lambda e, n=n, jj=jj, ob=ob: e.matmul(pO[ob][0:64, jj * 128:(jj + 1) * 128],
                                                               lhsT=sT[:, n, :], rhs=i_tm[:, n, :],
                                                               start=False, stop=True),
                         reads=[bsT[n // 8], bi], writes=[bpO[ob]])
                P.act(lambda e, q4=q4, ob=ob: e.copy(out=oseg[:, q4 * 4:(q4 + 1) * 4, :],
                                                     in_=pO[ob][0:64, :].rearrange("p (a b) -> p a b", a=4)),
                      reads=[bpO[ob]], writes=[bo[q4]])
            P.pool(lambda e: e.tensor_tensor(out=sq[:, :, :], in0=oseg[:, :, :], in1=oseg[:, :, :], op=ALU.mult),
                   reads=bo, writes=[bsq])
            P.dve(lambda e: e.tensor_reduce(out=st2[:, 0, :], in_=sq[:, :, :], axis=AX.X, op=ALU.add),
                  reads=[bsq], writes=[bst2])
            P.dve(lambda e: e.tensor_scalar(out=st2[:, 1, :], in0=st2[:, 0, :], scalar1=1.0 / 128, scalar2=EPS,
                                            op0=ALU.mult, op1=ALU.add), reads=[bst2], writes=[bst2])
            P.act(lambda e: e.activation(out=st2[:, 2, :], in_=st2[:, 1, :], func=AF.Sqrt), reads=[bst2],
                  writes=[bst2])
            P.dve(lambda e: e.reciprocal(out=st2[:, 3, :], in_=st2[:, 2, :]), reads=[bst2], writes=[bst2])
            P.dve(lambda e: e.tensor_tensor(out=oseg[:, :, :], in0=oseg[:, :, :], in1=bc_mid(st2[:, 3, :], 128),
                                            op=ALU.mult), reads=bo + [bst2], writes=bo)
            P.pool(lambda e: e.tensor_tensor(out=oseg[:, :, :], in0=oseg[:, :, :],
                                             in1=gnb[:, :].unsqueeze(1).to_broadcast([64, NCH, 128]), op=ALU.mult),
                   reads=bo + [bgn], writes=bo)
            P.act(lambda e: e.activation(out=g_tm[:, :, :], in_=g_tm[:, :, :], func=AF.Silu), reads=[bg],
                  writes=[bg])
            P.dve(lambda e: e.tensor_tensor(out=oseg[:, :, :], in0=oseg[:, :, :], in1=g_tm[:, :, :], op=ALU.mult),
                  reads=bo + [bg], writes=bo)
            P.dma(o_a[r0:r0 + SEG, :].rearrange("(n p) v -> p n v", p=64), oseg[:, :, :], reads=bo, writes=[bout])
    P.barrier()


def phase_c(P, nc, zT, bz, s5p, Bre, Bim, Cre, Cim, dsk, yT):
    L = 512
    NB = S // L
    PI = math.pi
    with contextlib.ExitStack() as st:
        c = Ctx(P, nc, st)
        par = c.sb("par", [128, 3, 4]); bpar = Buf()
        P.dma(par[:, :, :], s5p, writes=[bpar])
        wB = c.sb("wB", [128, 2, 4, 128]); bwB = Buf()
        P.dma(wB[:, 0, :, :], Bre, writes=[bwB])
        P.dma(wB[:, 1, :, :], Bim, writes=[bwB])
        wC = c.sb("wC", [128, 2, 4, 128]); bwC = Buf()
        P.dma(wC[:, 0, :, :], Cre, writes=[bwC])
        P.dma(wC[:, 1, :, :], Cim, writes=[bwC])
        P.dve(lambda e: e.tensor_scalar(out=wC[:, 1, :, :], in0=wC[:, 1, :, :], scalar1=-1.0, scalar2=None,
                                        op0=ALU.mult), reads=[bwC], writes=[bwC])
        dk = c.sb("dk", [128, 1]); bdk = Buf()
        P.dma(dk[:, :], dsk, writes=[bdk])
        NV = 40
        sc = c.sb("sc", [128, NV, 4]); bsc = Buf()
        names = {}

        vb = {}

        def V(n):
            if n not in names:
                names[n] = len(names)
                vb[n] = Buf(n)
                assert names[n] < NV
            return sc[:, names[n], :]

        def VB(*ns):
            for n in ns:
                V(n)
            return [vb[n] for n in ns]

        def tt(o, a, b, op):
            P.dve(lambda e: e.tensor_tensor(out=V(o), in0=V(a), in1=V(b), op=op), reads=VB(a, b), writes=VB(o))

        def ts(o, a, s1, op0, s2=None, op1=None):
            if op1 is None:
                P.dve(lambda e: e.tensor_scalar(out=V(o), in0=V(a), scalar1=s1, scalar2=None, op0=op0),
                      reads=VB(a), writes=VB(o))
            else:
                P.dve(lambda e: e.tensor_scalar(out=V(o), in0=V(a), scalar1=s1, scalar2=s2, op0=op0, op1=op1),
                      reads=VB(a), writes=VB(o))

        def stt(o, a, s, b, op0, op1):
            P.dve(lambda e: e.scalar_tensor_tensor(out=V(o), in0=V(a), scalar=s, in1=V(b), op0=op0, op1=op1),
                  reads=VB(a, b), writes=VB(o))

        def act(o, a, f, scale=1.0):
            P.act(lambda e: e.activation(out=V(o), in_=V(a), func=f, scale=scale), reads=VB(a), writes=VB(o))

        for n_, k_ in (("ar", 0), ("ai", 1), ("ldt", 2)):
            P.dve(lambda e, n_=n_, k_=k_: e.tensor_copy(out=V(n_), in_=par[:, k_, :]), reads=[bpar], writes=VB(n_))
        act("dt", "ldt", AF.Exp)
        tt("m1", "ar", "dt", ALU.mult)
        act("mag", "m1", AF.Exp)
        tt("ang", "ai", "dt", ALU.mult)
        ts("kq", "ang", PI, ALU.is_gt)
        for m_ in range(1, 7):
            stt("kq", "ang", (2 * m_ + 1) * PI, "kq", ALU.is_gt, ALU.add)
        stt("y", "kq", -2.0 * PI, "ang", ALU.mult, ALU.add)
        ts("x8", "y", 0.125, ALU.mult)
        tt("x2", "x8", "x8", ALU.mult)
        ts("p", "x2", -1.0 / 5040, ALU.mult)
        stt("p", "p", 1.0 / 120, "x2", ALU.add, ALU.mult)
        stt("p", "p", -1.0 / 6, "x2", ALU.add, ALU.mult)
        stt("s", "p", 1.0, "x8", ALU.add, ALU.mult)
        ts("q", "x2", 1.0 / 40320, ALU.mult)
        stt("q", "q", -1.0 / 720, "x2", ALU.add, ALU.mult)
        stt("q", "q", 1.0 / 24, "x2", ALU.add, ALU.mult)
        stt("q", "q", -0.5, "x2", ALU.add, ALU.mult)
        ts("c", "q", 1.0, ALU.add)
        for _ in range(3):
            tt("cc", "c", "c", ALU.mult)
            tt("ss", "s", "s", ALU.mult)
            stt("s", "s", 2.0, "c", ALU.mult, ALU.mult)
            tt("c", "cc", "ss", ALU.subtract)
        tt("abr", "mag", "c", ALU.mult)
        tt("abi", "mag", "s", ALU.mult)
        ts("nr", "abr", -1.0, ALU.add)
        tt("d1", "ar", "ar", ALU.mult)
        tt("d2", "ai", "ai", ALU.mult)
        tt("den", "d1", "d2", ALU.add)
        P.dve(lambda e: e.reciprocal(out=V("rden"), in_=V("den")), reads=VB("den"), writes=VB("rden"))
        tt("t1", "nr", "ar", ALU.mult)
        tt("t2", "abi", "ai", ALU.mult)
        tt("t1", "t1", "t2", ALU.add)
        tt("zr", "t1", "rden", ALU.mult)
        tt("t1", "abi", "ar", ALU.mult)
        tt("t2", "nr", "ai", ALU.mult)
        tt("t1", "t1", "t2", ALU.subtract)
        tt("zi", "t1", "rden", ALU.mult)
        pw = c.sb("pw", [128, 2, 10, 4]); bpw = Buf()
        P.dve(lambda e: e.tensor_copy(out=pw[:, 0, 0, :], in_=V("c")), reads=VB('c', 's'), writes=[bpw])
        P.dve(lambda e: e.tensor_copy(out=pw[:, 1, 0, :], in_=V("s")), reads=VB('c', 's'), writes=[bpw])
        for k in range(9):
            P.dve(lambda e, k=k: e.tensor_tensor(out=V("cc"), in0=pw[:, 0, k, :], in1=pw[:, 0, k, :], op=ALU.mult),
                  reads=[bpw], writes=VB('cc', 'ss'))
            P.dve(lambda e, k=k: e.tensor_tensor(out=V("ss"), in0=pw[:, 1, k, :], in1=pw[:, 1, k, :], op=ALU.mult),
                  reads=[bpw], writes=VB('cc', 'ss'))
            P.dve(lambda e, k=k: e.scalar_tensor_tensor(out=pw[:, 1, k + 1, :], in0=pw[:, 1, k, :], scalar=2.0,
                                                        in1=pw[:, 0, k, :], op0=ALU.mult, op1=ALU.mult),
                  reads=[bpw] + VB('cc', 'ss'), writes=[bpw])
            P.dve(lambda e, k=k: e.tensor_tensor(out=pw[:, 0, k + 1, :], in0=V("cc"), in1=V("ss"), op=ALU.subtract),
                  reads=[bpw] + VB('cc', 'ss'), writes=[bpw])
        Ec = c.sb("Ec", [128, 4, L]); Es = c.sb("Es", [128, 4, L])
        Tr = c.sb("Tr", [128, 4, L]); Ti = c.sb("Ti", [128, 4, L]); Rr = c.sb("Rr", [128, 4, L])
        btab = [Buf() for _ in range(4)]
        tmpa = c.sb("tmpa", [128, L]); btmp = Buf()
        for j in range(4):
            bt = btab[j]
            P.pool(lambda e, j=j: e.memset(Ec[:, j, 0:1], 1.0), writes=[bt])
            P.pool(lambda e, j=j: e.memset(Es[:, j, 0:1], 0.0), writes=[bt])
            P.pool(lambda e, j=j: e.memset(Rr[:, j, :], 1.0), writes=[bt])
            P.dve(lambda e, j=j: e.tensor_scalar(out=Rr[:, j, :], in0=Rr[:, j, :], scalar1=V("mag")[:, j:j + 1],
                                                 scalar2=None, op0=ALU.mult), reads=[bt] + VB('mag'), writes=[bt])
            for k in range(9):
                n = 1 << k
                ck = pw[:, 0, k, j:j + 1]
                sk = pw[:, 1, k, j:j + 1]
                P.dve(lambda e, j=j, n=n, sk=sk: e.tensor_scalar(out=tmpa[:, 0:n], in0=Es[:, j, 0:n], scalar1=sk,
                                                                 scalar2=None, op0=ALU.mult),
                      reads=[bt, bpw], writes=[btmp])
                P.dve(lambda e, j=j, n=n, ck=ck: e.scalar_tensor_tensor(out=Ec[:, j, n:2 * n], in0=Ec[:, j, 0:n],
                                                                        scalar=ck, in1=tmpa[:, 0:n],
                                                                        op0=ALU.mult, op1=ALU.subtract),
                      reads=[bt, bpw, btmp], writes=[bt])
                P.dve(lambda e, j=j, n=n, ck=ck: e.tensor_scalar(out=tmpa[:, 0:n], in0=Es[:, j, 0:n], scalar1=ck,
                                                                 scalar2=None, op0=ALU.mult),
                      reads=[bt, bpw], writes=[btmp])
                P.dve(lambda e, j=j, n=n, sk=sk: e.scalar_tensor_tensor(out=Es[:, j, n:2 * n], in0=Ec[:, j, 0:n],
                                                                        scalar=sk, in1=tmpa[:, 0:n],
                                                                        op0=ALU.mult, op1=ALU.add),
                      reads=[bt, bpw, btmp], writes=[bt])
            zr = V("zr")[:, j:j + 1]
            zi = V("zi")[:, j:j + 1]
            P.dve(lambda e, j=j, zi=zi: e.tensor_scalar(out=tmpa[:, :], in0=Es[:, j, :], scalar1=zi, scalar2=None,
                                                        op0=ALU.mult), reads=[bt] + VB('zr', 'zi'), writes=[btmp])
            P.dve(lambda e, j=j, zr=zr: e.scalar_tensor_tensor(out=Tr[:, j, :], in0=Ec[:, j, :], scalar=zr,
                                                               in1=tmpa[:, :], op0=ALU.mult, op1=ALU.add),
                  reads=[bt, btmp] + VB('zr', 'zi'), writes=[bt])
            P.dve(lambda e, j=j, zr=zr: e.tensor_scalar(out=tmpa[:, :], in0=Es[:, j, :], scalar1=zr, scalar2=None,
                                                        op0=ALU.mult), reads=[bt] + VB('zr', 'zi'), writes=[btmp])
            P.dve(lambda e, j=j, zi=zi: e.scalar_tensor_tensor(out=Ti[:, j, :], in0=Ec[:, j, :], scalar=zi,
                                                               in1=tmpa[:, :], op0=ALU.mult, op1=ALU.subtract),
                  reads=[bt, btmp] + VB('zr', 'zi'), writes=[bt])
        uT = [c.sb("uT%d" % i, [128, L]) for i in range(2)]; bu = [Buf(), Buf()]
        brs = c.sb("brs", [128, L]); bis = c.sb("bis", [128, L]); bbs = Buf()
        m1 = c.sb("m1", [128, L]); m2 = c.sb("m2", [128, L]); m3 = c.sb("m3", [128, L]); m4 = c.sb("m4", [128, L])
        bm = [Buf() for _ in range(4)]
        vr = c.sb("vr", [128, L]); vi = c.sb("vi", [128, L]); bv = [Buf(), Buf()]
        wr = c.sb("wr", [128, L]); wi = c.sb("wi", [128, L]); bw = [Buf(), Buf()]
        xr = c.sb("xr", [128, 4, L]); xi = c.sb("xi", [128, 4, L]); bx = [[Buf(), Buf()] for _ in range(4)]
        ini = c.sb("ini", [128, 4, 4]); bini = [Buf() for _ in range(4)]
        yo = [c.sb("yo%d" % i, [128, L]) for i in range(2)]; byo = [Buf(), Buf()]
        g1 = c.sb("g1", [128, L]); g2 = c.sb("g2", [128, L]); bg = [Buf(), Buf()]
        pB = [c.ps("pBr"), c.ps("pBi")]; bpB = [Buf(), Buf()]
        pY = [c.ps("pY0"), c.ps("pY1")]; bpY = [Buf(), Buf()]
        bout = Buf()
        GC = math.sqrt(2.0 / math.pi)
        for b in range(NB):
            ub = b % 2
            P.dma(uT[ub][:, :], zT[256:384, b * L:(b + 1) * L], reads=[bz], writes=[bu[ub]])
            for j in range(4):
                bt = btab[j]
                P.pe(lambda e, j=j, ub=ub: e.matmul(pB[0][:, :], lhsT=wB[:, 0, j, :], rhs=uT[ub][:, :],
                                                    start=True, stop=True), reads=[bwB, bu[ub]], writes=[bpB[0]])
                P.pe(lambda e, j=j, ub=ub: e.matmul(pB[1][:, :], lhsT=wB[:, 1, j, :], rhs=uT[ub][:, :],
                                                    start=True, stop=True), reads=[bwB, bu[ub]], writes=[bpB[1]])
                P.act(lambda e: e.copy(out=brs[:, :], in_=pB[0][:, :]), reads=[bpB[0]], writes=[bbs])
                P.act(lambda e: e.copy(out=bis[:, :], in_=pB[1][:, :]), reads=[bpB[1]], writes=[bbs])
                P.dve(lambda e, j=j: e.tensor_tensor(out=m1[:, :], in0=Tr[:, j, :], in1=brs[:, :], op=ALU.mult),
                      reads=[bt, bbs], writes=[bm[0]])
                P.pool(lambda e, j=j: e.tensor_tensor(out=m2[:, :], in0=Ti[:, j, :], in1=bis[:, :], op=ALU.mult),
                       reads=[bt, bbs], writes=[bm[1]])
                P.dve(lambda e, j=j: e.tensor_tensor(out=m3[:, :], in0=Tr[:, j, :], in1=bis[:, :], op=ALU.mult),
                      reads=[bt, bbs], writes=[bm[2]])
                P.pool(lambda e, j=j: e.tensor_tensor(out=m4[:, :], in0=Ti[:, j, :], in1=brs[:, :], op=ALU.mult),
                       reads=[bt, bbs], writes=[bm[3]])
                P.pool(lambda e: e.tensor_tensor(out=vr[:, :], in0=m1[:, :], in1=m2[:, :], op=ALU.subtract),
                       reads=[bm[0], bm[1]], writes=[bv[0]])
                P.pool(lambda e: e.tensor_tensor(out=vi[:, :], in0=m3[:, :], in1=m4[:, :], op=ALU.add),
                       reads=[bm[2], bm[3]], writes=[bv[1]])
                if b == 0:
                    P.dve(lambda e, j=j: e.memset(ini[:, j, :], 0.0), writes=[bini[j]])
                else:
                    c0 = pw[:, 0, 0, j:j + 1]
                    s0 = pw[:, 1, 0, j:j + 1]
                    xl = xr[:, j, L - 1:L]
                    yl = xi[:, j, L - 1:L]
                    P.dve(lambda e, j=j, s0=s0, yl=yl: e.tensor_tensor(out=ini[:, j, 2:3], in0=yl, in1=s0,
                                                                       op=ALU.mult),
                          reads=[bx[j][1], bpw], writes=[bini[j]])
                    P.dve(lambda e, j=j, c0=c0, xl=xl: e.scalar_tensor_tensor(out=ini[:, j, 0:1], in0=xl, scalar=c0,
                                                                              in1=ini[:, j, 2:3], op0=ALU.mult,
                                                                              op1=ALU.subtract),
                          reads=[bx[j][0], bpw, bini[j]], writes=[bini[j]])
                    P.dve(lambda e, j=j, c0=c0, yl=yl: e.tensor_tensor(out=ini[:, j, 3:4], in0=yl, in1=c0,
                                                                       op=ALU.mult),
                          reads=[bx[j][1], bpw], writes=[bini[j]])
                    P.dve(lambda e, j=j, s0=s0, xl=xl: e.scalar_tensor_tensor(out=ini[:, j, 1:2], in0=xl, scalar=s0,
                                                                              in1=ini[:, j, 3:4], op0=ALU.mult,
                                                                              op1=ALU.add),
                          reads=[bx[j][0], bpw, bini[j]], writes=[bini[j]])
                P.dve(lambda e, j=j: e.tensor_tensor_scan(out=wr[:, :], data0=Rr[:, j, :], data1=vr[:, :],
                                                          initial=ini[:, j, 0:1], op0=ALU.mult, op1=ALU.add),
                      reads=[bt, bv[0], bini[j]], writes=[bw[0]])
                P.dve(lambda e, j=j: e.tensor_tensor_scan(out=wi[:, :], data0=Rr[:, j, :], data1=vi[:, :],
                                                          initial=ini[:, j, 1:2], op0=ALU.mult, op1=ALU.add),
                      reads=[bt, bv[1], bini[j]], writes=[bw[1]])
                P.dve(lambda e, j=j: e.tensor_tensor(out=m1[:, :], in0=Ec[:, j, :], in1=wr[:, :], op=ALU.mult),
                      reads=[bt, bw[0]], writes=[bm[0]])
                P.pool(lambda e, j=j: e.tensor_tensor(out=m2[:, :], in0=Es[:, j, :], in1=wi[:, :], op=ALU.mult),
                       reads=[bt, bw[1]], writes=[bm[1]])
                P.dve(lambda e, j=j: e.tensor_tensor(out=m3[:, :], in0=Es[:, j, :], in1=wr[:, :], op=ALU.mult),
                      reads=[bt, bw[0]], writes=[bm[2]])
                P.pool(lambda e, j=j: e.tensor_tensor(out=m4[:, :], in0=Ec[:, j, :], in1=wi[:, :], op=ALU.mult),
                       reads=[bt, bw[1]], writes=[bm[3]])
                P.pool(lambda e, j=j: e.tensor_tensor(out=xr[:, j, :], in0=m1[:, :], in1=m2[:, :], op=ALU.subtract),
                       reads=[bm[0], bm[1]], writes=[bx[j][0]])
                P.pool(lambda e, j=j: e.tensor_tensor(out=xi[:, j, :], in0=m3[:, :], in1=m4[:, :], op=ALU.add),
                       reads=[bm[2], bm[3]], writes=[bx[j][1]])
            yb = b % 2
            for j in range(4):
                P.pe(lambda e, j=j, yb=yb: e.matmul(pY[yb][:, :], lhsT=wC[:, 0, j, :], rhs=xr[:, j, :],
                                                    start=(j == 0), stop=False),
                     reads=[bwC, bx[j][0]], writes=[bpY[yb]])
                P.pe(lambda e, j=j, yb=yb: e.matmul(pY[yb][:, :], lhsT=wC[:, 1, j, :], rhs=xi[:, j, :],
                                                    start=False, stop=(j == 3)),
                     reads=[bwC, bx[j][1]], writes=[bpY[yb]])
            P.dve(lambda e, yb=yb, ub=ub: e.scalar_tensor_tensor(out=yo[yb][:, :], in0=uT[ub][:, :], scalar=dk[:, 0:1],
                                                                 in1=pY[yb][:, :], op0=ALU.mult, op1=ALU.add),
                  reads=[bu[ub], bdk, bpY[yb]], writes=[byo[yb]])
            P.pool(lambda e, yb=yb: e.tensor_tensor(out=g1[:, :], in0=yo[yb][:, :], in1=yo[yb][:, :], op=ALU.mult),
                   reads=[byo[yb]], writes=[bg[0]])
            P.pool(lambda e: e.tensor_scalar(out=g1[:, :], in0=g1[:, :], scalar1=0.044715, scalar2=1.0,
                                             op0=ALU.mult, op1=ALU.add), reads=[bg[0]], writes=[bg[0]])
            P.pool(lambda e, yb=yb: e.tensor_tensor(out=g1[:, :], in0=g1[:, :], in1=yo[yb][:, :], op=ALU.mult),
                   reads=[bg[0], byo[yb]], writes=[bg[0]])
            P.act(lambda e: e.activation(out=g2[:, :], in_=g1[:, :], func=AF.Sigmoid, scale=2.0 * GC),
                  reads=[bg[0]], writes=[bg[1]])
            P.dve(lambda e, yb=yb: e.tensor_tensor(out=yo[yb][:, :], in0=yo[yb][:, :], in1=g2[:, :], op=ALU.mult),
                  reads=[byo[yb], bg[1]], writes=[byo[yb]])
            P.dma(yT[:, b * L:(b + 1) * L], yo[yb][:, :], reads=[byo[yb]], writes=[bout])
    P.barrier()


def phase_d(P, nc, zT, z, bz, bf, o_c):
    NKB = S // 128
    NQG = S // 512
    SCALE = 128 ** -0.5
    with contextlib.ExitStack() as st:
        c = Ctx(P, nc, st)
        qT = c.sb("qT", [128, S], BF16); bq = Buf()
        kT = c.sb("kT", [128, S], BF16); bk = Buf()
        va = c.sb("va", [128, NKB, 129], BF16); bva = Buf()
        CHK = min(2048, S)
        stq = [c.sb("stq%d" % i, [128, CHK]) for i in range(2)]; bstq = [Buf(), Buf()]
        rowA = c.sb("rowA", [1, S]); brA = Buf()
        rowB = c.sb("rowB", [1, S]); brB = Buf()
        onesr = c.sb("onesr", [1, 512]); bon = Buf()
        cst = c.sb("cst", [1, 4]); bcst = Buf()
        ccol = c.sb("ccol", [128, NKB]); bcc = Buf()
        tri = c.sb("tri", [128, 128]); btri = Buf()
        PT = [c.sb("PT%d" % i, [128, 512], BF16) for i in range(3)]; bPT = [Buf() for _ in range(3)]
        fg = [c.sb("fg%d" % i, [128, 128]) for i in range(2)]; bfg = [Buf(), Buf()]
        ot = [c.sb("ot%d" % i, [128, 128]) for i in range(2)]; bot = [Buf(), Buf()]
        rl = c.sb("rl", [128, 8]); brl = Buf()
        pST = [c.ps("pST0"), c.ps("pST1")]; bpST = [Buf(), Buf()]
        pO = [c.ps("pO%d" % i) for i in range(4)]; bpO = [Buf() for _ in range(4)]
        pC = c.ps("pC"); bpC = Buf()
        bout = Buf()

        P.pool(lambda e: e.memset(va[:, :, 128:129], 1.0), writes=[bva])
        si = 0
        for ch in range(S // CHK):
            cs = slice(ch * CHK, (ch + 1) * CHK)
            for (dst, bdst, r0) in ((qT, bq, 384), (kT, bk, 512)):
                sb_ = si % 2; si += 1
                P.dma(stq[sb_][:, :], zT[r0:r0 + 128, cs], reads=[bz], writes=[bstq[sb_]])
                if si % 2 == 0:
                    P.dve(lambda e, dst=dst, cs=cs, sb_=sb_: e.tensor_copy(out=dst[:, cs], in_=stq[sb_][:, :]),
                          reads=[bstq[sb_]], writes=[bdst])
                else:
                    P.pool(lambda e, dst=dst, cs=cs, sb_=sb_: e.tensor_copy(out=dst[:, cs], in_=stq[sb_][:, :]),
                           reads=[bstq[sb_]], writes=[bdst])
            sb_ = si % 2; si += 1
            nk = CHK // 128
            P.dma(stq[sb_][:, :].rearrange("p (n v) -> p n v", v=128),
                  z[ch * CHK:(ch + 1) * CHK, 256:384].rearrange("(n p) v -> p n v", p=128), reads=[bz],
                  writes=[bstq[sb_]])
            P.act(lambda e, ch=ch, nk=nk, sb_=sb_: e.copy(out=va[:, ch * nk:(ch + 1) * nk, 0:128],
                                                       in_=stq[sb_][:, :].rearrange("p (n v) -> p n v", v=128)),
                  reads=[bstq[sb_]], writes=[bva])
        P.dma(rowA[:, :], zT[640:641, :], reads=[bz], writes=[brA])
        P.dma(cst[:, 0:1], bf, writes=[bcst])
        P.dve(lambda e: e.tensor_scalar(out=cst[:, 1:2], in0=cst[:, 0:1], scalar1=-1.0, scalar2=None, op0=ALU.mult),
              reads=[bcst], writes=[bcst])
        P.pool(lambda e: e.memset(onesr[:, :], 1.0), writes=[bon])
        P.pool(lambda e: e.memset(tri[:, :], 1.0), writes=[btri])
        P.pool(lambda e: e.affine_select(out=tri[:, :], in_=tri[:, :], pattern=[[1, 128]],
                                         compare_op=ALU.is_ge, fill=0.0, base=0, channel_multiplier=-1),
               reads=[btri], writes=[btri])
        P.act(lambda e: e.activation(out=rowA[:, :], in_=rowA[:, :], func=AF.Exp, scale=-1.0, bias=cst[:, 1:2]),
              reads=[brA, bcst], writes=[brA])
        P.act(lambda e: e.activation(out=rowA[:, :], in_=rowA[:, :], func=AF.Ln, bias=1.0),
              reads=[brA], writes=[brA])
        for h2 in range(S // 512):
            sl = slice(h2 * 512, (h2 + 1) * 512)
            init = 0.0 if h2 == 0 else rowB[:, h2 * 512 - 1:h2 * 512]
            P.dve(lambda e, sl=sl, init=init: e.tensor_tensor_scan(out=rowB[:, sl], data0=onesr[:, :],
                                                                   data1=rowA[:, sl], initial=init,
                                                                   op0=ALU.mult, op1=ALU.add),
                  reads=[brA, brB, bon], writes=[brB])
        for kb in range(NKB):
            P.pe(lambda e, kb=kb: e.matmul(pC[:, kb:kb + 1], lhsT=rowB[0:1, kb * 128:(kb + 1) * 128],
                                           rhs=onesr[0:1, 0:1], start=True, stop=True),
                 reads=[brB, bon], writes=[bpC])
        P.dve(lambda e: e.tensor_copy(out=ccol[:, :], in_=pC[:, 0:NKB]), reads=[bpC], writes=[bcc])
        P.dve(lambda e: e.tensor_scalar(out=rowB[:, :], in0=rowB[:, :], scalar1=-1.0 / SCALE, scalar2=None,
                                        op0=ALU.mult), reads=[brB], writes=[brB])
        hib = c.sb("hib", [1, S], BF16); midb = c.sb("midb", [1, S], BF16); lob = c.sb("lob", [1, S], BF16)
        onesb = c.sb("onesb", [1, 128], BF16)
        bsp = Buf()
        P.pool(lambda e: e.memset(onesb[:, :], 1.0), writes=[bsp])
        P.dve(lambda e: e.tensor_copy(out=hib[:, :], in_=rowB[:, :]), reads=[brB], writes=[bsp])
        P.dve(lambda e: e.tensor_copy(out=rowA[:, :], in_=hib[:, :]), reads=[bsp, brA], writes=[brA])
        P.dve(lambda e: e.tensor_tensor(out=rowA[:, :], in0=rowB[:, :], in1=rowA[:, :], op=ALU.subtract),
              reads=[brA, brB], writes=[brA])
        P.dve(lambda e: e.tensor_copy(out=midb[:, :], in_=rowA[:, :]), reads=[brA], writes=[bsp])
        P.dve(lambda e: e.tensor_copy(out=rowB[:, :], in_=midb[:, :]), reads=[bsp, brB], writes=[brB])
        P.dve(lambda e: e.tensor_tensor(out=rowB[:, :], in0=rowA[:, :], in1=rowB[:, :], op=ALU.subtract),
              reads=[brA, brB], writes=[brB])
        P.dve(lambda e: e.tensor_copy(out=lob[:, :], in_=rowB[:, :]), reads=[brB], writes=[bsp])
        it = 0
        for Q in range(NQG):
            oset = (Q % 2) * 2
            for ob_ in (oset, oset + 1):
                P.dve(lambda e, ob_=ob_: e.memset(pO[ob_][:, 0:258], 0.0), writes=[bpO[ob_]])
            for kb in range(4 * Q + 4):
                jlo = max(0, kb - 4 * Q)
                q0 = Q * 512 + jlo * 128
                n = 512 - jlo * 128
                sb_ = it % 2
                pb = it % 3
                it += 1
                P.pe(lambda e, kb=kb, q0=q0, n=n, sb_=sb_: e.matmul(pST[sb_][:, 0:n],
                                                                  lhsT=kT[:, kb * 128:(kb + 1) * 128],
                                                                  rhs=qT[:, q0:q0 + n], start=True, stop=False),
                     reads=[bk, bq], writes=[bpST[sb_]])
                for ri_, rw_ in enumerate((hib, midb, lob)):
                    P.pe(lambda e, q0=q0, n=n, sb_=sb_, rw_=rw_, ri_=ri_: e.matmul(
                        pST[sb_][:, 0:n], lhsT=onesb[0:1, 0:128], rhs=rw_[0:1, q0:q0 + n],
                        start=False, stop=(ri_ == 2)), reads=[bsp], writes=[bpST[sb_]])
                P.act(lambda e, kb=kb, n=n, sb_=sb_, pb=pb: e.activation(out=PT[pb][:, 0:n], in_=pST[sb_][:, 0:n],
                                                                       func=AF.Exp, scale=SCALE,
                                                                       bias=ccol[:, kb:kb + 1]),
                      reads=[bpST[sb_], bcc], writes=[bPT[pb]])
                if kb >= 4 * Q:
                    P.pool(lambda e, pb=pb: e.tensor_tensor(out=PT[pb][:, 0:128], in0=PT[pb][:, 0:128],
                                                            in1=tri[:, :], op=ALU.mult),
                           reads=[bPT[pb], btri], writes=[bPT[pb]])
                for jq in range(jlo, 4):
                    qb = 4 * Q + jq
                    off = (jq - jlo) * 128
                    ob = oset + jq // 2
                    oc = (jq % 2) * 129
                    P.pe(lambda e, kb=kb, pb=pb, off=off, ob=ob, oc=oc, qb=qb: e.matmul(
                        pO[ob][:, oc:oc + 129], lhsT=PT[pb][:, off:off + 128], rhs=va[:, kb, :],
                        start=False, stop=True, skip_group_check=True),
                        reads=[bPT[pb], bva], writes=[bpO[ob]])
            for jq in range(4):
                qb = 4 * Q + jq
                ob = oset + jq // 2
                oc = (jq % 2) * 129
                fb = qb % 2
                P.dma(fg[fb][:, :], z[qb * 128:(qb + 1) * 128, 384:512], reads=[bz], writes=[bfg[fb]])
                P.dve(lambda e, ob=ob, oc=oc, jq=jq: e.reciprocal(out=rl[:, jq:jq + 1],
                                                                  in_=pO[ob][:, oc + 128:oc + 129]),
                      reads=[bpO[ob]], writes=[brl])
                P.dve(lambda e, ob=ob, oc=oc, jq=jq, fb=fb: e.tensor_scalar(out=ot[fb][:, :],
                                                                           in0=pO[ob][:, oc:oc + 128],
                                                                           scalar1=rl[:, jq:jq + 1], scalar2=None,
                                                                           op0=ALU.mult),
                      reads=[bpO[ob], brl], writes=[bot[fb]])
                P.act(lambda e, fb=fb: e.activation(out=fg[fb][:, :], in_=fg[fb][:, :], func=AF.Silu),
                      reads=[bfg[fb]], writes=[bfg[fb]])
                P.pool(lambda e, fb=fb: e.tensor_tensor(out=ot[fb][:, :], in0=ot[fb][:, :], in1=fg[fb][:, :],
                                                        op=ALU.mult),
                       reads=[bot[fb], bfg[fb]], writes=[bot[fb]])
                P.dma(o_c[qb * 128:(qb + 1) * 128, :], ot[fb][:, :], reads=[bot[fb]], writes=[bout])
    P.barrier()


T2 = 1024
HW = 512
NW2 = 7168


def build_k2(P, nc, x, gcol, w2, bgc, yT, o_a, o_c, wglu, wa, wb, wc, wout, fg, xo, xf, T2=1024, rd_src=(), rd_x=(), wr_xo=()):
    KC = D // 128
    with contextlib.ExitStack() as st:
        c = Ctx(P, nc, st)
        ident, bid = make_ident(P, c)
        gc = c.sb("gc", [128, KC]); bgc_ = Buf()
        P.dma(gc[:, :], gcol, writes=[bgc_])
        bg = c.sb("bg", [128, 48]); bbg = Buf()
        P.dma(bg[:, :], bgc, writes=[bbg])
        fgb = c.sb("fgb", [128, D]); bfgb = Buf()
        P.dma(fgb[:, :], fg.partition_broadcast(128), writes=[bfgb])
        xs = c.sb("xs", [128, 4, D]); bxs = [Buf() for _ in range(4)]
        hb = c.sb("hb", [128, D]); bhb = Buf()
        stg0 = c.sb("stg0", [128, 1024]); bstg0 = Buf()
        stg = [stg0, stg0]; bstg = [bstg0, bstg0]
        hT = c.sb("hT", [128, KC, HW], BF16); bhT = Buf()
        oT = c.sb("oT", [128, 8, HW], BF16); boT = Buf()
        ymT = c.sb("ymT", [128, KC, HW]); bym = [Buf() for _ in range(KC)]
        yTb = c.sb("yTb", [128, 8, HW], BF16); byTb = Buf()
        mTb = hT; bmTb = [bhT for _ in range(KC)]
        w2b = [c.sb("w2b%d" % i, [128, KC, 128], BF16) for i in range(2)]; bw2b = [Buf(), Buf()]
        wbb = [c.sb("wbb%d" % i, [128, 8, 128], BF16) for i in range(2)]; bwbb = [Buf(), Buf()]
        wgb = [c.sb("wgb%d" % i, [128, 2, 8, 128], BF16) for i in range(2)]; bwgb = [Buf(), Buf()]
        wob = w2b; bwob = bw2b
        castc = [0]

        def cast(dst, bdst, src, bsrc):
            i = castc[0]; castc[0] += 1
            if i % 2 == 0:
                P.pool(lambda e: e.tensor_copy(out=dst, in_=src), reads=[bsrc], writes=[bdst])
            else:
                P.act(lambda e: e.copy(out=dst, in_=src), reads=[bsrc], writes=[bdst])
        stat = c.sb("stat", [128, 8]); bstat = Buf()
        w2s = [c.sb("w2s%d" % i, [128, KC, 128]) for i in range(2)]; bw2s = [Buf(), Buf()]
        wbs = [c.sb("wbs%d" % i, [128, 8, 128]) for i in range(2)]; bwbs = [Buf(), Buf()]
        wgs = [c.sb("wgs%d" % i, [128, 2, 8, 128]) for i in range(2)]; bwgs = [Buf(), Buf()]
        wos = w2s; bwos = bw2s
        e1 = [c.sb("e1%d" % i, [128, HW]) for i in range(2)]; be1 = [Buf(), Buf()]
        e2 = [c.sb("e2%d" % i, [128, HW]) for i in range(2)]; be2 = [Buf(), Buf()]
        e3 = [c.sb("e3%d" % i, [128, HW]) for i in range(2)]; be3 = [Buf(), Buf()]
        banks = [c.ps("bk%d" % i) for i in range(8)]; bbk = [Buf() for _ in range(8)]
        bki = [0]
        bout = Buf()

        def nb():
            i = bki[0] % 8
            bki[0] += 1
            return banks[i], bbk[i]

        cnt = dict(w2=0, wb=0, wg=0, wo=0, e=0)
        w2v = w2.rearrange("(c p) n -> p c n", p=128)
        wgv = wglu.rearrange("(c p) n -> p c n", p=128)
        wov = wout.rearrange("(c p) n -> p c n", p=128)

        def evac_T(ps_t, bps, dst, bdst, use_act):
            if use_act:
                P.act(lambda e: e.copy(out=dst, in_=ps_t[:, :].rearrange("p (a b) -> p a b", a=4)),
                      reads=[bps], writes=[bdst])
            else:
                P.dve(lambda e: e.tensor_copy(out=dst, in_=ps_t[:, :].rearrange("p (a b) -> p a b", a=4)),
                      reads=[bps], writes=[bdst])

        for hf in range(T2 // HW):
            tok0 = hf * HW
            for tt in range(4):
                t0 = tok0 + tt * 128
                P.dma(xs[:, tt, :], x[t0:t0 + 128, :], reads=rd_x, writes=[bxs[tt]])
                P.act(lambda e, tt=tt: e.activation(out=hb[:, :], in_=xs[:, tt, :], func=AF.Square,
                                                    accum_out=stat[:, 0:1]),
                      reads=[bxs[tt]], writes=[bhb, bstat])
                P.dve(lambda e: e.tensor_scalar(out=stat[:, 1:2], in0=stat[:, 0:1], scalar1=1.0 / D, scalar2=EPS,
                                                op0=ALU.mult, op1=ALU.add), reads=[bstat], writes=[bstat])
                P.act(lambda e: e.activation(out=stat[:, 2:3], in_=stat[:, 1:2], func=AF.Sqrt),
                      reads=[bstat], writes=[bstat])
                P.dve(lambda e: e.reciprocal(out=stat[:, 3:4], in_=stat[:, 2:3]), reads=[bstat], writes=[bstat])
                P.dve(lambda e, tt=tt: e.tensor_scalar(out=hb[:, :], in0=xs[:, tt, :], scalar1=stat[:, 3:4],
                                                       scalar2=None, op0=ALU.mult),
                      reads=[bxs[tt], bstat], writes=[bhb])
                for q in range(KC // 4):
                    pt, bpt = nb()
                    for jj in range(4):
                        j = q * 4 + jj
                        P.pe(lambda e, j=j, jj=jj, pt=pt: e.transpose(out=pt[:, jj * 128:(jj + 1) * 128],
                                                                      in_=hb[:, j * 128:(j + 1) * 128],
                                                                      identity=ident[:, :]),
                             reads=[bhb, bid], writes=[bpt])
                    for jj in range(4):
                        j = q * 4 + jj
                        if q % 2 == 0:
                            P.act(lambda e, j=j, jj=jj, tt=tt, pt=pt: e.activation(
                                out=hT[:, j, tt * 128:(tt + 1) * 128], in_=pt[:, jj * 128:(jj + 1) * 128],
                                func=AF.Identity, scale=gc[:, j:j + 1]),
                                reads=[bpt, bgc_], writes=[bhT])
                        else:
                            P.dve(lambda e, j=j, jj=jj, tt=tt, pt=pt: e.tensor_scalar(
                                out=hT[:, j, tt * 128:(tt + 1) * 128], in0=pt[:, jj * 128:(jj + 1) * 128],
                                scalar1=gc[:, j:j + 1], scalar2=None, op0=ALU.mult),
                                reads=[bpt, bgc_], writes=[bhT])
            for bi_, br in enumerate(("b", "a", "c")):
                if br == "b":
                    P.add("sp", lambda e, tok0=tok0: e.dma_start(out=ymT[:, 0:8, :], in_=yT(e, tok0)),
                          reads=rd_src, writes=bym[0:8], dma=True)
                    P.dve(lambda e: e.tensor_copy(out=yTb[:, :, :], in_=ymT[:, 0:8, :]), reads=bym[0:8], writes=[byTb])
                    for eb in range(8):
                        sg = cnt["wg"] % 2; cnt["wg"] += 1
                        s2 = cnt["w2"] % 2; cnt["w2"] += 1
                        P.dma(wgs[sg][:, 0, :, :], wgv[:, :, eb * 128:(eb + 1) * 128], writes=[bwgs[sg]])
                        P.dma(wgs[sg][:, 1, :, :], wgv[:, :, 1024 + eb * 128:1024 + (eb + 1) * 128],
                              writes=[bwgs[sg]])
                        P.dma(w2s[s2][:, :, :], w2v[:, :, eb * 128:(eb + 1) * 128], writes=[bw2s[s2]])
                        cast(wgb[sg][:, :, :, :], bwgb[sg], wgs[sg][:, :, :, :], bwgs[sg])
                        cast(w2b[s2][:, :, :], bw2b[s2], w2s[s2][:, :, :], bw2s[s2])
                        pa, bpa = nb()
                        pb_, bpb = nb()
                        pc_, bpc = nb()
                        for kc in range(8):
                            P.pe(lambda e, kc=kc, sg=sg, pa=pa: e.matmul(pa[:, :], lhsT=wgb[sg][:, 0, kc, :],
                                                                        rhs=yTb[:, kc, :], start=(kc == 0),
                                                                        stop=(kc == 7)),
                                 reads=[bwgb[sg], byTb], writes=[bpa])
                        for kc in range(8):
                            P.pe(lambda e, kc=kc, sg=sg, pb_=pb_: e.matmul(pb_[:, :], lhsT=wgb[sg][:, 1, kc, :],
                                                                          rhs=yTb[:, kc, :], start=(kc == 0),
                                                                          stop=(kc == 7)),
                                 reads=[bwgb[sg], byTb], writes=[bpb])
                        for kc in range(KC):
                            P.pe(lambda e, kc=kc, s2=s2, pc_=pc_: e.matmul(pc_[:, :], lhsT=w2b[s2][:, kc, :],
                                                                          rhs=hT[:, kc, :], start=(kc == 0),
                                                                          stop=(kc == KC - 1)),
                                 reads=[bw2b[s2], bhT], writes=[bpc])
                        ei = cnt["e"] % 2; cnt["e"] += 1
                        P.act(lambda e, ei=ei, pb_=pb_: e.activation(out=e1[ei][:, :], in_=pb_[:, :],
                                                                    func=AF.Sigmoid),
                              reads=[bpb], writes=[be1[ei]])
                        P.act(lambda e, ei=ei, pc_=pc_: e.activation(out=e2[ei][:, :], in_=pc_[:, :], func=AF.Silu),
                              reads=[bpc], writes=[be2[ei]])
                        P.dve(lambda e, ei=ei, pa=pa: e.tensor_tensor(out=e3[ei][:, :], in0=pa[:, :],
                                                                     in1=e1[ei][:, :], op=ALU.mult),
                              reads=[bpa, be1[ei]], writes=[be3[ei]])
                        P.pool(lambda e, ei=ei, eb=eb: e.tensor_tensor(out=oT[:, eb, :], in0=e3[ei][:, :],
                                                                      in1=e2[ei][:, :], op=ALU.mult),
                               reads=[be3[ei], be2[ei]], writes=[boT])
                else:
                    src = o_a if br == "a" else o_c
                    for tt in range(4):
                        t0 = tok0 + tt * 128
                        sb_ = tt % 2
                        P.add("sp", lambda e, t0=t0, sb_=sb_, src=src: e.dma_start(
                            out=stg[sb_][:, :].rearrange("p (r v) -> p r v", r=8), in_=src(e, t0)),
                            reads=rd_src, writes=[bstg[sb_]], dma=True)
                        for q in range(2):
                            pt, bpt = nb()
                            for jj in range(4):
                                j = q * 4 + jj
                                P.pe(lambda e, j=j, jj=jj, pt=pt, sb_=sb_: e.transpose(
                                    out=pt[:, jj * 128:(jj + 1) * 128], in_=stg[sb_][:, j * 128:(j + 1) * 128],
                                    identity=ident[:, :]), reads=[bstg[sb_], bid], writes=[bpt])
                            evac_T(pt, bpt, oT[:, q * 4:(q + 1) * 4, tt * 128:(tt + 1) * 128], boT, q == 0)
                wbr = dict(a=wa, b=wb, c=wc)[br]
                wbv = wbr.rearrange("(c p) n -> p c n", p=128)
                gofs = dict(a=0, b=1, c=2)[br]
                for db in range(KC):
                    s2 = cnt["w2"] % 2; cnt["w2"] += 1
                    sb2 = cnt["wb"] % 2; cnt["wb"] += 1
                    col0 = 1024 + gofs * 2048 + db * 128
                    P.dma(w2s[s2][:, :, :], w2v[:, :, col0:col0 + 128], writes=[bw2s[s2]])
                    P.dma(wbs[sb2][:, :, :], wbv[:, :, db * 128:(db + 1) * 128], writes=[bwbs[sb2]])
                    cast(w2b[s2][:, :, :], bw2b[s2], w2s[s2][:, :, :], bw2s[s2])
                    cast(wbb[sb2][:, :, :], bwbb[sb2], wbs[sb2][:, :, :], bwbs[sb2])
                    pg, bpg = nb()
                    pm, bpm = nb()
                    for kc in range(KC):
                        P.pe(lambda e, kc=kc, s2=s2, pg=pg: e.matmul(pg[:, :], lhsT=w2b[s2][:, kc, :],
                                                                    rhs=hT[:, kc, :], start=(kc == 0),
                                                                    stop=(kc == KC - 1)),
                             reads=[bw2b[s2], bhT], writes=[bpg])
                    for kc in range(8):
                        P.pe(lambda e, kc=kc, sb2=sb2, pm=pm: e.matmul(pm[:, :], lhsT=wbb[sb2][:, kc, :],
                                                                      rhs=oT[:, kc, :], start=(kc == 0),
                                                                      stop=(kc == 7)),
                             reads=[bwbb[sb2], boT], writes=[bpm])
                    ei = cnt["e"] % 2; cnt["e"] += 1
                    bcol = gofs * 16 + db
                    P.act(lambda e, ei=ei, pg=pg, bcol=bcol: e.activation(out=e1[ei][:, :], in_=pg[:, :],
                                                                         func=AF.Sigmoid,
                                                                         bias=bg[:, bcol:bcol + 1]),
                          reads=[bpg, bbg], writes=[be1[ei]])
                    if bi_ == 0:
                        P.dve(lambda e, ei=ei, pm=pm, db=db: e.tensor_tensor(out=ymT[:, db, :], in0=pm[:, :],
                                                                            in1=e1[ei][:, :], op=ALU.mult),
                              reads=[bpm, be1[ei]], writes=[bym[db]])
                    else:
                        P.dve(lambda e, ei=ei, pm=pm: e.tensor_tensor(out=e3[ei][:, :], in0=pm[:, :],
                                                                     in1=e1[ei][:, :], op=ALU.mult),
                              reads=[bpm, be1[ei]], writes=[be3[ei]])
                        P.pool(lambda e, ei=ei, db=db: e.tensor_tensor(out=ymT[:, db, :], in0=ymT[:, db, :],
                                                                      in1=e3[ei][:, :], op=ALU.add),
                               reads=[bym[db], be3[ei]], writes=[bym[db]])
            for kc in range(KC):
                cast(mTb[:, kc, :], bmTb[kc], ymT[:, kc, :], bym[kc])
            for ob in range(KC):
                so = cnt["w2"] % 2; cnt["w2"] += 1
                P.dma(wos[so][:, :, :], wov[:, :, ob * 128:(ob + 1) * 128], writes=[bwos[so]])
                cast(wob[so][:, :, :], bwob[so], wos[so][:, :, :], bwos[so])
                po, bpo = nb()
                for kc in range(KC):
                    P.pe(lambda e, kc=kc, so=so, po=po: e.matmul(po[:, :], lhsT=wob[so][:, kc, :], rhs=mTb[:, kc, :],
                                                                start=(kc == 0), stop=(kc == KC - 1)),
                         reads=[bwob[so], bmTb[kc]], writes=[bpo])
                ei = cnt["e"] % 2; cnt["e"] += 1
                P.act(lambda e, ei=ei, po=po: e.copy(out=e1[ei][:, :], in_=po[:, :]), reads=[bpo], writes=[be1[ei]])
                pt, bpt = nb()
                for tt in range(4):
                    P.pe(lambda e, tt=tt, ei=ei, pt=pt: e.transpose(out=pt[:, tt * 128:(tt + 1) * 128],
                                                                   in_=e1[ei][:, tt * 128:(tt + 1) * 128],
                                                                   identity=ident[:, :]),
                         reads=[be1[ei], bid], writes=[bpt])
                P.dve(lambda e, ob=ob, pt=pt: e.tensor_tensor(out=xs[:, :, ob * 128:(ob + 1) * 128],
                                                             in0=xs[:, :, ob * 128:(ob + 1) * 128],
                                                             in1=pt[:, :].rearrange("p (a b) -> p a b", a=4),
                                                             op=ALU.add),
                      reads=bxs + [bpt], writes=bxs)
            for tt in range(4):
                t0 = tok0 + tt * 128
                if xo is not None:
                    P.dma(xo[t0:t0 + 128, :], xs[:, tt, :], reads=[bxs[tt]], writes=[bout] + list(wr_xo))
                if xf is None:
                    continue
                P.act(lambda e, tt=tt: e.activation(out=hb[:, :], in_=xs[:, tt, :], func=AF.Square,
                                                    accum_out=stat[:, 4:5]),
                      reads=[bxs[tt]], writes=[bhb, bstat])
                P.dve(lambda e: e.tensor_scalar(out=stat[:, 5:6], in0=stat[:, 4:5], scalar1=1.0 / D, scalar2=EPS,
                                                op0=ALU.mult, op1=ALU.add), reads=[bstat], writes=[bstat])
                P.act(lambda e: e.activation(out=stat[:, 6:7], in_=stat[:, 5:6], func=AF.Sqrt),
                      reads=[bstat], writes=[bstat])
                P.dve(lambda e: e.reciprocal(out=stat[:, 7:8], in_=stat[:, 6:7]), reads=[bstat], writes=[bstat])
                P.dve(lambda e, tt=tt: e.scalar_tensor_tensor(out=hb[:, :], in0=xs[:, tt, :], scalar=stat[:, 7:8],
                                                              in1=fgb[:, :], op0=ALU.mult, op1=ALU.mult),
                      reads=[bxs[tt], bstat, bfgb], writes=[bhb])
                P.dma(xf[t0:t0 + 128, :], hb[:, :], reads=[bhb], writes=[bout])
    P.barrier()


NCORE = 8


def build_fused():
    D = 2048
    T2 = S // NCORE
    nc = bass.Bass("TRN2", target_bir_lowering=False)
    i_ = lambda n, s: nc.dram_tensor(n, s, F32, kind="ExternalInput").ap()
    o_ = lambda n, s: nc.dram_tensor(n, s, F32, kind="ExternalOutput").ap()
    t_ = lambda n, s: nc.dram_tensor(n, s, F32)
    xs = i_("xs", [T2, D])
    fg = i_("fg", [1, D])
    L = []
    for l in range(2):
        s = "_%d" % l
        L.append(dict(
            g=i_("g" + s, [1, D]), w=i_("w" + s, [D, NCOL]), lb0=i_("lb0" + s, [128, 1]), lb1=i_("lb1" + s, [128, 1]),
            lsel=i_("lsel" + s, [128, 1]), hgn=i_("hgn" + s, [1, 128]), s5p=i_("s5p" + s, [128, 3, 4]),
            Bre=i_("Bre" + s, [128, 4, 128]), Bim=i_("Bim" + s, [128, 4, 128]), Cre=i_("Cre" + s, [128, 4, 128]),
            Cim=i_("Cim" + s, [128, 4, 128]), dsk=i_("dsk" + s, [128, 1]), bf=i_("bf" + s, [1, 1]),
            gcol=i_("gcol" + s, [128, 16]), w2=i_("w2" + s, [D, 7168]), bgc=i_("bgc" + s, [128, 48]),
            wglu=i_("wglu" + s, [1024, 2048]), wa=i_("wa" + s, [1024, 2048]), wb=i_("wb" + s, [1024, 2048]),
            wc=i_("wc" + s, [1024, 2048]), wout=i_("wout" + s, [2048, 2048])))
    out = o_("out", [T2, D])
    xsb = t_("xsb", [T2, D]); x_all = t_("x_all", [S, D])
    zT = t_("zT", [NFM, S]).ap(); z = t_("z", [S, NTM]).ap()
    ib_oa = t_("ib_oa", [S, 128]); ib_y = t_("ib_y", [128, S]); ib_oc = t_("ib_oc", [S, 128])
    g_oa = t_("g_oa", [NCORE * S, 128]); g_y = t_("g_y", [NCORE * 128, S]); g_oc = t_("g_oc", [NCORE * S, 128])
    P = Prog(nc)
    RG = [list(range(NCORE))]

    def allgather(src, dst):
        P.cc(lambda e: e.collective_compute("AllGather", ALU.bypass, replica_groups=RG,
                                            ins=[src.ap().opt()], outs=[dst.ap().opt()]))

    P.dma(xsb.ap(), xs)
    P.barrier()
    allgather(xsb, x_all)
    P.barrier()
    for l in range(2):
        a = L[l]
        bz = Buf("z")
        phase_a(P, nc, x_all.ap(), a["g"], a["w"], zT, z, bz)
        phase_b(P, nc, zT, z, bz, a["lb0"], a["lb1"], a["lsel"], a["hgn"], ib_oa.ap())
        phase_c(P, nc, zT, bz, a["s5p"], a["Bre"], a["Bim"], a["Cre"], a["Cim"], a["dsk"], ib_y.ap())
        phase_d(P, nc, zT, z, bz, a["bf"], ib_oc.ap())
        allgather(ib_oa, g_oa)
        allgather(ib_y, g_y)
        allgather(ib_oc, g_oc)
        P.barrier()
        g_oa_v = g_oa.ap().rearrange("(r t) v -> t r v", r=NCORE)
        g_oc_v = g_oc.ap().rearrange("(r t) v -> t r v", r=NCORE)
        g_y_ap = g_y.ap()
        ysrc = lambda e, tok0: g_y_ap[:, bass.ds(P.core(e) * T2 + tok0, HW)].rearrange("(c p) t -> p c t", p=128)
        oasrc = lambda e, t0: g_oa_v[bass.ds(P.core(e) * T2 + t0, 128), :, :]
        ocsrc = lambda e, t0: g_oc_v[bass.ds(P.core(e) * T2 + t0, 128), :, :]
        xin = xs if l == 0 else xsb.ap()
        build_k2(P, nc, xin, a["gcol"], a["w2"], a["bgc"], ysrc, oasrc, ocsrc, a["wglu"], a["wa"], a["wb"],
                 a["wc"], a["wout"], fg, xsb.ap() if l == 0 else None, out if l == 1 else None, T2=T2)
        if l == 0:
            allgather(xsb, x_all)
            P.barrier()
    P.emit()
    return nc


OFFS = [0, 1024, 2048, 3072, 4096, 5120, 6144, 7168, 8192, 9216, 9224, 10248, 16392]


def _head_cols(c):
    hq, hf, hi, hgate, su, sgate, fq, fk, fv, ff, fgate, mg = OFFS[:12]
    r = lambda o: list(range(o + 128 * c, o + 128 * c + 128))
    return np.array(r(hq) + r(hf) + r(su) + r(fq) + r(fk) + [ff + c] + r(hi) + r(hgate) + r(fv) + r(fgate))


def _layer_inputs(inp, l, c, shared):
    sl = slice(128 * c, 128 * c + 128)
    f = np.float32
    ca = np.ascontiguousarray
    m = dict(g=ca(inp["norm_g"][l][None, :]), w=ca(inp["w_in"][l][:, _head_cols(c)]),
             lb0=ca(inp["hg_lb"][0, sl][:, None]), lb1=ca(inp["hg_lb"][1, sl][:, None]),
             lsel=np.full((128, 1), float(l), f), hgn=ca(inp["hg_norm_g"][l, sl][None, :]))
    s5p = np.zeros((128, 3, 4), f)
    Bre = np.zeros((128, 4, 128), f); Bim = np.zeros((128, 4, 128), f)
    Cre = np.zeros((128, 4, 128), f); Cim = np.zeros((128, 4, 128), f)
    for j in range(4):
        for gg in range(2):
            gl = 2 * j + gg
            g = 8 * c + gl
            ps = slice(64 * gg, 64 * gg + 64)
            s5p[ps, 0, j] = inp["s5_a_re"][l, g]
            s5p[ps, 1, j] = inp["s5_a_im"][l, g]
            s5p[ps, 2, j] = inp["s5_log_dt"][l, g]
            cs = slice(16 * gl, 16 * gl + 16)
            Bre[cs, j, ps] = inp["s5_b_re"][l, g].T
            Bim[cs, j, ps] = inp["s5_b_im"][l, g].T
            Cre[ps, j, cs] = inp["s5_c_re"][l, g].T
            Cim[ps, j, cs] = inp["s5_c_im"][l, g].T
    m.update(s5p=s5p, Bre=Bre, Bim=Bim, Cre=Cre, Cim=Cim, dsk=ca(inp["s5_d"][l, sl][:, None]),
             bf=ca(inp["fox_bf"][l, c].reshape(1, 1)))
    m.update(shared[l])
    return {k + "_%d" % l: v for k, v in m.items()}


def _shared(inp, l):
    ca = np.ascontiguousarray
    sg, mg = OFFS[5], OFFS[11]
    w = inp["w_in"][l]
    return dict(gcol=ca(inp["norm_g"][l].reshape(16, 128).T),
                w2=ca(np.concatenate([w[:, sg:sg + 1024], w[:, mg:mg + 6144]], axis=1)),
                bgc=ca(inp["b_gate"][l].reshape(48, 128).T), wglu=ca(inp["s5_w_glu"][l]),
                wa=ca(inp["w_br_a"][l]), wb=ca(inp["w_br_b"][l]), wc=ca(inp["w_br_c"][l]), wout=ca(inp["w_out"][l]))


def _core_inputs(inp, c, shared, S_):
    T2_ = S_ // 8
    m = dict(xs=np.ascontiguousarray(inp["x"][0, T2_ * c:T2_ * (c + 1)]), fg=np.ascontiguousarray(inp["final_g"][None, :]))
    for l in range(2):
        m.update(_layer_inputs(inp, l, c, shared))
    return m


def kernel(**inputs):
    inp = {k: np.asarray(v, dtype=np.float32) for k, v in inputs.items()}
    shared = [_shared(inp, l) for l in range(2)]
    nc = build_fused()
    maps = [_core_inputs(inp, c, shared, 8192) for c in range(8)]
    res = run_bass_kernel_spmd(nc, maps, core_ids=list(range(8)))
    out = np.concatenate([r["out"] for r in res.results], axis=0)
    return out[None].astype(np.float32)
```
